# Optimizing a Trainium2 kernel written in Bass

```python
import math
import jax
import jax.numpy as jnp
from jax import lax
import numpy as np

D_MODEL = 1024
BATCH = 8
SEQ = 4096
DEPTH = 4
DEC_BATCH = 1
DEC_SEQ = 16384
PAST_LEN = 128

CHUNK = 128
SHORT_CONV = 3
ML_HEADS = 4
ML_HEAD_DIM = D_MODEL // 8
ML_WIDTH = ML_HEADS * ML_HEAD_DIM
RT_HEADS = 4
RT_HEAD_DIM = D_MODEL // 8
RT_WIDTH = RT_HEADS * RT_HEAD_DIM
HY_WIDTH = D_MODEL // 2
HY_ORDER = 2
HY_BANDS = 16
HY_POS_DIM = 1 + 2 * HY_BANDS
HY_FILTER_HIDDEN = 64
D_MIX = ML_WIDTH + RT_WIDTH + HY_WIDTH
IN_SIZES = (2 * ML_WIDTH, ML_WIDTH, ML_WIDTH, ML_WIDTH, 4 * ML_HEADS,
            3 * RT_WIDTH, RT_WIDTH, 3 * HY_WIDTH, HY_WIDTH)
IN_COLS = sum(IN_SIZES)
RT_LOG_GAMMA_FWD = tuple(math.log(1.0 - 2.0 ** (-5.0 - h)) for h in range(RT_HEADS))
RT_LOG_GAMMA_BWD = tuple(math.log(1.0 - 2.0 ** (-5.5 - h)) for h in range(RT_HEADS))
ROPE_BASE = 10000.0
RMS_EPS = 1e-6
HEAD_NORM_EPS = 1e-5
M_INIT = -1e30
HY_FAST_DECAY = math.log(1e-2) / 0.3
HY_SLOW_DECAY = math.log(1e-2) / 1.5

kernel_name = 'hybrid_mlstm_retention_hyena_encoder'


def rmsnorm(x, g):
    xf = x.astype(jnp.float32)
    y = xf * lax.rsqrt(jnp.mean(xf * xf, axis=-1, keepdims=True) + RMS_EPS) * g.astype(jnp.float32)
    return y.astype(x.dtype)


def head_norm(h, g):
    b, l = h.shape[0], h.shape[1]
    mu = jnp.mean(h, axis=-1, keepdims=True)
    hc = h - mu
    var = jnp.mean(hc * hc, axis=-1, keepdims=True)
    return (hc * lax.rsqrt(var + HEAD_NORM_EPS)).reshape(b, l, -1) * g.astype(jnp.float32)


def centred_conv3(u, w, bias):
    up = jnp.pad(u, ((0, 0), (1, 1), (0, 0)))
    return up[:, :-2] * w[0] + up[:, 1:-1] * w[1] + up[:, 2:] * w[2] + bias


def _to_heads(t, n_heads):
    b, l, _ = t.shape
    return t.reshape(b, l, n_heads, -1).transpose(0, 2, 1, 3)


def _flip(t):
    return jnp.flip(t, axis=2)


def rotary(x):
    l, dh = x.shape[1], x.shape[3]
    inv = ROPE_BASE ** (-jnp.arange(0, dh, 2, dtype=jnp.float32) / dh)
    ang = jnp.arange(l, dtype=jnp.float32)[:, None] * inv[None, :]
    cos = jnp.cos(ang)[None, :, None, :]
    sin = jnp.sin(ang)[None, :, None, :]
    x1, x2 = x[..., : dh // 2], x[..., dh // 2:]
    return jnp.concatenate([x1 * cos - x2 * sin, x1 * sin + x2 * cos], axis=-1)


def mlstm_chunkwise(q, k, v, log_i, log_f):
    b, h, l, dh = q.shape
    n = l // CHUNK
    qc = q.reshape(b, h, n, CHUNK, dh)
    kc = k.reshape(b, h, n, CHUNK, dh)
    vc = v.reshape(b, h, n, CHUNK, dh)
    lic = log_i.reshape(b, h, n, CHUNK)
    cum_f = jnp.cumsum(log_f.reshape(b, h, n, CHUNK), axis=-1)
    causal = jnp.tril(jnp.ones((CHUNK, CHUNK), dtype=bool))
    d_intra = jnp.where(causal, cum_f[..., :, None] - cum_f[..., None, :] + lic[..., None, :], -jnp.inf)
    m_intra = jnp.max(d_intra, axis=-1)
    f_total = cum_f[..., -1]
    g = f_total[..., None] - cum_f + lic
    m_chunk = jnp.max(g, axis=-1)
    kw = kc * jnp.exp(g - m_chunk[..., None])[..., None]
    kv_chunk = jnp.einsum('bhncd,bhnce->bhnde', kw, vc)
    k_chunk = jnp.sum(kw, axis=-2)

    def step(carry, xs):
        c_prev, n_prev, m_prev = carry
        kv_n, k_n, m_n, ft_n = xs
        m_new = jnp.maximum(ft_n + m_prev, m_n)
        a = jnp.exp(ft_n + m_prev - m_new)
        s = jnp.exp(m_n - m_new)
        c_new = a[..., None, None] * c_prev + s[..., None, None] * kv_n
        n_new = a[..., None] * n_prev + s[..., None] * k_n
        return (c_new, n_new, m_new), (c_prev, n_prev, m_prev)

    init = (jnp.zeros((b, h, dh, dh), jnp.float32), jnp.zeros((b, h, dh), jnp.float32),
            jnp.full((b, h), M_INIT, jnp.float32))
    xs = (jnp.moveaxis(kv_chunk, 2, 0), jnp.moveaxis(k_chunk, 2, 0),
          jnp.moveaxis(m_chunk, 2, 0), jnp.moveaxis(f_total, 2, 0))
    _, (c_st, n_st, m_st) = lax.scan(step, init, xs)
    c_st = jnp.moveaxis(c_st, 0, 2)
    n_st = jnp.moveaxis(n_st, 0, 2)
    m_st = jnp.moveaxis(m_st, 0, 2)
    m_inter = cum_f + m_st[..., None]
    m_tot = jnp.maximum(m_inter, m_intra)
    w_inter = jnp.exp(m_inter - m_tot)
    s = jnp.einsum('bhnid,bhnjd->bhnij', qc, kc) * jnp.exp(d_intra - m_tot[..., None])
    num = jnp.einsum('bhnij,bhnjd->bhnid', s, vc) + w_inter[..., None] * jnp.einsum('bhnid,bhnde->bhnie', qc, c_st)
    den = jnp.sum(s, axis=-1) + w_inter * jnp.einsum('bhnid,bhnd->bhni', qc, n_st)
    out = num / jnp.maximum(jnp.abs(den), jnp.exp(-m_tot))[..., None]
    return out.reshape(b, h, l, dh)


def retention_chunkwise(q, k, v, log_gamma):
    b, h, l, dh = q.shape
    n = l // CHUNK
    qc = q.reshape(b, h, n, CHUNK, dh)
    kc = k.reshape(b, h, n, CHUNK, dh)
    vc = v.reshape(b, h, n, CHUNK, dh)
    pos = jnp.arange(CHUNK, dtype=jnp.float32)
    rel = pos[:, None] - pos[None, :]
    lg = log_gamma[:, None, None]
    decay = jnp.where(rel >= 0, jnp.exp(jnp.where(rel >= 0, rel, 0.0) * lg), 0.0)
    scores = jnp.einsum('bhnid,bhnjd->bhnij', qc, kc) * decay[None, :, None]
    out = jnp.einsum('bhnij,bhnje->bhnie', scores, vc)
    k_w = kc * jnp.exp((CHUNK - 1 - pos)[None, :] * log_gamma[:, None])[None, :, None, :, None]
    kv_chunk = jnp.einsum('bhncd,bhnce->bhnde', k_w, vc)
    chunk_decay = jnp.exp(CHUNK * log_gamma)[None, :, None, None]

    def step(r_prev, kv_n):
        return chunk_decay * r_prev + kv_n, r_prev

    _, r_st = lax.scan(step, jnp.zeros((b, h, dh, dh), jnp.float32), jnp.moveaxis(kv_chunk, 2, 0))
    r_st = jnp.moveaxis(r_st, 0, 2)
    q_w = qc * jnp.exp((pos + 1.0)[None, :] * log_gamma[:, None])[None, :, None, :, None]
    out = out + jnp.einsum('bhnid,bhnde->bhnie', q_w, r_st)
    return out.reshape(b, h, l, dh)


def hyena_filter_spectrum(l, w1, b1, w2, b2, w3, freq, deltas):
    f32 = jnp.float32
    t = jnp.linspace(0.0, 1.0, l, dtype=f32)[:, None]
    w = (2.0 * math.pi / l) * jnp.arange(l, dtype=f32)[:, None]
    bands = jnp.linspace(1e-4, HY_BANDS - 1, HY_BANDS, dtype=f32)[None, :]
    feats = jnp.concatenate([t, jnp.cos(bands * w), -jnp.sin(bands * w)], axis=-1)
    fr = freq.astype(f32)
    hid = jnp.sin(fr * (feats @ w1.astype(f32) + b1.astype(f32)))
    hid = jnp.sin(fr * (hid @ w2.astype(f32) + b2.astype(f32)))
    filt = (hid @ w3.astype(f32)).reshape(l, HY_ORDER, 2, HY_WIDTH)
    filt = filt * jnp.exp(-t[:, :, None, None] * jnp.abs(deltas.astype(f32)))
    fwd, bwd = filt[:, :, 0], filt[:, :, 1]
    two_sided = jnp.concatenate([fwd[:1] + bwd[:1], fwd[1:], jnp.zeros_like(fwd[:1]),
                                 jnp.flip(bwd[1:], axis=0)], axis=0)
    two_sided = two_sided / jnp.sum(jnp.abs(two_sided), axis=0, keepdims=True)
    return jnp.fft.rfft(two_sided, axis=0)


def long_conv(u, spec, skip):
    l = u.shape[1]
    y = jnp.fft.irfft(jnp.fft.rfft(u, n=2 * l, axis=1) * spec[None], n=2 * l, axis=1)[:, :l]
    return y + u * skip


def mlstm_branch(ml_qk, ml_v, ml_o, ml_z, ml_gates, conv_w, conv_b, gate_b, norm_g):
    b, l, _ = ml_v.shape
    f32 = jnp.float32
    qk = jax.nn.silu(centred_conv3(ml_qk.astype(f32), conv_w.astype(f32), conv_b.astype(f32)))
    q = _to_heads(qk[..., :ML_WIDTH], ML_HEADS)
    k = _to_heads(qk[..., ML_WIDTH:], ML_HEADS) * (ML_HEAD_DIM ** -0.5)
    v = _to_heads(ml_v.astype(f32), ML_HEADS)
    pre = (ml_gates.astype(f32).reshape(b, l, 4, ML_HEADS) + gate_b.astype(f32)).transpose(2, 0, 3, 1)
    h_fwd = mlstm_chunkwise(q, k, v, pre[0], jax.nn.log_sigmoid(pre[1]))
    h_bwd = _flip(mlstm_chunkwise(_flip(q), _flip(k), _flip(v), _flip(pre[2]),
                                  _flip(jax.nn.log_sigmoid(pre[3]))))
    h = (h_fwd + h_bwd).transpose(0, 2, 1, 3)
    h = jax.nn.sigmoid(ml_o.astype(f32)).reshape(b, l, ML_HEADS, ML_HEAD_DIM) * h
    return head_norm(h, norm_g) * jax.nn.silu(ml_z.astype(f32))


def retention_branch(rt_qkv, rt_z, norm_g):
    b, l, _ = rt_z.shape
    f32 = jnp.float32
    qkv = rt_qkv.astype(f32)
    q = rotary(qkv[..., :RT_WIDTH].reshape(b, l, RT_HEADS, RT_HEAD_DIM)) * (RT_HEAD_DIM ** -0.5)
    k = rotary(qkv[..., RT_WIDTH:2 * RT_WIDTH].reshape(b, l, RT_HEADS, RT_HEAD_DIM))
    q = q.transpose(0, 2, 1, 3)
    k = k.transpose(0, 2, 1, 3)
    v = _to_heads(qkv[..., 2 * RT_WIDTH:], RT_HEADS)
    lg_f = jnp.asarray(RT_LOG_GAMMA_FWD, dtype=f32)
    lg_b = jnp.asarray(RT_LOG_GAMMA_BWD, dtype=f32)
    o = retention_chunkwise(q, k, v, lg_f) + _flip(retention_chunkwise(_flip(q), _flip(k), _flip(v), lg_b))
    return head_norm(o.transpose(0, 2, 1, 3), norm_g) * jax.nn.silu(rt_z.astype(f32))


def hyena_branch(hy_in, hy_z, conv_w, conv_b, spec, skip):
    f32 = jnp.float32
    u = centred_conv3(hy_in.astype(f32), conv_w.astype(f32), conv_b.astype(f32))
    v, x1, x2 = u[..., :HY_WIDTH], u[..., HY_WIDTH:2 * HY_WIDTH], u[..., 2 * HY_WIDTH:]
    sk = skip.astype(f32)
    z = x1 * long_conv(v, spec[:, 0], sk[0])
    z = x2 * long_conv(z, spec[:, 1], sk[1])
    return z * jax.nn.silu(hy_z.astype(f32))


def encoder_trunk(x, norm_g, w_in, ml_conv_w, ml_conv_b, ml_gate_b, ml_norm_g, rt_norm_g,
                  hy_conv_w, hy_conv_b, hy_w1, hy_b1, hy_w2, hy_b2, hy_w3, hy_freq, hy_deltas,
                  hy_skip, w_out, final_g):
    l = x.shape[1]
    split_points = np.cumsum(IN_SIZES)[:-1].tolist()
    for layer in range(DEPTH):
        h = rmsnorm(x, norm_g[layer])
        u = jnp.einsum('bld,de->ble', h, w_in[layer])
        ml_qk, ml_v, ml_o, ml_z, ml_gates, rt_qkv, rt_z, hy_in, hy_z = jnp.split(u, split_points, axis=-1)
        y_ml = mlstm_branch(ml_qk, ml_v, ml_o, ml_z, ml_gates, ml_conv_w[layer], ml_conv_b[layer],
                            ml_gate_b[layer], ml_norm_g[layer])
        y_rt = retention_branch(rt_qkv, rt_z, rt_norm_g[layer])
        spec = hyena_filter_spectrum(l, hy_w1[layer], hy_b1[layer], hy_w2[layer], hy_b2[layer],
                                     hy_w3[layer], hy_freq[layer], hy_deltas[layer])
        y_hy = hyena_branch(hy_in, hy_z, hy_conv_w[layer], hy_conv_b[layer], spec, hy_skip[layer])
        mixed = jnp.concatenate([y_ml, y_rt, y_hy], axis=-1).astype(x.dtype)
        x = x + jnp.einsum('ble,ed->bld', mixed, w_out[layer])
    return rmsnorm(x, final_g)


def setup_inputs(seed: int = 0) -> dict:
    key = jax.random.key(seed)
    ks = jax.random.split(key, 24)
    f32 = jnp.float32

    def nrm(k, shape, scale):
        return scale * jax.random.normal(k, shape, f32)

    x_prompt = nrm(ks[0], (BATCH, SEQ, D_MODEL), 1.0)
    x_sample = nrm(ks[1], (DEC_BATCH, DEC_SEQ, D_MODEL), 1.0)
    norm_g = 1.0 + nrm(ks[2], (DEPTH, D_MODEL), 0.01)
    w_in = nrm(ks[3], (DEPTH, D_MODEL, IN_COLS), D_MODEL ** -0.5)
    ml_conv_w = nrm(ks[4], (DEPTH, SHORT_CONV, 2 * ML_WIDTH), SHORT_CONV ** -0.5)
    ml_conv_b = nrm(ks[5], (DEPTH, 2 * ML_WIDTH), 0.01)
    f_bias = jnp.linspace(3.0, 6.0, ML_HEADS, dtype=f32)
    i_b = nrm(ks[6], (DEPTH, 2, ML_HEADS), 0.1)
    f_n = nrm(ks[7], (DEPTH, 2, ML_HEADS), 0.01)
    ml_gate_b = jnp.stack([i_b[:, 0], f_bias + f_n[:, 0], i_b[:, 1], f_bias + f_n[:, 1]], axis=1)
    ml_norm_g = 1.0 + nrm(ks[8], (DEPTH, ML_WIDTH), 0.01)
    rt_norm_g = 1.0 + nrm(ks[9], (DEPTH, RT_WIDTH), 0.01)
    hy_conv_w = nrm(ks[10], (DEPTH, SHORT_CONV, 3 * HY_WIDTH), SHORT_CONV ** -0.5)
    hy_conv_b = nrm(ks[11], (DEPTH, 3 * HY_WIDTH), 0.01)
    hy_w1 = nrm(ks[12], (DEPTH, HY_POS_DIM, HY_FILTER_HIDDEN), HY_POS_DIM ** -0.5)
    hy_b1 = nrm(ks[13], (DEPTH, HY_FILTER_HIDDEN), 0.01)
    hy_w2 = nrm(ks[14], (DEPTH, HY_FILTER_HIDDEN, HY_FILTER_HIDDEN), HY_FILTER_HIDDEN ** -0.5)
    hy_b2 = nrm(ks[15], (DEPTH, HY_FILTER_HIDDEN), 0.01)
    hy_w3 = nrm(ks[16], (DEPTH, HY_FILTER_HIDDEN, HY_ORDER * 2 * HY_WIDTH), HY_FILTER_HIDDEN ** -0.5)
    hy_freq = 1.0 + nrm(ks[17], (DEPTH, HY_FILTER_HIDDEN), 0.01)
    base_decay = jnp.abs(jnp.linspace(HY_FAST_DECAY, HY_SLOW_DECAY, HY_WIDTH, dtype=f32))
    hy_deltas = base_decay * (1.0 + nrm(ks[18], (DEPTH, HY_ORDER, 2, HY_WIDTH), 0.05))
    hy_skip = nrm(ks[19], (DEPTH, HY_ORDER, HY_WIDTH), 0.1)
    w_out = nrm(ks[20], (DEPTH, D_MIX, D_MODEL), D_MIX ** -0.5)
    final_g = 1.0 + nrm(ks[21], (D_MODEL,), 0.01)
    return {'x_prompt': x_prompt, 'x_sample': x_sample, 'norm_g': norm_g, 'w_in': w_in,
            'ml_conv_w': ml_conv_w, 'ml_conv_b': ml_conv_b, 'ml_gate_b': ml_gate_b,
            'ml_norm_g': ml_norm_g, 'rt_norm_g': rt_norm_g, 'hy_conv_w': hy_conv_w,
            'hy_conv_b': hy_conv_b, 'hy_w1': hy_w1, 'hy_b1': hy_b1, 'hy_w2': hy_w2, 'hy_b2': hy_b2,
            'hy_w3': hy_w3, 'hy_freq': hy_freq, 'hy_deltas': hy_deltas, 'hy_skip': hy_skip,
            'w_out': w_out, 'final_g': final_g}


def reference(x_prompt, x_sample, norm_g, w_in, ml_conv_w, ml_conv_b, ml_gate_b, ml_norm_g,
              rt_norm_g, hy_conv_w, hy_conv_b, hy_w1, hy_b1, hy_w2, hy_b2, hy_w3, hy_freq,
              hy_deltas, hy_skip, w_out, final_g):
    y_prompt = encoder_trunk(x_prompt, norm_g, w_in, ml_conv_w, ml_conv_b, ml_gate_b, ml_norm_g,
                             rt_norm_g, hy_conv_w, hy_conv_b, hy_w1, hy_b1, hy_w2, hy_b2, hy_w3,
                             hy_freq, hy_deltas, hy_skip, w_out, final_g)
    y_sample = encoder_trunk(x_sample, norm_g, w_in, ml_conv_w, ml_conv_b, ml_gate_b, ml_norm_g,
                             rt_norm_g, hy_conv_w, hy_conv_b, hy_w1, hy_b1, hy_w2, hy_b2, hy_w3,
                             hy_freq, hy_deltas, hy_skip, w_out, final_g)
    return (y_prompt, y_sample)
```

```python
import math
from contextlib import ExitStack
import numpy as np
import ml_dtypes
import concourse.bass as bass
import concourse.mybir as mybir
from concourse.bass_utils import run_bass_kernel_spmd

F32 = mybir.dt.float32
BF16 = mybir.dt.bfloat16
ALU = mybir.AluOpType
AF = mybir.ActivationFunctionType
AX = mybir.AxisListType

D = 1024
DEPTH = 4
LA = 4096
LB = 16384
P = 4096
NF = 8192
NKC = 33
INC = 6672
UT = 5648
C_MLV, C_MLO, C_MLZ, C_G, C_RQ, C_RK, C_RV, C_RZ, C_HV, C_H1, C_H2, C_HZ = (
    0, 512, 1024, 1536, 1552, 2064, 2576, 3088, 3600, 4112, 4624, 5136)
RT_LGF = [math.log(1.0 - 2.0 ** (-5.0 - h)) for h in range(4)]
RT_LGB = [math.log(1.0 - 2.0 ** (-5.5 - h)) for h in range(4)]
HY_BANDS = 16
S_ML = 128 ** -0.5
S_RT = 128 ** -0.5
TWO_PI = 2.0 * math.pi


class Prog:
    NDMA = 6

    def __init__(self, nc, es):
        self.nc = nc
        self.eng = {'pe': nc.tensor, 'act': nc.scalar, 'dve': nc.vector, 'pool': nc.gpsimd, 'sp': nc.sync}
        self.sem = {}
        for k in ['pe', 'act', 'dve', 'pool']:
            self.sem[k] = es.enter_context(nc.semaphore('s_' + k))
        self.dq = {}
        for q in ['sp', 'pool', 'act']:
            self.dq[q] = 0
            for i in range(self.NDMA):
                self.sem[('d', q, i)] = es.enter_context(nc.semaphore('d_%s%d' % (q, i)))
        self.cnt = {k: 0 for k in ['pe', 'act', 'dve', 'pool']}
        self.last = {}
        self.seen = {k: {} for k in self.eng}
        self.state = {}
        self.ninst = 0

    def _deps(self, e, reads, writes):
        deps = {}

        def add(k, v):
            if deps.get(k, 0) < v:
                deps[k] = v
        for key in reads:
            st = self.state.get(key)
            if st and st[0]:
                add(*st[0])
        for key in writes:
            st = self.state.get(key)
            if st:
                if st[0]:
                    add(*st[0])
                for k, v in st[1].items():
                    if k == e:
                        continue
                    add(k, v)
        for k, v in deps.items():
            if e == 'pe' and k == 'pe':
                continue
            if self.seen[e].get(k, 0) >= v:
                continue
            self.seen[e][k] = v
            self.eng[e].wait_ge(self.sem[k], v)
            self.ninst += 1

    def _record(self, reads, writes, tok):
        for key in reads:
            st = self.state.get(key)
            if st is None:
                st = self.state[key] = [None, {}]
            if st[1].get(tok[0], 0) < tok[1]:
                st[1][tok[0]] = tok[1]
        for key in writes:
            self.state[key] = [tok, {}]
        self.last[tok[0]] = tok[1]

    def op(self, e, fn, reads=(), writes=()):
        self._deps(e, reads, writes)
        self.cnt[e] += 1
        tok = (e, self.cnt[e])
        fn(self.eng[e]).then_inc(self.sem[e], 1)
        self.ninst += 1
        self._record(reads, writes, tok)

    def dma(self, q, out, in_, reads=(), writes=(), **kw):
        n = self.dq[q]
        self.dq[q] += 1
        slot = n % self.NDMA
        val = 16 * (n // self.NDMA + 1)
        key = ('d', q, slot)
        self._deps(q, reads, writes)
        if val > 16 and self.seen[q].get(key, 0) < val - 16:
            self.seen[q][key] = val - 16
            self.eng[q].wait_ge(self.sem[key], val - 16)
        self.eng[q].dma_start(out=out, in_=in_, **kw).then_inc(self.sem[key], 16)
        self.ninst += 1
        self._record(reads, writes, (key, val))

    def barrier(self):
        for e in self.eng:
            for k, v in self.last.items():
                if self.seen[e].get(k, 0) >= v:
                    continue
                self.seen[e][k] = v
                self.eng[e].wait_ge(self.sem[k], v)
        self.state = {}


class Ctx:
    pass


USPLIT = 3088


class USplit:
    def __init__(self, u1, u2):
        self.u1, self.u2 = u1, u2

    def __getitem__(self, key):
        rows, cols = key
        a, b = cols.start, cols.stop
        if b <= USPLIT:
            return self.u1[rows, a:b]
        assert a >= USPLIT, (a, b)
        return self.u2[rows, a - USPLIT:b - USPLIT]

    def pieces(self, rows, a, b):
        out = []
        if a < USPLIT:
            e = min(b, USPLIT)
            out.append((self.u1[rows, a:e], 0, e - a))
        if b > USPLIT:
            st = max(a, USPLIT)
            out.append((self.u2[rows, st - USPLIT:b - USPLIT], st - a, b - a))
        return out


def bf(a):
    return np.ascontiguousarray(a).astype(ml_dtypes.bfloat16)


_CONST = None


def host_consts():
    global _CONST
    if _CONST is not None:
        return _CONST
    c = {}
    c['identb'] = bf(np.eye(128))
    c['identf'] = np.eye(128, dtype=np.float32)
    c['jrev'] = np.ascontiguousarray(np.eye(128, dtype=np.float32)[::-1])
    sel = np.zeros((8, 8, 128), np.float32)
    for r in range(8):
        sel[r, r, :] = 1.0
    c['sel'] = sel
    j = np.arange(128)[:, None]
    i = np.arange(128)[None, :]
    c['maskf'] = (S_ML * (j <= i)).astype(np.float32)
    c['maskb'] = (S_ML * (j >= i)).astype(np.float32)
    dct = np.zeros((128, 4, 128), np.float64)
    qwf = np.zeros((128, 4, 128), np.float64)
    qwb = np.zeros((128, 4, 128), np.float64)
    kwf = np.zeros((128, 4), np.float64)
    kwb = np.zeros((128, 4), np.float64)
    for h in range(4):
        gf, gb = RT_LGF[h], RT_LGB[h]
        dct[:, h, :] = S_RT * (np.where(i >= j, np.exp(gf * np.maximum(i - j, 0)), 0.0)
                               + np.where(j >= i, np.exp(gb * np.maximum(j - i, 0)), 0.0))
        qwf[:, h, :] = S_RT * np.exp(gf * (i + 1.0))
        qwb[:, h, :] = S_RT * np.exp(gb * (128.0 - i))
        kwf[:, h] = np.exp(gf * (127.0 - j[:, 0]))
        kwb[:, h] = np.exp(gb * (j[:, 0] * 1.0))
    c['dct'] = dct.astype(np.float32)
    c['qwf'] = qwf.astype(np.float32)
    c['qwb'] = qwb.astype(np.float32)
    c['kwf'] = kwf.astype(np.float32)
    c['kwb'] = kwb.astype(np.float32)
    inv = (np.float32(10000.0) ** (-np.arange(0, 128, 2, dtype=np.float32) / np.float32(128))).astype(np.float32)
    ang = (np.arange(LB, dtype=np.float32)[:, None] * inv[None, :]).astype(np.float32)
    c['rot'] = np.concatenate([np.cos(ang), np.sin(ang)], axis=1).astype(np.float32)
    for name, L, nb in (('a', LA, 1), ('b', LB, 4)):
        t = np.linspace(0.0, 1.0, L, dtype=np.float32)
        w = (np.float32(2.0 * math.pi / L) * np.arange(L, dtype=np.float32)).astype(np.float32)
        bands = np.linspace(1e-4, HY_BANDS - 1, HY_BANDS, dtype=np.float32)
        feats = np.concatenate([t[:, None], np.cos(bands[None, :] * w[:, None]),
                                -np.sin(bands[None, :] * w[:, None])], axis=1).astype(np.float32)
        pieces = list(range(-(nb - 1), nb))
        ft = np.zeros((len(pieces), 2, 33, P), np.float32)
        tn = np.zeros((len(pieces), 2, 128, 32), np.float32)
        for pi, d in enumerate(pieces):
            for half in range(2):
                jj = np.arange(P) + half * P
                m = np.where(jj < P, jj, jj - NF)
                lag = np.abs(d * P + m)
                lag = np.minimum(lag, L - 1)
                ft[pi, half] = feats[lag].T
                tn[pi, half] = (-t[lag]).reshape(32, 128).T
        c['feat_' + name] = ft
        c['ntn_' + name] = tn
    s = np.arange(NF, dtype=np.int64)
    k = np.arange(NKC * 128, dtype=np.int64)
    ph = (s[:, None] * k[None, :]) % NF
    angm = (2.0 * np.pi / NF) * ph
    valid = (k <= NF // 2)[None, :]
    cosm = np.where(valid, np.cos(angm), 0.0)
    sinm = np.where(valid, np.sin(angm), 0.0)
    fc = cosm.reshape(64, 128, NKC, 128)
    fs = (-sinm).reshape(64, 128, NKC, 128)
    ftab = np.stack([fc, fs], axis=0)
    c['fwd'] = bf(ftab.transpose(3, 2, 1, 0, 4))
    wk = np.where((k == 0) | (k == NF // 2), 1.0, 2.0) / NF
    ci = (cosm[:P] * wk[None, :]).T
    si = (-sinm[:P] * wk[None, :]).T
    ci = ci.reshape(NKC, 128, 32, 128)
    si = si.reshape(NKC, 128, 32, 128)
    itab = np.stack([ci, si], axis=0)
    c['inv'] = bf(itab.transpose(3, 2, 1, 0, 4))
    _CONST = c
    return c


CONST_SPECS = {
    'identb': ([128, 128], BF16), 'identf': ([128, 128], F32), 'jrev': ([128, 128], F32),
    'sel': ([8, 8, 128], F32), 'maskf': ([128, 128], F32), 'maskb': ([128, 128], F32),
    'dct': ([128, 4, 128], F32), 'qwf': ([128, 4, 128], F32), 'qwb': ([128, 4, 128], F32),
    'kwf': ([128, 4], F32), 'kwb': ([128, 4], F32), 'rot': ([LB, 128], F32),
    'feat_a': ([1, 2, 33, P], F32), 'ntn_a': ([1, 2, 128, 32], F32),
    'feat_b': ([7, 2, 33, P], F32), 'ntn_b': ([7, 2, 128, 32], F32),
    'fwd': ([NKC, 128, 64, 2, 128], BF16), 'inv': ([32, 128, NKC, 2, 128], BF16),
}
WEIGHT_SPECS = {
    'norm_g': [DEPTH, D], 'w_in': [DEPTH, D, INC], 'ml_conv_w': [DEPTH, 3, 1024], 'ml_conv_b': [DEPTH, 1024],
    'ml_gate_b': [DEPTH, 4, 4], 'ml_norm_g': [DEPTH, 512], 'rt_norm_g': [DEPTH, 512],
    'hy_conv_w': [DEPTH, 3, 1536], 'hy_conv_b': [DEPTH, 1536], 'hy_w1': [DEPTH, 33, 64], 'hy_b1': [DEPTH, 64],
    'hy_w2': [DEPTH, 64, 64], 'hy_b2': [DEPTH, 64], 'hy_w3': [DEPTH, 64, 2048], 'hy_freq': [DEPTH, 64],
    'hy_deltas': [DEPTH, 2, 2, 512], 'hy_skip': [DEPTH, 2, 512], 'w_out': [DEPTH, 1536, D], 'final_g': [D],
}


def bcast_rows(ap1d, nparts):
    return ap1d.partition_broadcast(nparts)


_UID = [0]


def sb(nc, es, name, shape, dt):
    _UID[0] += 1
    return es.enter_context(nc.sbuf_tensor('%s_%d' % (name, _UID[0]), shape, dt))


def ps(nc, es, name, shape, dt=F32):
    _UID[0] += 1
    return es.enter_context(nc.psum_tensor('%s_%d' % (name, _UID[0]), shape, dt))


def rms_rows(G, xt, key_x, junk, ss, rstd, keyp):
    Pg = G.P
    Pg.op('act', lambda e: e.activation(out=junk[:], in_=xt, func=AF.Square), reads=[key_x], writes=[keyp + 'junk'])
    Pg.op('dve', lambda e: e.reduce_sum(out=ss[:], in_=junk[:], axis=AX.X), reads=[keyp + 'junk'], writes=[keyp + 'ss'])
    Pg.op('dve', lambda e: e.tensor_scalar(out=rstd[:], in0=ss[:], scalar1=1.0 / D, scalar2=1e-6,
                                           op0=ALU.mult, op1=ALU.add), reads=[keyp + 'ss'], writes=[keyp + 'rstd'])
    Pg.op('act', lambda e: e.sqrt(out=rstd[:], in_=rstd[:]), reads=[keyp + 'rstd'], writes=[keyp + 'rstd'])
    Pg.op('dve', lambda e: e.reciprocal(out=rstd[:], in_=rstd[:]), reads=[keyp + 'rstd'], writes=[keyp + 'rstd'])


def phase_A(G, layer, seq, blk):
    nc, Pg = G.nc, G.P
    xsrc = seq.xin if layer == 0 else seq.X
    t0 = blk * P
    with ExitStack() as es:
        hT = sb(nc, es, 'a_hT', [128, 8, P], BF16)
        gbc = sb(nc, es, 'a_gbc', [128, D], F32)
        xt = [sb(nc, es, 'a_xt%d' % i, [128, D], F32) for i in range(2)]
        junk = sb(nc, es, 'a_junk', [128, D], F32)
        ss = sb(nc, es, 'a_ss', [128, 1], F32)
        rstd = sb(nc, es, 'a_rstd', [128, 1], F32)
        h16 = [sb(nc, es, 'a_h16%d' % i, [128, D], BF16) for i in range(2)]
        wst = [sb(nc, es, 'a_wst%d' % i, [128, 8, 512], F32) for i in range(2)]
        w16 = [sb(nc, es, 'a_w16%d' % i, [128, 8, 512], BF16) for i in range(2)]
        ost = [sb(nc, es, 'a_ost%d' % i, [128, 512], F32) for i in range(4)]
        tp = [ps(nc, es, 'a_tp%d' % i, [128, 512], BF16) for i in range(2)]
        mm = [ps(nc, es, 'a_mm%d' % i, [128, 512], F32) for i in range(4)]
        Pg.dma('sp', gbc[:], bcast_rows(G.w['norm_g'][layer], 128), writes=['a_gbc'])
        for t in range(32):
            b = t % 2
            Pg.dma('sp', xt[b][:], xsrc[t0 + t * 128: t0 + (t + 1) * 128, :], writes=['a_xt%d' % b])
            rms_rows(G, xt[b][:], 'a_xt%d' % b, junk, ss, rstd, 'a_')
            Pg.op('dve', lambda e: e.scalar_tensor_tensor(out=h16[b][:], in0=xt[b][:], scalar=rstd[:, 0:1], in1=gbc[:],
                                                          op0=ALU.mult, op1=ALU.mult),
                  reads=['a_xt%d' % b, 'a_rstd', 'a_gbc'], writes=['a_h16%d' % b])
            for half in range(2):
                for jj in range(4):
                    kc = half * 4 + jj
                    Pg.op('pe', lambda e: e.transpose(tp[half][:, jj * 128:(jj + 1) * 128],
                                                      h16[b][:, kc * 128:(kc + 1) * 128], G.identb[:]),
                          reads=['a_h16%d' % b], writes=['a_tp%d' % half])
                dst = hT[:, half * 4:half * 4 + 4, t * 128:(t + 1) * 128]
                src = tp[half][:].rearrange('p (j t) -> p j t', j=4)
                if half == 0:
                    Pg.op('act', lambda e: e.copy(out=dst, in_=src), reads=['a_tp0'], writes=[('a_hT', t, 0)])
                else:
                    Pg.op('dve', lambda e: e.tensor_copy(out=dst, in_=src), reads=['a_tp1'], writes=[('a_hT', t, 1)])
        groups = [('f', g * 128, 128) for g in range(8)] + [('t', 1024 + g * 512, 512) for g in range(11)] + [('t', 1024 + 11 * 512, 16)]
        oi = 0
        for gi, (kind, c0, wd) in enumerate(groups):
            wb = gi % 2
            wsrc = G.w['w_in'][layer, :, c0:c0 + wd].rearrange('(kc p) c -> p kc c', p=128)
            Pg.dma('sp', wst[wb][:, :, 0:wd], wsrc, writes=['a_wst%d' % wb])
            ceng = 'pool' if gi % 2 == 0 else 'dve'
            Pg.op(ceng, lambda e: e.tensor_copy(out=w16[wb][:, :, 0:wd], in_=wst[wb][:, :, 0:wd]),
                  reads=['a_wst%d' % wb], writes=['a_w16%d' % wb])
            for t in range(32 if kind == 't' else 8):
                pb = oi % 4
                if kind == 't':
                    for kc in range(8):
                        Pg.op('pe', lambda e: e.matmul(mm[pb][:, 0:wd], lhsT=hT[:, kc, t * 128:(t + 1) * 128],
                                                       rhs=w16[wb][:, kc, 0:wd], start=(kc == 0), stop=(kc == 7)),
                              reads=[('a_hT', t, 0), ('a_hT', t, 1), 'a_w16%d' % wb], writes=['a_mm%d' % pb])
                    dst = None
                    ow = wd
                else:
                    hk = [('a_hT', 4 * t + q, hh) for q in range(4) for hh in range(2)]
                    for kc in range(8):
                        Pg.op('pe', lambda e: e.matmul(mm[pb][:, :], lhsT=w16[wb][:, kc, 0:128],
                                                       rhs=hT[:, kc, t * 512:(t + 1) * 512], start=(kc == 0), stop=(kc == 7)),
                              reads=hk + ['a_w16%d' % wb], writes=['a_mm%d' % pb])
                    dst = seq.QKT[c0:c0 + 128, t0 + t * 512:t0 + (t + 1) * 512]
                    ow = 512
                if oi % 2 == 0:
                    Pg.op('act', lambda e: e.copy(out=ost[pb][:, 0:ow], in_=mm[pb][:, 0:ow]),
                          reads=['a_mm%d' % pb], writes=['a_ost%d' % pb])
                else:
                    Pg.op('dve', lambda e: e.tensor_copy(out=ost[pb][:, 0:ow], in_=mm[pb][:, 0:ow]),
                          reads=['a_mm%d' % pb], writes=['a_ost%d' % pb])
                if dst is None:
                    for (dap, o0, o1) in seq.U.pieces(slice(t0 + t * 128, t0 + (t + 1) * 128), c0 - 1024, c0 - 1024 + wd):
                        Pg.dma('pool' if oi % 2 == 0 else 'sp', dap, ost[pb][:, o0:o1], reads=['a_ost%d' % pb])
                else:
                    Pg.dma('pool' if oi % 2 == 0 else 'sp', dst, ost[pb][:, 0:ow], reads=['a_ost%d' % pb])
                oi += 1
    Pg.barrier()


def phase_O(G, layer, seq):
    nc, Pg = G.nc, G.P
    xsrc = seq.xin if layer == 0 else seq.X
    last = layer == DEPTH - 1
    with ExitStack() as es:
        wo = sb(nc, es, 'o_wo', [128, 12, D], BF16)
        wst = [sb(nc, es, 'o_wst%d' % i, [128, D], F32) for i in range(2)]
        mix = [sb(nc, es, 'o_mix%d' % i, [128, 1536], F32) for i in range(2)]
        m16 = [sb(nc, es, 'o_m16%d' % i, [128, 1536], BF16) for i in range(2)]
        mT = [sb(nc, es, 'o_mT%d' % i, [128, 12, 128], BF16) for i in range(2)]
        xt = [sb(nc, es, 'o_xt%d' % i, [128, D], F32) for i in range(2)]
        xn = [sb(nc, es, 'o_xn%d' % i, [128, D], F32) for i in range(2)]
        gbc = sb(nc, es, 'o_gbc', [128, D], F32)
        junk = sb(nc, es, 'o_junk', [128, D], F32)
        ss = sb(nc, es, 'o_ss', [128, 1], F32)
        rstd = sb(nc, es, 'o_rstd', [128, 1], F32)
        yo = [sb(nc, es, 'o_yo%d' % i, [128, D], F32) for i in range(2)]
        tp = [ps(nc, es, 'o_tp%d' % i, [128, 512], BF16) for i in range(3)]
        mm = [ps(nc, es, 'o_mm%d' % i, [128, 512], F32) for i in range(4)]
        for kc in range(12):
            b = kc % 2
            Pg.dma('sp', wst[b][:], G.w['w_out'][layer, kc * 128:(kc + 1) * 128, :], writes=['o_wst%d' % b])
            Pg.op('pool' if b else 'dve', lambda e: e.tensor_copy(out=wo[:, kc, :], in_=wst[b][:]),
                  reads=['o_wst%d' % b], writes=[('o_wo', kc)])
        wokeys = [('o_wo', kc) for kc in range(12)]
        if last:
            Pg.dma('sp', gbc[:], bcast_rows(G.w['final_g'], 128), writes=['o_gbc'])
        for t in range(seq.L // 128):
            b = t % 2
            rows = slice(t * 128, (t + 1) * 128)
            Pg.dma('sp', mix[b][:], seq.MIX[rows, :], writes=['o_mix%d' % b])
            Pg.dma('sp', xt[b][:], xsrc[rows, :], writes=['o_xt%d' % b])
            Pg.op('pool', lambda e: e.tensor_copy(out=m16[b][:], in_=mix[b][:]), reads=['o_mix%d' % b], writes=['o_m16%d' % b])
            for g3 in range(3):
                for jj in range(4):
                    kc = g3 * 4 + jj
                    Pg.op('pe', lambda e: e.transpose(tp[g3][:, jj * 128:(jj + 1) * 128],
                                                      m16[b][:, kc * 128:(kc + 1) * 128], G.identb[:]),
                          reads=['o_m16%d' % b], writes=['o_tp%d' % g3])
                dst = mT[b][:, g3 * 4:g3 * 4 + 4, :]
                src = tp[g3][:].rearrange('p (j t) -> p j t', j=4)
                if g3 == 1:
                    Pg.op('dve', lambda e: e.tensor_copy(out=dst, in_=src), reads=['o_tp%d' % g3], writes=[('o_mT%d' % b, g3)])
                else:
                    Pg.op('act', lambda e: e.copy(out=dst, in_=src), reads=['o_tp%d' % g3], writes=[('o_mT%d' % b, g3)])
            for n in range(2):
                pb = (t * 2 + n) % 4
                for kc in range(12):
                    Pg.op('pe', lambda e: e.matmul(mm[pb][:], lhsT=mT[b][:, kc, :], rhs=wo[:, kc, n * 512:(n + 1) * 512],
                                                   start=(kc == 0), stop=(kc == 11)),
                          reads=[('o_mT%d' % b, g3) for g3 in range(3)] + wokeys, writes=['o_mm%d' % pb])
                Pg.op('dve', lambda e: e.tensor_tensor(out=xn[b][:, n * 512:(n + 1) * 512], in0=mm[pb][:],
                                                       in1=xt[b][:, n * 512:(n + 1) * 512], op=ALU.add),
                      reads=['o_mm%d' % pb, 'o_xt%d' % b], writes=[('o_xn%d' % b, n)])
            xk = [('o_xn%d' % b, 0), ('o_xn%d' % b, 1)]
            if not last:
                Pg.dma('pool', seq.X[rows, :], xn[b][:], reads=xk)
            else:
                Pg.op('pool', lambda e: e.tensor_copy(out=xn[b][:, 0:1], in_=xn[b][:, 0:1]), reads=xk, writes=['o_xnf%d' % b])
                rms_rows(G, xn[b][:], 'o_xnf%d' % b, junk, ss, rstd, 'o_')
                Pg.op('dve', lambda e: e.scalar_tensor_tensor(out=yo[b][:], in0=xn[b][:], scalar=rstd[:, 0:1], in1=gbc[:],
                                                              op0=ALU.mult, op1=ALU.mult),
                      reads=xk + ['o_xnf%d' % b, 'o_rstd', 'o_gbc'], writes=['o_yo%d' % b])
                Pg.dma('pool', seq.yout[rows, :], yo[b][:], reads=['o_yo%d' % b])
    Pg.barrier()


def headnorm_gate(G, pre, h, hkeys, z32, zkey, gbc, gkey, dst, dma_q):
    Pg = G.P
    T = G.hn
    h3 = h[:].rearrange('p (a d) -> p a d', a=4)
    Pg.op('dve', lambda e: e.reduce_sum(out=T['s1'][:], in_=h3, axis=AX.X), reads=hkeys, writes=[pre + 's1'])
    Pg.op('dve', lambda e: e.tensor_scalar(out=T['s1'][:], in0=T['s1'][:], scalar1=1.0 / 128, scalar2=None, op0=ALU.mult),
          reads=[pre + 's1'], writes=[pre + 's1'])
    hc3 = T['hc'][:].rearrange('p (a d) -> p a d', a=4)
    Pg.op('dve', lambda e: e.tensor_tensor(out=hc3, in0=h3, in1=T['s1'][:].unsqueeze(2).to_broadcast([128, 4, 128]),
                                           op=ALU.subtract), reads=hkeys + [pre + 's1'], writes=[pre + 'hc'])
    Pg.op('act', lambda e: e.activation(out=T['sq'][:], in_=T['hc'][:], func=AF.Square), reads=[pre + 'hc'], writes=[pre + 'sq'])
    Pg.op('dve', lambda e: e.reduce_sum(out=T['s2'][:], in_=T['sq'][:].rearrange('p (a d) -> p a d', a=4), axis=AX.X),
          reads=[pre + 'sq'], writes=[pre + 's2'])
    Pg.op('dve', lambda e: e.tensor_scalar(out=T['s2'][:], in0=T['s2'][:], scalar1=1.0 / 128, scalar2=1e-5,
                                           op0=ALU.mult, op1=ALU.add), reads=[pre + 's2'], writes=[pre + 's2'])
    Pg.op('act', lambda e: e.sqrt(out=T['s2'][:], in_=T['s2'][:]), reads=[pre + 's2'], writes=[pre + 's2'])
    Pg.op('dve', lambda e: e.reciprocal(out=T['s2'][:], in_=T['s2'][:]), reads=[pre + 's2'], writes=[pre + 's2'])
    Pg.op('dve', lambda e: e.tensor_tensor(out=hc3, in0=hc3, in1=T['s2'][:].unsqueeze(2).to_broadcast([128, 4, 128]),
                                           op=ALU.mult), reads=[pre + 'hc', pre + 's2'], writes=[pre + 'hc'])
    Pg.op('pool', lambda e: e.tensor_tensor(out=T['hc'][:], in0=T['hc'][:], in1=gbc[:], op=ALU.mult),
          reads=[pre + 'hc', gkey], writes=[pre + 'hc'])
    Pg.op('act', lambda e: e.activation(out=T['sq'][:], in_=z32, func=AF.Silu), reads=[zkey, pre + 'sq'], writes=[pre + 'sq'])
    Pg.op('dve', lambda e: e.tensor_tensor(out=T['out'][:], in0=T['hc'][:], in1=T['sq'][:], op=ALU.mult),
          reads=[pre + 'hc', pre + 'sq'], writes=[pre + 'out'])
    Pg.dma(dma_q, dst, T['out'][:], reads=[pre + 'out'])


def alloc_hn(G, es, pre):
    nc = G.nc
    G.hn = {'s1': sb(nc, es, pre + 's1', [128, 4], F32), 's2': sb(nc, es, pre + 's2', [128, 4], F32),
            'hc': sb(nc, es, pre + 'hc', [128, 512], F32), 'sq': sb(nc, es, pre + 'sq', [128, 512], F32),
            'out': sb(nc, es, pre + 'out', [128, 512], F32)}


def phase_G(G, layer, seq):
    nc, Pg = G.nc, G.P
    L = seq.L
    NC = L // 128
    nseg = L // P
    gb = G.w['ml_gate_b'][layer].rearrange('(t s) h -> s t h', s=2)
    with ExitStack() as es:
        identf = sb(nc, es, 'g_identf', [128, 128], F32)
        jrev = sb(nc, es, 'g_jrev', [128, 128], F32)
        sel = sb(nc, es, 'g_sel', [8, 8, 128], F32)
        ones = sb(nc, es, 'g_ones', [8, P], F32)
        one1 = sb(nc, es, 'g_one1', [8, 1], F32)
        bi8 = sb(nc, es, 'g_bi8', [8, 1], F32)
        bf8 = sb(nc, es, 'g_bf8', [8, 1], F32)
        cr = sb(nc, es, 'g_cr', [8, 1], F32)
        cnb = sb(nc, es, 'g_cnb', [8, 1], F32)
        gtF = sb(nc, es, 'g_gtF', [128, 32, 16], F32)
        gtB = sb(nc, es, 'g_gtB', [128, 32, 16], F32)
        liF = sb(nc, es, 'g_liF', [128, 32, 8], F32)
        liB = sb(nc, es, 'g_liB', [128, 32, 8], F32)
        pfF = sb(nc, es, 'g_pfF', [128, 32, 8], F32)
        pfB = sb(nc, es, 'g_pfB', [128, 32, 8], F32)
        LI = sb(nc, es, 'g_LI', [8, P], F32)
        SP = sb(nc, es, 'g_SP', [8, P], F32)
        NB = sb(nc, es, 'g_NB', [8, P], F32)
        AL = sb(nc, es, 'g_AL', [8, P], F32)
        RR = sb(nc, es, 'g_RR', [8, P], F32)
        EN = sb(nc, es, 'g_EN', [8, P], F32)
        KW = sb(nc, es, 'g_KW', [8, P], F32)
        CC = sb(nc, es, 'g_CC', [8, P], F32)
        RP = sb(nc, es, 'g_RP', [8, 33], F32)
        AA = sb(nc, es, 'g_AA', [8, 32], F32)
        T1s = [sb(nc, es, 'g_T1s%d' % i, [128, 24], F32) for i in range(2)]
        GTF = sb(nc, es, 'g_GTF', [128, 32, 12], F32)
        GTB = sb(nc, es, 'g_GTB', [128, 32, 12], F32)
        ABCs = sb(nc, es, 'g_ABCs', [128, 8, 32], F32)
        li_ps = [ps(nc, es, 'g_lips%d' % i, [8, 512]) for i in range(2)]
        pf_ps = [ps(nc, es, 'g_pfps%d' % i, [8, 512]) for i in range(2)]
        t_ps = [ps(nc, es, 'g_tps%d' % i, [128, 24]) for i in range(2)]
        t2_ps = ps(nc, es, 'g_t2ps', [128, 24])
        bc_ps = ps(nc, es, 'g_bcps', [128, 256])
        Pg.dma('sp', identf[:], G.c['identf'][:, :], writes=['g_identf'])
        Pg.dma('sp', jrev[:], G.c['jrev'][:, :], writes=['g_jrev'])
        Pg.dma('sp', sel[:], G.c['sel'][:, :, :], writes=['g_sel'])
        gbl = G.w['ml_gate_b'][layer]
        Pg.dma('sp', bi8[0:4, :], gbl[0].unsqueeze(1), writes=['g_bi8'])
        Pg.dma('sp', bi8[4:8, :], gbl[2].unsqueeze(1), writes=['g_bi8'])
        Pg.dma('sp', bf8[0:4, :], gbl[1].unsqueeze(1), writes=['g_bf8'])
        Pg.dma('sp', bf8[4:8, :], gbl[3].unsqueeze(1), writes=['g_bf8'])
        Pg.op('dve', lambda e: e.tensor_scalar(out=bf8[:], in0=bf8[:], scalar1=-1.0, scalar2=None, op0=ALU.mult),
              reads=['g_bf8'], writes=['g_bf8'])
        Pg.op('pool', lambda e: e.memset(ones[:], 1.0), writes=['g_ones'])
        Pg.op('pool', lambda e: e.memset(one1[:], 1.0), writes=['g_one1'])
        Pg.op('pool', lambda e: e.memset(cr[:], -1e30), writes=['g_cr'])
        Pg.op('pool', lambda e: e.memset(cnb[:], 0.0), writes=['g_cnb'])
        for t_, nm in ((liF, 'g_liF'), (liB, 'g_liB'), (pfF, 'g_pfF'), (pfB, 'g_pfB')):
            Pg.op('pool', lambda e: e.memset(t_[:], 0.0), writes=[nm])
        for sg in range(nseg):
            c0 = NC - 32 * (sg + 1)
            Pg.dma('sp', gtF[:], seq.U[sg * P:(sg + 1) * P, C_G:C_G + 16].rearrange('(n p) c -> p n c', p=128),
                   writes=['g_gtF'])
            Pg.dma('pool', gtB[:], seq.U[c0 * 128:(c0 + 32) * 128, C_G:C_G + 16].rearrange('(n p) c -> p n c', p=128),
                   writes=['g_gtB'])
            Pg.op('dve', lambda e: e.tensor_copy(out=liF[:, :, 0:4], in_=gtF[:, :, 0:4]), reads=['g_gtF'], writes=['g_liF'])
            Pg.op('dve', lambda e: e.tensor_copy(out=pfF[:, :, 0:4], in_=gtF[:, :, 4:8]), reads=['g_gtF'], writes=['g_pfF'])
            Pg.op('dve', lambda e: e.tensor_copy(out=liB[:, :, 4:8], in_=gtB[:, :, 8:12]), reads=['g_gtB'], writes=['g_liB'])
            Pg.op('dve', lambda e: e.tensor_copy(out=pfB[:, :, 4:8], in_=gtB[:, :, 12:16]), reads=['g_gtB'], writes=['g_pfB'])
            for gq in range(8):
                pb = gq % 2
                for mm_ in range(4):
                    m = gq * 4 + mm_
                    cs = slice(mm_ * 128, (mm_ + 1) * 128)
                    Pg.op('pe', lambda e: e.matmul(li_ps[pb][:, cs], lhsT=liF[:, m, :], rhs=identf[:], start=True, stop=False),
                          reads=['g_liF', 'g_identf'], writes=['g_lips%d' % pb])
                    Pg.op('pe', lambda e: e.matmul(li_ps[pb][:, cs], lhsT=liB[:, 31 - m, :], rhs=jrev[:], start=False, stop=True),
                          reads=['g_liB', 'g_jrev'], writes=['g_lips%d' % pb])
                    Pg.op('pe', lambda e: e.matmul(pf_ps[pb][:, cs], lhsT=pfF[:, m, :], rhs=identf[:], start=True, stop=False),
                          reads=['g_pfF', 'g_identf'], writes=['g_pfps%d' % pb])
                    Pg.op('pe', lambda e: e.matmul(pf_ps[pb][:, cs], lhsT=pfB[:, 31 - m, :], rhs=jrev[:], start=False, stop=True),
                          reads=['g_pfB', 'g_jrev'], writes=['g_pfps%d' % pb])
                gs = slice(gq * 512, (gq + 1) * 512)
                Pg.op('act', lambda e: e.activation(out=LI[:, gs], in_=li_ps[pb][:], func=AF.Identity, bias=bi8[:, 0:1], scale=1.0),
                      reads=['g_lips%d' % pb, 'g_bi8'], writes=['g_LI'])
                Pg.op('act', lambda e: e.activation(out=SP[:, gs], in_=pf_ps[pb][:], func=AF.Exp, bias=bf8[:, 0:1], scale=-1.0),
                      reads=['g_pfps%d' % pb, 'g_bf8'], writes=['g_SP'])
            Pg.op('act', lambda e: e.activation(out=SP[:], in_=SP[:], func=AF.Ln, bias=one1[:, 0:1], scale=1.0),
                  reads=['g_SP', 'g_one1'], writes=['g_SP'])
            Pg.op('dve', lambda e: e.tensor_tensor_scan(out=NB[:], data0=ones[:], data1=SP[:], initial=cnb[:, 0:1],
                                                        op0=ALU.mult, op1=ALU.add),
                  reads=['g_ones', 'g_SP', 'g_cnb'], writes=['g_NB'])
            Pg.op('dve', lambda e: e.tensor_tensor(out=AL[:], in0=LI[:], in1=NB[:], op=ALU.add),
                  reads=['g_LI', 'g_NB'], writes=['g_AL'])
            Pg.op('dve', lambda e: e.tensor_tensor_scan(out=RR[:], data0=ones[:], data1=AL[:], initial=cr[:, 0:1],
                                                        op0=ALU.mult, op1=ALU.max),
                  reads=['g_ones', 'g_AL', 'g_cr'], writes=['g_RR'])
            Pg.op('dve', lambda e: e.tensor_tensor(out=EN[:], in0=RR[:], in1=NB[:], op=ALU.subtract),
                  reads=['g_RR', 'g_NB'], writes=['g_EN'])
            Pg.op('act', lambda e: e.activation(out=EN[:], in_=EN[:], func=AF.Exp, scale=-1.0), reads=['g_EN'], writes=['g_EN'])
            Pg.op('dve', lambda e: e.tensor_copy(out=RP[:, 0:1], in_=cr[:]), reads=['g_cr'], writes=['g_RP'])
            Pg.op('dve', lambda e: e.tensor_copy(out=RP[:, 1:33], in_=RR[:].rearrange('p (n t) -> p n t', t=128)[:, :, 127]),
                  reads=['g_RR', 'g_RP'], writes=['g_RP'])
            Pg.op('dve', lambda e: e.tensor_tensor(out=AA[:], in0=RP[:, 0:32], in1=RP[:, 1:33], op=ALU.subtract),
                  reads=['g_RP'], writes=['g_AA'])
            Pg.op('act', lambda e: e.activation(out=AA[:], in_=AA[:], func=AF.Exp), reads=['g_AA'], writes=['g_AA'])
            rb = RP[:, 1:33].unsqueeze(2).to_broadcast([8, 32, 128])
            Pg.op('dve', lambda e: e.tensor_tensor(out=KW[:].rearrange('p (n t) -> p n t', t=128),
                                                   in0=AL[:].rearrange('p (n t) -> p n t', t=128), in1=rb, op=ALU.subtract),
                  reads=['g_AL', 'g_RP'], writes=['g_KW'])
            Pg.op('act', lambda e: e.activation(out=KW[:], in_=KW[:], func=AF.Exp), reads=['g_KW'], writes=['g_KW'])
            Pg.op('dve', lambda e: e.tensor_tensor(out=CC[:].rearrange('p (n t) -> p n t', t=128), in0=rb,
                                                   in1=RR[:].rearrange('p (n t) -> p n t', t=128), op=ALU.subtract),
                  reads=['g_RR', 'g_RP'], writes=['g_CC'])
            Pg.op('act', lambda e: e.activation(out=CC[:], in_=CC[:], func=AF.Exp), reads=['g_CC'], writes=['g_CC'])
            Pg.op('dve', lambda e: e.tensor_copy(out=cr[:], in_=RR[:, P - 1:P]), reads=['g_RR', 'g_RP'], writes=['g_cr'])
            Pg.op('dve', lambda e: e.tensor_copy(out=cnb[:], in_=NB[:, P - 1:P]), reads=['g_NB'], writes=['g_cnb'])
            for m in range(32):
                tb = m % 2
                cs = slice(m * 128, (m + 1) * 128)
                for q, QT in enumerate((KW, CC, EN)):
                    Pg.op('pe', lambda e: e.matmul(t_ps[tb][:, q * 8:(q + 1) * 8], lhsT=QT[:, cs], rhs=identf[0:8, 0:8],
                                                   start=True, stop=True),
                          reads=['g_KW', 'g_CC', 'g_EN', 'g_identf'], writes=['g_tps%d' % tb])
                Pg.op('act', lambda e: e.copy(out=T1s[tb][:], in_=t_ps[tb][:]), reads=['g_tps%d' % tb], writes=['g_T1s%d' % tb])
                Pg.op('pe', lambda e: e.matmul(t2_ps[:], lhsT=jrev[:], rhs=T1s[tb][:], start=True, stop=True),
                      reads=['g_T1s%d' % tb, 'g_jrev'], writes=['g_t2ps'])
                Pg.op('dve', lambda e: e.tensor_copy(out=GTF[:, m, :].rearrange('p (q r) -> p q r', q=3),
                                                     in_=T1s[tb][:].rearrange('p (q r) -> p q r', q=3)[:, :, 0:4]),
                      reads=['g_T1s%d' % tb], writes=['g_GTF'])
                Pg.op('dve', lambda e: e.tensor_copy(out=GTB[:, 31 - m, :].rearrange('p (q r) -> p q r', q=3),
                                                     in_=t2_ps[:].rearrange('p (q r) -> p q r', q=3)[:, :, 4:8]),
                      reads=['g_t2ps'], writes=['g_GTB'])
            for r in range(8):
                Pg.op('pe', lambda e: e.matmul(bc_ps[:, r * 32:(r + 1) * 32], lhsT=sel[:, r, :], rhs=AA[:], start=True, stop=True),
                      reads=['g_sel', 'g_AA'], writes=['g_bcps'])
            Pg.op('act', lambda e: e.copy(out=ABCs[:].rearrange('p r n -> p (r n)'), in_=bc_ps[:]), reads=['g_bcps'], writes=['g_ABCs'])
            Pg.dma('sp', seq.GTF[sg * P:(sg + 1) * P, :].rearrange('(n p) c -> p n c', p=128), GTF[:], reads=['g_GTF'])
            Pg.dma('sp', seq.GTB[c0 * 128:(c0 + 32) * 128, :].rearrange('(n p) c -> p n c', p=128), GTB[:], reads=['g_GTB'])
            Pg.dma('sp', seq.ABC[:, :, sg * 32:(sg + 1) * 32], ABCs[:], reads=['g_ABCs'])
    Pg.barrier()


def phase_B(G, layer, seq):
    nc, Pg = G.nc, G.P
    L = seq.L
    NC = L // 128
    NG = L // 512
    with ExitStack() as es:
        cw = sb(nc, es, 'b_cw', [128, 8, 3], F32)
        cb = sb(nc, es, 'b_cb', [128, 8], F32)
        maskf = sb(nc, es, 'b_maskf', [128, 128], F32)
        maskb = sb(nc, es, 'b_maskb', [128, 128], F32)
        abc = sb(nc, es, 'b_abc', [128, 8, 128], F32)
        gbc = sb(nc, es, 'b_gbc', [128, 512], F32)
        win = [sb(nc, es, 'b_win%d' % i, [128, 8, 514], F32) for i in range(2)]
        tmp = [sb(nc, es, 'b_tmp%d' % i, [128, 512], F32) for i in range(2)]
        qk16 = [sb(nc, es, 'b_qk16%d' % i, [128, 8, 512], BF16) for i in range(2)]
        v32 = [sb(nc, es, 'b_v32%d' % i, [128, 512], F32) for i in range(2)]
        vaug = [sb(nc, es, 'b_vaug%d' % i, [128, 4, 129], BF16) for i in range(2)]
        gt = [sb(nc, es, 'b_gt%d' % i, [128, 12], F32) for i in range(2)]
        S32 = sb(nc, es, 'b_S32', [128, 4, 129], F32)
        S16 = sb(nc, es, 'b_S16', [128, 4, 129], BF16)
        kw16 = [sb(nc, es, 'b_kw16%d' % i, [128, 128], BF16) for i in range(2)]
        pT = [sb(nc, es, 'b_pT%d' % i, [128, 128], BF16) for i in range(2)]
        hout = [sb(nc, es, 'b_hout%d' % i, [128, 512], F32) for i in range(2)]
        hf = [sb(nc, es, 'b_hf%d' % i, [128, 512], F32) for i in range(2)]
        o32 = [sb(nc, es, 'b_o32%d' % i, [128, 512], F32) for i in range(2)]
        z32 = [sb(nc, es, 'b_z32%d' % i, [128, 512], F32) for i in range(2)]
        dd = [sb(nc, es, 'b_dd%d' % i, [128, 4], F32) for i in range(2)]
        alloc_hn(G, es, 'bn_')
        kt_ps = [ps(nc, es, 'b_ktps%d' % i, [128, 128], BF16) for i in range(2)]
        sT = [ps(nc, es, 'b_sT%d' % i, [128, 128]) for i in range(2)]
        o_ps = [ps(nc, es, 'b_ops%d' % i, [128, 129]) for i in range(2)]
        kv_ps = [ps(nc, es, 'b_kvps%d' % i, [128, 129]) for i in range(2)]
        for k3 in range(3):
            Pg.dma('sp', cw[:, :, k3], G.w['ml_conv_w'][layer, k3].rearrange('(g p) -> p g', p=128), writes=['b_cw'],
                   allow_slow_non_contiguous=True)
        Pg.dma('sp', cb[:], G.w['ml_conv_b'][layer].rearrange('(g p) -> p g', p=128), writes=['b_cb'],
               allow_slow_non_contiguous=True)
        Pg.dma('sp', maskf[:], G.c['maskf'][:, :], writes=['b_maskf'])
        Pg.dma('sp', maskb[:], G.c['maskb'][:, :], writes=['b_maskb'])
        Pg.dma('sp', abc[:, :, 0:NC], seq.ABC[:, :, 0:NC], writes=['b_abc'])
        Pg.dma('sp', gbc[:], bcast_rows(G.w['ml_norm_g'][layer], 128), writes=['b_gbc'])
        for i in range(2):
            Pg.op('pool', lambda e: e.memset(vaug[i][:], 1.0), writes=['b_vaug%d' % i])
        it = 0
        for sweep in range(2):
            mask = maskf if sweep == 0 else maskb
            Pg.op('pool', lambda e: e.memset(S32[:], 0.0), reads=['b_S16'], writes=['b_S32'])
            gorder = range(NG) if sweep == 0 else range(NG - 1, -1, -1)
            for gi, g in enumerate(gorder):
                wb = gi % 2
                t0 = g * 512
                lo = max(t0 - 1, 0)
                hi = min(t0 + 513, L)
                wlo = lo - (t0 - 1)
                whi = wlo + (hi - lo)
                if wlo > 0:
                    Pg.op('pool', lambda e: e.memset(win[wb][:, :, 0:1], 0.0), writes=['b_win%d' % wb])
                if whi < 514:
                    Pg.op('pool', lambda e: e.memset(win[wb][:, :, 513:514], 0.0), writes=['b_win%d' % wb])
                for q8 in range(8):
                    Pg.dma('sp' if q8 % 2 == 0 else 'act', win[wb][:, q8, wlo:whi], seq.QKT[q8 * 128:(q8 + 1) * 128, lo:hi],
                           writes=[('b_win%d' % wb, q8)], reads=['b_win%d' % wb])
                for q8 in range(8):
                    ce = 'dve'
                    tb = q8 % 2
                    Pg.op(ce, lambda e: e.tensor_scalar(out=tmp[tb][:], in0=win[wb][:, q8, 0:512], scalar1=cw[:, q8, 0:1],
                                                        scalar2=None, op0=ALU.mult),
                          reads=[('b_win%d' % wb, q8), 'b_win%d' % wb, 'b_cw'], writes=['b_tmp%d' % tb])
                    for kk in (1, 2):
                        Pg.op(ce, lambda e: e.scalar_tensor_tensor(out=tmp[tb][:], in0=win[wb][:, q8, kk:kk + 512],
                                                                   scalar=cw[:, q8, kk:kk + 1], in1=tmp[tb][:],
                                                                   op0=ALU.mult, op1=ALU.add),
                              reads=[('b_win%d' % wb, q8), 'b_tmp%d' % tb], writes=['b_tmp%d' % tb])
                    Pg.op('act', lambda e: e.activation(out=qk16[wb][:, q8, :], in_=tmp[tb][:], func=AF.Silu,
                                                        bias=cb[:, q8:q8 + 1], scale=1.0),
                          reads=['b_tmp%d' % tb, 'b_cb'], writes=[('b_qk16%d' % wb, q8)])
                corder = range(4) if sweep == 0 else range(3, -1, -1)
                for cc_ in corder:
                    n = g * 4 + cc_
                    M = n if sweep == 0 else NC - 1 - n
                    cb_ = it % 2
                    it += 1
                    rows = slice(n * 128, (n + 1) * 128)
                    csl = slice(cc_ * 128, (cc_ + 1) * 128)
                    Pg.dma('sp', v32[cb_][:], seq.U[rows, C_MLV:C_MLV + 512], writes=['b_v32%d' % cb_])
                    Pg.dma('sp', gt[cb_][:], (seq.GTF if sweep == 0 else seq.GTB)[rows, :], writes=['b_gt%d' % cb_])
                    if sweep == 1:
                        Pg.dma('act', hf[cb_][:], seq.HF[rows, :], writes=['b_hf%d' % cb_])
                        Pg.dma('act', o32[cb_][:], seq.U[rows, C_MLO:C_MLO + 512], writes=['b_o32%d' % cb_])
                        Pg.dma('act', z32[cb_][:], seq.U[rows, C_MLZ:C_MLZ + 512], writes=['b_z32%d' % cb_])
                    Pg.op('pool', lambda e: e.tensor_copy(out=vaug[cb_][:, :, 0:128],
                                                          in_=v32[cb_][:].rearrange('p (a d) -> p a d', a=4)),
                          reads=['b_v32%d' % cb_], writes=['b_vaug%d' % cb_])
                    for h in range(4):
                        hb = h % 2
                        r = h + 4 * sweep
                        kT = qk16[wb][:, 4 + h, csl]
                        qT = qk16[wb][:, h, csl]
                        kwt = gt[cb_][:, h:h + 1]
                        ccol = gt[cb_][:, 4 + h:5 + h]
                        encol = gt[cb_][:, 8 + h:9 + h]
                        Pg.op('pe', lambda e: e.transpose(kt_ps[hb][:], kT, G.identb[:]),
                              reads=[('b_qk16%d' % wb, 4 + h)], writes=['b_ktps%d' % hb])
                        Pg.op('act', lambda e: e.mul(out=kw16[hb][:], in_=kt_ps[hb][:], mul=kwt),
                              reads=['b_ktps%d' % hb, 'b_gt%d' % cb_], writes=['b_kw16%d' % hb])
                        Pg.op('pe', lambda e: e.matmul(sT[hb][:], lhsT=kT, rhs=qT, start=True, stop=True),
                              reads=[('b_qk16%d' % wb, 4 + h), ('b_qk16%d' % wb, h)], writes=['b_sT%d' % hb])
                        Pg.op('dve', lambda e: e.scalar_tensor_tensor(out=pT[hb][:], in0=sT[hb][:], scalar=kwt, in1=mask[:],
                                                                      op0=ALU.mult, op1=ALU.mult),
                              reads=['b_sT%d' % hb, 'b_gt%d' % cb_, 'b_maskf', 'b_maskb'], writes=['b_pT%d' % hb])
                        Pg.op('dve', lambda e: e.tensor_scalar(out=S32[:, h, :], in0=S32[:, h, :], scalar1=abc[:, r, M:M + 1],
                                                               scalar2=None, op0=ALU.mult),
                              reads=[('b_S32', h), 'b_S32', 'b_abc'], writes=[('b_S32', h)])
                        Pg.op('act', lambda e: e.mul(out=S16[:, h, :], in_=S32[:, h, :], mul=S_ML),
                              reads=[('b_S32', h)], writes=[('b_S16', h)])
                        Pg.op('pe', lambda e: e.matmul(o_ps[hb][:], lhsT=pT[hb][:], rhs=vaug[cb_][:, h, :], start=True, stop=False),
                              reads=['b_pT%d' % hb, 'b_vaug%d' % cb_], writes=['b_ops%d' % hb])
                        Pg.op('pe', lambda e: e.matmul(o_ps[hb][:], lhsT=qT, rhs=S16[:, h, :], start=False, stop=True),
                              reads=[('b_qk16%d' % wb, h), ('b_S16', h)], writes=['b_ops%d' % hb])
                        Pg.op('pe', lambda e: e.matmul(kv_ps[hb][:], lhsT=kw16[hb][:], rhs=vaug[cb_][:, h, :], start=True, stop=True),
                              reads=['b_kw16%d' % hb, 'b_vaug%d' % cb_], writes=['b_kvps%d' % hb])
                        Pg.op('dve', lambda e: e.tensor_tensor(out=S32[:, h, :], in0=S32[:, h, :], in1=kv_ps[hb][:], op=ALU.add),
                              reads=[('b_S32', h), 'b_kvps%d' % hb], writes=[('b_S32', h)])
                        d = dd[hb]
                        dk = 'b_dd%d' % hb
                        Pg.op('act', lambda e: e.activation(out=d[:, 0:1], in_=o_ps[hb][:, 128:129], func=AF.Abs, scale=ccol),
                              reads=['b_ops%d' % hb, 'b_gt%d' % cb_], writes=[dk])
                        Pg.op('dve', lambda e: e.tensor_tensor(out=d[:, 1:2], in0=d[:, 0:1], in1=encol, op=ALU.max),
                              reads=[dk, 'b_gt%d' % cb_], writes=[dk])
                        Pg.op('dve', lambda e: e.reciprocal(out=d[:, 2:3], in_=d[:, 1:2]), reads=[dk], writes=[dk])
                        Pg.op('dve', lambda e: e.tensor_tensor(out=d[:, 3:4], in0=d[:, 2:3], in1=ccol, op=ALU.mult),
                              reads=[dk, 'b_gt%d' % cb_], writes=[dk])
                        Pg.op('act', lambda e: e.mul(out=hout[cb_][:, h * 128:(h + 1) * 128], in_=o_ps[hb][:, 0:128], mul=d[:, 3:4]),
                              reads=['b_ops%d' % hb, dk], writes=[('b_hout%d' % cb_, h)])
                    hk = [('b_hout%d' % cb_, h) for h in range(4)]
                    if sweep == 0:
                        Pg.dma('pool', seq.HF[rows, :], hout[cb_][:], reads=hk)
                    else:
                        Pg.op('dve', lambda e: e.tensor_tensor(out=hout[cb_][:], in0=hout[cb_][:], in1=hf[cb_][:], op=ALU.add),
                              reads=hk + ['b_hf%d' % cb_], writes=['b_hsum%d' % cb_])
                        Pg.op('act', lambda e: e.activation(out=o32[cb_][:], in_=o32[cb_][:], func=AF.Sigmoid),
                              reads=['b_o32%d' % cb_], writes=['b_o32%d' % cb_])
                        Pg.op('dve', lambda e: e.tensor_tensor(out=hout[cb_][:], in0=hout[cb_][:], in1=o32[cb_][:], op=ALU.mult),
                              reads=['b_hsum%d' % cb_, 'b_o32%d' % cb_], writes=['b_hsum%d' % cb_])
                        headnorm_gate(G, 'bn_', hout[cb_], ['b_hsum%d' % cb_], z32[cb_][:], 'b_z32%d' % cb_, gbc, 'b_gbc',
                                      seq.MIX[rows, 0:512], 'pool')
            Pg.barrier()
    Pg.barrier()


def phase_R(G, layer, seq):
    nc, Pg = G.nc, G.P
    L = seq.L
    NC = L // 128
    g128f = [math.exp(128.0 * x) for x in RT_LGF]
    g128b = [math.exp(128.0 * x) for x in RT_LGB]
    with ExitStack() as es:
        dct = sb(nc, es, 'r_dct', [128, 4, 128], F32)
        qwf = sb(nc, es, 'r_qwf', [128, 4, 128], F32)
        qwb = sb(nc, es, 'r_qwb', [128, 4, 128], F32)
        kwf = sb(nc, es, 'r_kwf', [128, 4], F32)
        kwb = sb(nc, es, 'r_kwb', [128, 4], F32)
        gbc = sb(nc, es, 'r_gbc', [128, 512], F32)
        qkv = [sb(nc, es, 'r_qkv%d' % i, [128, 1536], F32) for i in range(2)]
        rot = [sb(nc, es, 'r_rot%d' % i, [128, 128], F32) for i in range(2)]
        ta = [sb(nc, es, 'r_ta%d' % i, [128, 256], F32) for i in range(2)]
        tb = [sb(nc, es, 'r_tb%d' % i, [128, 256], F32) for i in range(2)]
        kr32 = [sb(nc, es, 'r_kr32%d' % i, [128, 512], F32) for i in range(2)]
        q16 = [sb(nc, es, 'r_q16%d' % i, [128, 512], BF16) for i in range(2)]
        k16 = [sb(nc, es, 'r_k16%d' % i, [128, 512], BF16) for i in range(2)]
        v16 = [sb(nc, es, 'r_v16%d' % i, [128, 512], BF16) for i in range(2)]
        qT = [sb(nc, es, 'r_qT%d' % i, [128, 4, 128], BF16) for i in range(2)]
        kT = [sb(nc, es, 'r_kT%d' % i, [128, 4, 128], BF16) for i in range(2)]
        pT = [sb(nc, es, 'r_pT%d' % i, [128, 128], BF16) for i in range(2)]
        qw16 = [sb(nc, es, 'r_qw16%d' % i, [128, 128], BF16) for i in range(2)]
        kw16 = [sb(nc, es, 'r_kw16%d' % i, [128, 128], BF16) for i in range(2)]
        R32 = sb(nc, es, 'r_R32', [128, 4, 128], F32)
        R16 = sb(nc, es, 'r_R16', [128, 4, 128], BF16)
        outt = [sb(nc, es, 'r_out%d' % i, [128, 512], F32) for i in range(2)]
        rf = [sb(nc, es, 'r_rf%d' % i, [128, 512], F32) for i in range(2)]
        z32 = [sb(nc, es, 'r_z32%d' % i, [128, 512], F32) for i in range(2)]
        alloc_hn(G, es, 'rn_')
        qt_ps = ps(nc, es, 'r_qtps', [128, 512], BF16)
        kt_ps = ps(nc, es, 'r_ktps', [128, 512], BF16)
        sT = [ps(nc, es, 'r_sT%d' % i, [128, 128]) for i in range(2)]
        o_ps = [ps(nc, es, 'r_ops%d' % i, [128, 512]) for i in range(2)]
        kv_ps = [ps(nc, es, 'r_kvps%d' % i, [128, 128]) for i in range(2)]
        for nm, t_ in (('dct', dct), ('qwf', qwf), ('qwb', qwb)):
            Pg.dma('sp', t_[:], G.c[nm][:, :, :], writes=['r_' + nm])
        Pg.dma('sp', kwf[:], G.c['kwf'][:, :], writes=['r_kwf'])
        Pg.dma('sp', kwb[:], G.c['kwb'][:, :], writes=['r_kwb'])
        Pg.dma('sp', gbc[:], bcast_rows(G.w['rt_norm_g'][layer], 128), writes=['r_gbc'])
        it = 0
        for sweep in range(2):
            Pg.op('pool', lambda e: e.memset(R32[:], 0.0), writes=['r_R32'])
            Pg.op('pool', lambda e: e.memset(R16[:], 0.0), reads=['r_R16'], writes=[('r_R16', h) for h in range(4)] + ['r_R16'])
            order = range(NC) if sweep == 0 else range(NC - 1, -1, -1)
            for n in order:
                b = it % 2
                it += 1
                rows = slice(n * 128, (n + 1) * 128)
                Pg.dma('sp', qkv[b][:], seq.U[rows, C_RQ:C_RQ + 1536], writes=['r_qkv%d' % b])
                Pg.dma('act', rot[b][:], G.c['rot'][rows, :], writes=['r_rot%d' % b])
                if sweep == 1:
                    Pg.dma('act', rf[b][:], seq.HF[rows, :], writes=['r_rf%d' % b])
                    Pg.dma('act', z32[b][:], seq.U[rows, C_RZ:C_RZ + 512], writes=['r_z32%d' % b])
                cosb = rot[b][:, 0:64].unsqueeze(1).to_broadcast([128, 4, 64])
                sinb = rot[b][:, 64:128].unsqueeze(1).to_broadcast([128, 4, 64])
                for qi, (eng, src0, dst, dkey) in enumerate((('dve', 0, q16[b], 'r_q16%d' % b), ('pool', 512, kr32[b], 'r_kr32%d' % b))):
                    xv = qkv[b][:, src0:src0 + 512].rearrange('p (a t d) -> p a t d', a=4, t=2)
                    dv = dst[:].rearrange('p (a t d) -> p a t d', a=4, t=2)
                    A_ = ta[qi][:].rearrange('p (a d) -> p a d', a=4)
                    B_ = tb[qi][:].rearrange('p (a d) -> p a d', a=4)
                    ka, kb = 'r_ta%d' % qi, 'r_tb%d' % qi
                    rk = ['r_qkv%d' % b, 'r_rot%d' % b]
                    Pg.op(eng, lambda e: e.tensor_tensor(out=A_, in0=xv[:, :, 0, :], in1=cosb, op=ALU.mult), reads=rk, writes=[ka])
                    Pg.op(eng, lambda e: e.tensor_tensor(out=B_, in0=xv[:, :, 1, :], in1=sinb, op=ALU.mult), reads=rk, writes=[kb])
                    Pg.op(eng, lambda e: e.tensor_tensor(out=dv[:, :, 0, :], in0=A_, in1=B_, op=ALU.subtract),
                          reads=[ka, kb], writes=[dkey])
                    Pg.op(eng, lambda e: e.tensor_tensor(out=A_, in0=xv[:, :, 0, :], in1=sinb, op=ALU.mult), reads=rk + [dkey], writes=[ka])
                    Pg.op(eng, lambda e: e.tensor_tensor(out=B_, in0=xv[:, :, 1, :], in1=cosb, op=ALU.mult), reads=rk + [dkey], writes=[kb])
                    Pg.op(eng, lambda e: e.tensor_tensor(out=dv[:, :, 1, :], in0=A_, in1=B_, op=ALU.add),
                          reads=[ka, kb], writes=[dkey])
                Pg.op('act', lambda e: e.copy(out=k16[b][:], in_=kr32[b][:]), reads=['r_kr32%d' % b], writes=['r_k16%d' % b])
                Pg.op('act', lambda e: e.copy(out=v16[b][:], in_=qkv[b][:, 1024:1536]), reads=['r_qkv%d' % b], writes=['r_v16%d' % b])
                for h in range(4):
                    Pg.op('pe', lambda e: e.transpose(qt_ps[:, h * 128:(h + 1) * 128], q16[b][:, h * 128:(h + 1) * 128], G.identb[:]),
                          reads=['r_q16%d' % b], writes=['r_qtps'])
                    Pg.op('pe', lambda e: e.transpose(kt_ps[:, h * 128:(h + 1) * 128], k16[b][:, h * 128:(h + 1) * 128], G.identb[:]),
                          reads=['r_k16%d' % b], writes=['r_ktps'])
                Pg.op('act', lambda e: e.copy(out=qT[b][:].rearrange('p a t -> p (a t)'), in_=qt_ps[:]), reads=['r_qtps'], writes=['r_qT%d' % b])
                Pg.op('dve', lambda e: e.tensor_copy(out=kT[b][:].rearrange('p a t -> p (a t)'), in_=kt_ps[:]), reads=['r_ktps'], writes=['r_kT%d' % b])
                ob = o_ps[b]
                for h in range(4):
                    hb = h % 2
                    hs = slice(h * 128, (h + 1) * 128)
                    qw = qwf if sweep == 0 else qwb
                    kwc = (kwf if sweep == 0 else kwb)[:, h:h + 1]
                    g128 = (g128f if sweep == 0 else g128b)[h]
                    Pg.op('pool', lambda e: e.tensor_tensor(out=qw16[hb][:], in0=qT[b][:, h, :], in1=qw[:, h, :], op=ALU.mult),
                          reads=['r_qT%d' % b, 'r_qwf', 'r_qwb'], writes=['r_qw16%d' % hb])
                    if sweep == 0:
                        Pg.op('pe', lambda e: e.matmul(sT[hb][:], lhsT=kT[b][:, h, :], rhs=qT[b][:, h, :], start=True, stop=True),
                              reads=['r_kT%d' % b, 'r_qT%d' % b], writes=['r_sT%d' % hb])
                        Pg.op('dve', lambda e: e.tensor_tensor(out=pT[hb][:], in0=sT[hb][:], in1=dct[:, h, :], op=ALU.mult),
                              reads=['r_sT%d' % hb, 'r_dct'], writes=['r_pT%d' % hb])
                        Pg.op('pe', lambda e: e.matmul(ob[:, hs], lhsT=pT[hb][:], rhs=v16[b][:, hs], start=True, stop=False),
                              reads=['r_pT%d' % hb, 'r_v16%d' % b], writes=['r_ops%d' % b])
                    Pg.op('pe', lambda e: e.matmul(ob[:, hs], lhsT=qw16[hb][:], rhs=R16[:, h, :], start=(sweep == 1), stop=True),
                          reads=['r_qw16%d' % hb, ('r_R16', h)], writes=['r_ops%d' % b])
                    Pg.op('act', lambda e: e.mul(out=kw16[hb][:], in_=kr32[b][:, hs], mul=kwc),
                          reads=['r_kr32%d' % b, 'r_kwf', 'r_kwb'], writes=['r_kw16%d' % hb])
                    Pg.op('pe', lambda e: e.matmul(kv_ps[hb][:], lhsT=kw16[hb][:], rhs=v16[b][:, hs], start=True, stop=True),
                          reads=['r_kw16%d' % hb, 'r_v16%d' % b], writes=['r_kvps%d' % hb])
                    Pg.op('dve', lambda e: e.scalar_tensor_tensor(out=R32[:, h, :], in0=R32[:, h, :], scalar=g128, in1=kv_ps[hb][:],
                                                                  op0=ALU.mult, op1=ALU.add),
                          reads=['r_R32', ('r_R32', h), 'r_kvps%d' % hb], writes=[('r_R32', h)])
                    Pg.op('act', lambda e: e.copy(out=R16[:, h, :], in_=R32[:, h, :]), reads=[('r_R32', h)], writes=[('r_R16', h)])
                if sweep == 0:
                    Pg.op('act', lambda e: e.copy(out=outt[b][:], in_=ob[:]), reads=['r_ops%d' % b], writes=['r_out%d' % b])
                    Pg.dma('pool', seq.HF[rows, :], outt[b][:], reads=['r_out%d' % b])
                else:
                    Pg.op('dve', lambda e: e.tensor_tensor(out=outt[b][:], in0=ob[:], in1=rf[b][:], op=ALU.add),
                          reads=['r_ops%d' % b, 'r_rf%d' % b], writes=['r_out%d' % b])
                    headnorm_gate(G, 'rn_', outt[b], ['r_out%d' % b], z32[b][:], 'r_z32%d' % b, gbc, 'r_gbc',
                                  seq.MIX[rows, 512:1024], 'pool')
            Pg.barrier()
    Pg.barrier()


def conv3_tile(G, pre, seq, col0, stream, t, cwb, cbb, um, u0, up, acc, q):
    Pg = G.P
    L = seq.L
    r0 = t * 128
    ks = [pre + 'um', pre + 'u0', pre + 'up']
    if r0 == 0:
        Pg.op('pool', lambda e: e.memset(um[:], 0.0), writes=[ks[0]])
        Pg.dma(q, um[1:128, :], seq.U[0:127, col0:col0 + 512], reads=[ks[0]], writes=[ks[0] + 'd'])
    else:
        Pg.dma(q, um[:], seq.U[r0 - 1:r0 + 127, col0:col0 + 512], writes=[ks[0], ks[0] + 'd'])
    Pg.dma(q, u0[:], seq.U[r0:r0 + 128, col0:col0 + 512], writes=[ks[1]])
    if r0 + 128 == L:
        Pg.op('pool', lambda e: e.memset(up[:], 0.0), writes=[ks[2]])
        Pg.dma(q, up[0:127, :], seq.U[r0 + 1:L, col0:col0 + 512], reads=[ks[2]], writes=[ks[2] + 'd'])
    else:
        Pg.dma(q, up[:], seq.U[r0 + 1:r0 + 129, col0:col0 + 512], writes=[ks[2], ks[2] + 'd'])
    ak = pre + 'acc'
    Pg.op('dve', lambda e: e.tensor_tensor(out=acc[:], in0=um[:], in1=cwb[:, stream, 0, :], op=ALU.mult),
          reads=[ks[0], ks[0] + 'd', 'h_cwb'], writes=[ak])
    Pg.op('pool', lambda e: e.tensor_tensor(out=u0[:], in0=u0[:], in1=cwb[:, stream, 1, :], op=ALU.mult),
          reads=[ks[1], 'h_cwb'], writes=[ks[1]])
    Pg.op('pool', lambda e: e.tensor_tensor(out=up[:], in0=up[:], in1=cwb[:, stream, 2, :], op=ALU.mult),
          reads=[ks[2], ks[2] + 'd', 'h_cwb'], writes=[ks[2]])
    Pg.op('dve', lambda e: e.tensor_tensor(out=acc[:], in0=acc[:], in1=u0[:], op=ALU.add), reads=[ak, ks[1]], writes=[ak])
    Pg.op('dve', lambda e: e.tensor_tensor(out=acc[:], in0=acc[:], in1=up[:], op=ALU.add), reads=[ak, ks[2]], writes=[ak])
    Pg.op('dve', lambda e: e.tensor_tensor(out=acc[:], in0=acc[:], in1=cbb[:, stream, :], op=ALU.add), reads=[ak, 'h_cbb'], writes=[ak])


def phase_H(G, layer, seq):
    nc, Pg = G.nc, G.P
    L, nb = seq.L, seq.nb
    npc = 2 * nb - 1
    feat = G.c['feat_' + seq.name]
    ntn = G.c['ntn_' + seq.name]
    HPI = math.pi / 2
    with ExitStack() as es0:
        cwb = sb(nc, es0, 'h_cwb', [128, 3, 3, 512], F32)
        cbb = sb(nc, es0, 'h_cbb', [128, 3, 512], F32)
        skb = sb(nc, es0, 'h_skb', [128, 2, 512], F32)
        RNt = sb(nc, es0, 'h_RNt', [128, 512], F32)
        for st in range(3):
            for k3 in range(3):
                Pg.dma('sp', cwb[:, st, k3, :], bcast_rows(G.w['hy_conv_w'][layer, k3, st * 512:(st + 1) * 512], 128), writes=['h_cwb'])
            Pg.dma('sp', cbb[:, st, :], bcast_rows(G.w['hy_conv_b'][layer, st * 512:(st + 1) * 512], 128), writes=['h_cbb'])
        for o in range(2):
            Pg.dma('sp', skb[:, o, :], bcast_rows(G.w['hy_skip'][layer, o], 128), writes=['h_skb'])
        Pg.barrier()
        for o in range(2):
            with ExitStack() as es:
                w1 = sb(nc, es, 's_w1', [33, 64], F32)
                w2 = sb(nc, es, 's_w2', [64, 64], F32)
                w3 = sb(nc, es, 's_w3', [64, 2, 512], F32)
                fr = sb(nc, es, 's_fr', [64, 1], F32)
                fb1 = sb(nc, es, 's_fb1', [64, 1], F32)
                fb2 = sb(nc, es, 's_fb2', [64, 1], F32)
                absd = sb(nc, es, 's_absd', [128, 2, 512], F32)
                ones = sb(nc, es, 's_ones', [128, 128], F32)
                ftT = [sb(nc, es, 's_ftT%d' % i, [33, 512], F32) for i in range(2)]
                ntt = sb(nc, es, 's_ntt', [128, 32], F32)
                zt = sb(nc, es, 's_zt', [64, 512], F32)
                ct = sb(nc, es, 's_ct', [64, 512], F32)
                hid1 = sb(nc, es, 's_hid1', [64, 512], F32)
                hid2 = sb(nc, es, 's_hid2', [64, 512], F32)
                Et = [sb(nc, es, 's_E%d' % i, [128, 512], F32) for i in range(2)]
                g32 = [sb(nc, es, 's_g32%d' % i, [128, 512], F32) for i in range(2)]
                gab = [sb(nc, es, 's_gab%d' % i, [128, 512], F32) for i in range(2)]
                GT16 = sb(nc, es, 's_GT16', [128, 64, 512], BF16)
                FT = [sb(nc, es, 's_FT%d' % i, [128, 64, 2, 128], BF16) for i in range(2)]
                go16 = [sb(nc, es, 's_go16%d' % i, [128, 2, 512], BF16) for i in range(2)]
                m_ps = ps(nc, es, 's_mps', [64, 512])
                f_ps = [ps(nc, es, 's_fps%d' % i, [128, 512]) for i in range(2)]
                b0_ps = ps(nc, es, 's_b0ps', [1, 512])
                n_ps = ps(nc, es, 's_nps', [128, 512])
                x_ps = [ps(nc, es, 's_xps%d' % i, [128, 512]) for i in range(2)]
                Pg.dma('sp', w1[:], G.w['hy_w1'][layer], writes=['s_w1'])
                Pg.dma('sp', w2[:], G.w['hy_w2'][layer], writes=['s_w2'])
                Pg.dma('sp', w3[:], G.w['hy_w3'][layer, :, o * 1024:(o + 1) * 1024].rearrange('k (d c) -> k d c', d=2), writes=['s_w3'])
                Pg.dma('sp', fr[:], G.w['hy_freq'][layer].unsqueeze(1), writes=['s_fr'])
                Pg.dma('sp', fb1[:], G.w['hy_b1'][layer].unsqueeze(1), writes=['s_fb1'])
                Pg.dma('sp', fb2[:], G.w['hy_b2'][layer].unsqueeze(1), writes=['s_fb2'])
                for dr in range(2):
                    Pg.dma('sp', absd[:, dr, :], bcast_rows(G.w['hy_deltas'][layer, o, dr], 128), writes=['s_absd'])
                Pg.op('act', lambda e: e.activation(out=absd[:].rearrange('p a c -> p (a c)'), in_=absd[:].rearrange('p a c -> p (a c)'),
                                                    func=AF.Abs), reads=['s_absd'], writes=['s_absd'])
                Pg.op('dve', lambda e: e.tensor_tensor(out=fb1[:], in0=fb1[:], in1=fr[:], op=ALU.mult), reads=['s_fb1', 's_fr'], writes=['s_fb1'])
                Pg.op('dve', lambda e: e.tensor_tensor(out=fb2[:], in0=fb2[:], in1=fr[:], op=ALU.mult), reads=['s_fb2', 's_fr'], writes=['s_fb2'])
                Pg.op('pool', lambda e: e.memset(ones[:], 1.0), writes=['s_ones'])
                norm_tiles = [(pi, hf) for pi in range(npc) for hf in range(2) if hf == 0 or pi == 0]
                nleft = len(norm_tiles) * 32
                ncount = 0
                ti = 0
                xi_ = 0
                for pi in range(npc):
                    d = pi - (nb - 1)
                    for hf in range(2):
                        dr = 0 if ((hf == 0 and d >= 0) or (hf == 1 and d >= 1)) else 1
                        Pg.dma('act', ntt[:], ntn[pi, hf], writes=['s_ntt'])
                        for grp in range(8):
                            fb_ = grp % 2
                            Pg.dma('act', ftT[fb_][:], feat[pi, hf, :, grp * 512:(grp + 1) * 512], writes=['s_ftT%d' % fb_])
                            src_keys = ['s_ftT%d' % fb_]
                            rhs_ = ftT[fb_]
                            for (wm, fbm, hid, wk, hk) in ((w1, fb1, hid1, 's_w1', 's_hid1'), (w2, fb2, hid2, 's_w2', 's_hid2')):
                                Pg.op('pe', lambda e: e.matmul(m_ps[:], lhsT=wm[:], rhs=rhs_[:], start=True, stop=True),
                                      reads=src_keys + [wk], writes=['s_mps'])
                                Pg.op('dve', lambda e: e.tensor_scalar(out=zt[:], in0=m_ps[:], scalar1=fr[:, 0:1], scalar2=fbm[:, 0:1],
                                                                       op0=ALU.mult, op1=ALU.add),
                                      reads=['s_mps', 's_fr', 's_fb1', 's_fb2'], writes=['s_zt'])
                                for _ in range(2):
                                    Pg.op('dve', lambda e: e.tensor_scalar(out=ct[:], in0=zt[:], scalar1=-HPI, scalar2=HPI,
                                                                           op0=ALU.max, op1=ALU.min), reads=['s_zt'], writes=['s_ct'])
                                    Pg.op('dve', lambda e: e.scalar_tensor_tensor(out=zt[:], in0=ct[:], scalar=2.0, in1=zt[:],
                                                                                  op0=ALU.mult, op1=ALU.subtract),
                                          reads=['s_ct', 's_zt'], writes=['s_zt'])
                                Pg.op('act', lambda e: e.activation(out=hid[:], in_=zt[:], func=AF.Sin), reads=['s_zt'], writes=[hk])
                                src_keys = [hk]
                                rhs_ = hid
                            for tt in range(4):
                                tile_i = grp * 4 + tt
                                b2 = ti % 2
                                ti += 1
                                Pg.op('pe', lambda e: e.matmul(f_ps[b2][:], lhsT=hid2[:, tt * 128:(tt + 1) * 128], rhs=w3[:, dr, :],
                                                               start=True, stop=True), reads=['s_hid2', 's_w3'], writes=['s_fps%d' % b2])
                                Pg.op('act', lambda e: e.activation(out=Et[b2][:], in_=absd[:, dr, :], func=AF.Exp,
                                                                    scale=ntt[:, tile_i:tile_i + 1]),
                                      reads=['s_absd', 's_ntt'], writes=['s_E%d' % b2])
                                Pg.op('dve', lambda e: e.tensor_tensor(out=g32[b2][:], in0=f_ps[b2][:], in1=Et[b2][:], op=ALU.mult),
                                      reads=['s_fps%d' % b2, 's_E%d' % b2], writes=['s_g32%d' % b2])
                                if d == 0 and hf == 0 and tile_i == 0:
                                    Pg.op('pe', lambda e: e.matmul(b0_ps[:], lhsT=hid2[:, 0:1], rhs=w3[:, 1, :], start=True, stop=True),
                                          reads=['s_hid2', 's_w3'], writes=['s_b0ps'])
                                    Pg.op('dve', lambda e: e.tensor_tensor(out=g32[b2][0:1, :], in0=g32[b2][0:1, :], in1=b0_ps[:], op=ALU.add),
                                          reads=['s_g32%d' % b2, 's_b0ps'], writes=['s_g32%d' % b2])
                                if hf == 1 and tile_i == 0:
                                    Pg.op('dve', lambda e: e.memset(g32[b2][0:1, :], 0.0), reads=['s_g32%d' % b2], writes=['s_g32%d' % b2])
                                if (pi, hf) in norm_tiles:
                                    Pg.op('act', lambda e: e.activation(out=gab[b2][:], in_=g32[b2][:], func=AF.Abs),
                                          reads=['s_g32%d' % b2], writes=['s_gab%d' % b2])
                                    Pg.op('pe', lambda e: e.matmul(n_ps[:], lhsT=ones[:], rhs=gab[b2][:], start=(ncount == 0),
                                                                   stop=(ncount == nleft - 1)), reads=['s_gab%d' % b2, 's_ones'], writes=['s_nps'])
                                    ncount += 1
                                Pg.op('pool', lambda e: e.tensor_copy(out=GT16[:, hf * 32 + tile_i, :], in_=g32[b2][:]),
                                      reads=['s_g32%d' % b2], writes=[('s_GT16', hf * 32 + tile_i)])
                    gkeys = [('s_GT16', i) for i in range(64)]
                    for kc in range(NKC):
                        fb_ = xi_ % 2
                        xi_ += 1
                        Pg.dma('sp' if kc % 2 == 0 else 'pool', FT[fb_][:], G.c['fwd'][kc], writes=['s_FT%d' % fb_])
                        for ri in range(2):
                            for sc in range(64):
                                Pg.op('pe', lambda e: e.matmul(x_ps[ri][:], lhsT=FT[fb_][:, sc, ri, :], rhs=GT16[:, sc, :],
                                                               start=(sc == 0), stop=(sc == 63)),
                                      reads=['s_FT%d' % fb_] + (gkeys if sc == 0 else []), writes=['s_xps%d' % ri])
                        Pg.op('act', lambda e: e.copy(out=go16[fb_][:, 0, :], in_=x_ps[0][:]), reads=['s_xps0'], writes=[('s_go%d' % fb_, 0)])
                        Pg.op('dve', lambda e: e.tensor_copy(out=go16[fb_][:, 1, :], in_=x_ps[1][:]), reads=['s_xps1'], writes=[('s_go%d' % fb_, 1)])
                        Pg.dma('act', seq.GS[pi, kc], go16[fb_][:], reads=[('s_go%d' % fb_, 0), ('s_go%d' % fb_, 1)])
                Pg.op('dve', lambda e: e.reciprocal(out=RNt[:], in_=n_ps[:]), reads=['s_nps'], writes=['h_RNt'])
            Pg.barrier()
            with ExitStack() as es:
                um = sb(nc, es, 'f_um', [128, 512], F32)
                u0 = sb(nc, es, 'f_u0', [128, 512], F32)
                up = sb(nc, es, 'f_up', [128, 512], F32)
                acc = sb(nc, es, 'f_acc', [128, 512], F32)
                v16 = sb(nc, es, 'f_v16', [128, 32, 512], BF16)
                FT = [sb(nc, es, 'f_FT%d' % i, [128, 32, 2, 128], BF16) for i in range(2)]
                xo16 = [sb(nc, es, 'f_xo16%d' % i, [128, 2, 512], BF16) for i in range(2)]
                x_ps = [ps(nc, es, 'f_xps%d' % i, [128, 512]) for i in range(2)]
                xi_ = 0
                for b in range(nb):
                    for t in range(32):
                        tg = b * 32 + t
                        if o == 0:
                            conv3_tile(G, 'f_', seq, C_HV, 0, tg, cwb, cbb, um, u0, up, acc, 'sp')
                        else:
                            Pg.dma('sp', acc[:], seq.Z1[tg * 128:(tg + 1) * 128, :], writes=['f_acc'])
                        Pg.op('act', lambda e: e.copy(out=v16[:, t, :], in_=acc[:]), reads=['f_acc'], writes=[('f_v16', t)])
                    vkeys = [('f_v16', t) for t in range(32)]
                    for kc in range(NKC):
                        fb_ = xi_ % 2
                        xi_ += 1
                        Pg.dma('sp' if kc % 2 == 0 else 'pool', FT[fb_][:], G.c['fwd'][kc, :, 0:32], writes=['f_FT%d' % fb_])
                        for ri in range(2):
                            for sc in range(32):
                                Pg.op('pe', lambda e: e.matmul(x_ps[ri][:], lhsT=FT[fb_][:, sc, ri, :], rhs=v16[:, sc, :],
                                                               start=(sc == 0), stop=(sc == 31)),
                                      reads=['f_FT%d' % fb_] + (vkeys if sc == 0 else []), writes=['f_xps%d' % ri])
                        Pg.op('act', lambda e: e.copy(out=xo16[fb_][:, 0, :], in_=x_ps[0][:]), reads=['f_xps0'], writes=[('f_xo%d' % fb_, 0)])
                        Pg.op('dve', lambda e: e.tensor_copy(out=xo16[fb_][:, 1, :], in_=x_ps[1][:]), reads=['f_xps1'], writes=[('f_xo%d' % fb_, 1)])
                        Pg.dma('act', seq.XS[b, kc], xo16[fb_][:], reads=[('f_xo%d' % fb_, 0), ('f_xo%d' % fb_, 1)])
            Pg.barrier()
            with ExitStack() as es:
                Y16 = sb(nc, es, 'i_Y16', [128, NKC, 2, 512], BF16)
                Xt = [sb(nc, es, 'i_X%d' % i, [128, 2, 512], BF16) for i in range(2)]
                Gt = [sb(nc, es, 'i_G%d' % i, [128, 2, 512], BF16) for i in range(2)]
                Yr = sb(nc, es, 'i_Yr', [128, 512], F32)
                Yi = sb(nc, es, 'i_Yi', [128, 512], F32)
                t1 = sb(nc, es, 'i_t1', [128, 512], F32)
                t2 = sb(nc, es, 'i_t2', [128, 512], F32)
                t3 = sb(nc, es, 'i_t3', [128, 512], F32)
                t4 = sb(nc, es, 'i_t4', [128, 512], F32)
                IT = [sb(nc, es, 'i_IT%d' % i, [128, NKC, 2, 128], BF16) for i in range(2)]
                um = sb(nc, es, 'i_um', [128, 512], F32)
                u0 = sb(nc, es, 'i_u0', [128, 512], F32)
                up = sb(nc, es, 'i_up', [128, 512], F32)
                vacc = sb(nc, es, 'i_vacc', [128, 512], F32)
                xacc = sb(nc, es, 'i_xacc', [128, 512], F32)
                yt = sb(nc, es, 'i_yt', [128, 512], F32)
                zt_ = sb(nc, es, 'i_zt', [128, 512], F32)
                hz = sb(nc, es, 'i_hz', [128, 512], F32)
                y_ps = [ps(nc, es, 'i_yps%d' % i, [128, 512]) for i in range(2)]
                pi_ = 0
                for a in range(nb):
                    for kc in range(NKC):
                        for b in range(nb):
                            xb_ = pi_ % 2
                            pi_ += 1
                            Pg.dma('sp', Xt[xb_][:], seq.XS[b, kc], writes=['i_X%d' % xb_])
                            Pg.dma('act', Gt[xb_][:], seq.GS[a - b + nb - 1, kc], writes=['i_G%d' % xb_])
                            rk = ['i_X%d' % xb_, 'i_G%d' % xb_]
                            first = b == 0
                            Pg.op('dve', lambda e: e.tensor_tensor(out=(Yr if first else t1)[:], in0=Xt[xb_][:, 0, :], in1=Gt[xb_][:, 0, :], op=ALU.mult),
                                  reads=rk, writes=['i_Yr' if first else 'i_t1'])
                            Pg.op('pool', lambda e: e.tensor_tensor(out=t2[:], in0=Xt[xb_][:, 1, :], in1=Gt[xb_][:, 1, :], op=ALU.mult),
                                  reads=rk, writes=['i_t2'])
                            Pg.op('dve', lambda e: e.tensor_tensor(out=(Yi if first else t3)[:], in0=Xt[xb_][:, 0, :], in1=Gt[xb_][:, 1, :], op=ALU.mult),
                                  reads=rk, writes=['i_Yi' if first else 'i_t3'])
                            Pg.op('pool', lambda e: e.tensor_tensor(out=t4[:], in0=Xt[xb_][:, 1, :], in1=Gt[xb_][:, 0, :], op=ALU.mult),
                                  reads=rk, writes=['i_t4'])
                            if not first:
                                Pg.op('dve', lambda e: e.tensor_tensor(out=Yr[:], in0=Yr[:], in1=t1[:], op=ALU.add), reads=['i_Yr', 'i_t1'], writes=['i_Yr'])
                                Pg.op('pool', lambda e: e.tensor_tensor(out=Yi[:], in0=Yi[:], in1=t3[:], op=ALU.add), reads=['i_Yi', 'i_t3'], writes=['i_Yi'])
                            Pg.op('dve', lambda e: e.tensor_tensor(out=Yr[:], in0=Yr[:], in1=t2[:], op=ALU.subtract), reads=['i_Yr', 'i_t2'], writes=['i_Yr'])
                            Pg.op('pool', lambda e: e.tensor_tensor(out=Yi[:], in0=Yi[:], in1=t4[:], op=ALU.add), reads=['i_Yi', 'i_t4'], writes=['i_Yi'])
                        Pg.op('act', lambda e: e.copy(out=Y16[:, kc, 0, :], in_=Yr[:]), reads=['i_Yr'], writes=[('i_Y16', kc, 0)])
                        Pg.op('act', lambda e: e.copy(out=Y16[:, kc, 1, :], in_=Yi[:]), reads=['i_Yi'], writes=[('i_Y16', kc, 1)])
                    ykeys = [('i_Y16', kc, ri) for kc in range(NKC) for ri in range(2)]
                    for sc in range(32):
                        ib = sc % 2
                        tg = a * 32 + sc
                        rows = slice(tg * 128, (tg + 1) * 128)
                        Pg.dma('sp' if sc % 2 == 0 else 'pool', IT[ib][:], G.c['inv'][sc], writes=['i_IT%d' % ib])
                        n_mm = 0
                        for kc in range(NKC):
                            for ri in range(2):
                                Pg.op('pe', lambda e: e.matmul(y_ps[ib][:], lhsT=IT[ib][:, kc, ri, :], rhs=Y16[:, kc, ri, :],
                                                               start=(n_mm == 0), stop=(n_mm == 2 * NKC - 1)),
                                      reads=['i_IT%d' % ib] + (ykeys if n_mm == 0 else []), writes=['i_yps%d' % ib])
                                n_mm += 1
                        if o == 0:
                            conv3_tile(G, 'i_', seq, C_HV, 0, tg, cwb, cbb, um, u0, up, vacc, 'act')
                            vk = 'i_acc'
                            vt = vacc
                        else:
                            Pg.dma('act', vacc[:], seq.Z1[rows, :], writes=['i_vz'])
                            vk = 'i_vz'
                            vt = vacc
                        Pg.op('dve', lambda e: e.tensor_tensor(out=yt[:], in0=y_ps[ib][:], in1=RNt[:], op=ALU.mult),
                              reads=['i_yps%d' % ib, 'h_RNt'], writes=['i_yt'])
                        Pg.op('pool', lambda e: e.tensor_tensor(out=zt_[:], in0=vt[:], in1=skb[:, o, :], op=ALU.mult),
                              reads=[vk, 'h_skb'], writes=['i_zt'])
                        Pg.op('dve', lambda e: e.tensor_tensor(out=yt[:], in0=yt[:], in1=zt_[:], op=ALU.add), reads=['i_yt', 'i_zt'], writes=['i_yt'])
                        Pg.op('pool', lambda e: e.tensor_copy(out=zt_[:, 0:1], in_=zt_[:, 0:1]), reads=['i_yt', vk, 'i_acc', 'i_vz'], writes=['i_zt'])
                        conv3_tile(G, 'i_', seq, C_H1 if o == 0 else C_H2, 1 + o, tg, cwb, cbb, um, u0, up, xacc, 'act')
                        Pg.op('dve', lambda e: e.tensor_tensor(out=yt[:], in0=yt[:], in1=xacc[:], op=ALU.mult), reads=['i_yt', 'i_acc'], writes=['i_yt'])
                        if o == 0:
                            Pg.dma('pool', seq.Z1[rows, :], yt[:], reads=['i_yt'])
                        else:
                            Pg.dma('act', hz[:], seq.U[rows, C_HZ:C_HZ + 512], writes=['i_hz'])
                            Pg.op('act', lambda e: e.activation(out=hz[:], in_=hz[:], func=AF.Silu), reads=['i_hz'], writes=['i_hz'])
                            Pg.op('dve', lambda e: e.tensor_tensor(out=yt[:], in0=yt[:], in1=hz[:], op=ALU.mult), reads=['i_yt', 'i_hz'], writes=['i_yt'])
                            Pg.dma('pool', seq.MIX[rows, 1024:1536], yt[:], reads=['i_yt'])
            Pg.barrier()
    Pg.barrier()


def build(dbg=None, mixers=None):
    dbg = dbg or {}
    nlayers = dbg.get('nlayers', DEPTH)
    nc = bass.Bass("TRN2", target_bir_lowering=False)
    G = Ctx()
    G.nc = nc
    G.dbg = dbg
    G.w = {k: nc.dram_tensor(k, s, F32, kind="ExternalInput").ap() for k, s in WEIGHT_SPECS.items()}
    G.c = {k: nc.dram_tensor('c_' + k, s, dt, kind="ExternalInput").ap() for k, (s, dt) in CONST_SPECS.items()}
    xa = nc.dram_tensor('xa', [LA, D], F32, kind="ExternalInput").ap()
    xb = nc.dram_tensor('xb', [LB, D], F32, kind="ExternalInput").ap()
    ya = nc.dram_tensor('ya', [LA, D], F32, kind="ExternalOutput").ap()
    yb = nc.dram_tensor('yb', [LB, D], F32, kind="ExternalOutput").ap()
    sk = "ExternalOutput" if dbg.get('dump') else "Internal"
    U = USplit(nc.dram_tensor('s_U1', [LB, USPLIT], F32, kind="Internal").ap(),
               nc.dram_tensor('s_U2', [LB, UT - USPLIT], F32, kind="Internal").ap())
    QKT = nc.dram_tensor('s_QKT', [1024, LB], F32, kind=sk).ap()
    MIX = nc.dram_tensor('s_MIX', [LB, 1536], F32, kind=sk).ap()
    Xa = nc.dram_tensor('s_Xa', [LA, D], F32, kind=sk).ap()
    Xb = nc.dram_tensor('s_Xb', [LB, D], F32, kind="Internal").ap()
    GTFd = nc.dram_tensor('s_GTF', [LB, 12], F32, kind=sk).ap()
    GTBd = nc.dram_tensor('s_GTB', [LB, 12], F32, kind=sk).ap()
    ABCd = nc.dram_tensor('s_ABC', [128, 8, 128], F32, kind=sk).ap()
    HFd = nc.dram_tensor('s_HF', [LB, 512], F32, kind="Internal").ap()
    XSd = nc.dram_tensor('s_XS', [4, NKC, 128, 2, 512], BF16, kind="Internal").ap()
    GSd = nc.dram_tensor('s_GS', [7, NKC, 128, 2, 512], BF16, kind="Internal").ap()
    Z1d = nc.dram_tensor('s_Z1', [LB, 512], F32, kind="Internal").ap()
    seqs = []
    for name, L, xin, X, yout in (('a', LA, xa, Xa, ya), ('b', LB, xb, Xb, yb)):
        s = Ctx()
        s.name, s.L, s.nb, s.xin, s.X, s.yout = name, L, L // P, xin, X, yout
        s.U, s.QKT, s.MIX = U, QKT, MIX
        s.GTF, s.GTB, s.ABC, s.HF = GTFd, GTBd, ABCd, HFd
        s.XS, s.GS, s.Z1 = XSd, GSd, Z1d
        seqs.append(s)
    if dbg.get('only_a'):
        seqs = seqs[:1]
    with ExitStack() as es:
        G.mixers = mixers or all_mixers
        G.P = Prog(nc, es)
        G.identb = sb(nc, es, 'identb', [128, 128], BF16)
        G.P.dma('sp', G.identb[:], G.c['identb'][:, :], writes=['identb'])
        G.P.barrier()
        for layer in range(nlayers):
            for seq in seqs:
                for blk in range(seq.nb):
                    phase_A(G, layer, seq, blk)
                G.mixers(G, layer, seq)
                if not dbg.get('skip_O'):
                    phase_O(G, layer, seq)
        G.P.barrier()
    return nc, G


def no_mixers(G, layer, seq):
    pass


def all_mixers(G, layer, seq):
    phase_G(G, layer, seq)
    phase_B(G, layer, seq)
    phase_R(G, layer, seq)
    phase_H(G, layer, seq)


def hy_mixers(G, layer, seq):
    phase_H(G, layer, seq)


def rt_mixers(G, layer, seq):
    phase_R(G, layer, seq)


def ml_mixers(G, layer, seq):
    phase_G(G, layer, seq)
    phase_B(G, layer, seq)


def build_with(dbg, mixers):
    return build(dbg, mixers)


_NC = None


def kernel(**inputs):
    global _NC
    if _NC is None:
        _NC = build()[0]
    nc = _NC
    c = host_consts()
    maps = []
    xs = np.ascontiguousarray(np.asarray(inputs['x_sample'], dtype=np.float32)[0])
    for i in range(8):
        m = {k: np.ascontiguousarray(np.asarray(inputs[k], dtype=np.float32)) for k in WEIGHT_SPECS}
        for k in CONST_SPECS:
            m['c_' + k] = c[k]
        m['xa'] = np.ascontiguousarray(np.asarray(inputs['x_prompt'], dtype=np.float32)[i])
        m['xb'] = xs
        maps.append(m)
    res = run_bass_kernel_spmd(nc, maps, core_ids=list(range(8)))
    y_prompt = np.stack([np.asarray(r['ya'], dtype=np.float32) for r in res.results], axis=0)
    y_sample = np.asarray(res.results[0]['yb'], dtype=np.float32)[None]
    return (y_prompt, y_sample)
```

```python
import math
from contextlib import ExitStack
import numpy as np
import ml_dtypes
import concourse.bass as bass
import concourse.mybir as mybir
from concourse.bass_utils import run_bass_kernel_spmd

F32 = mybir.dt.float32
BF16 = mybir.dt.bfloat16
ALU = mybir.AluOpType
AF = mybir.ActivationFunctionType
AX = mybir.AxisListType

D = 1024
DEPTH = 4
LA = 4096
LB = 16384
P = 4096
NF = 8192
NKC = 33
INC = 6672
UT = 5648
C_MLV, C_MLO, C_MLZ, C_G, C_RQ, C_RK, C_RV, C_RZ, C_HV, C_H1, C_H2, C_HZ = (
    0, 512, 1024, 1536, 1552, 2064, 2576, 3088, 3600, 4112, 4624, 5136)
RT_LGF = [math.log(1.0 - 2.0 ** (-5.0 - h)) for h in range(4)]
RT_LGB = [math.log(1.0 - 2.0 ** (-5.5 - h)) for h in range(4)]
HY_BANDS = 16
S_ML = 128 ** -0.5
S_RT = 128 ** -0.5
TWO_PI = 2.0 * math.pi


class Prog:
    NDMA = 6

    def __init__(self, nc, es):
        self.nc = nc
        self.eng = {'pe': nc.tensor, 'act': nc.scalar, 'dve': nc.vector, 'pool': nc.gpsimd, 'sp': nc.sync}
        self.sem = {}
        for k in ['pe', 'act', 'dve', 'pool']:
            self.sem[k] = es.enter_context(nc.semaphore('s_' + k))
        self.dq = {}
        for q in ['sp', 'pool', 'act']:
            self.dq[q] = 0
            for i in range(self.NDMA):
                self.sem[('d', q, i)] = es.enter_context(nc.semaphore('d_%s%d' % (q, i)))
        self.ccsems = [es.enter_context(nc.semaphore('cc_%d' % i)) for i in range(24)]
        self.ncc = 0
        self.cnt = {k: 0 for k in ['pe', 'act', 'dve', 'pool']}
        self.last = {}
        self.seen = {k: {} for k in self.eng}
        self.state = {}
        self.ninst = 0

    def _deps(self, e, reads, writes):
        deps = {}

        def add(k, v):
            if deps.get(k, 0) < v:
                deps[k] = v
        for key in reads:
            st = self.state.get(key)
            if st and st[0]:
                add(*st[0])
        for key in writes:
            st = self.state.get(key)
            if st:
                if st[0]:
                    add(*st[0])
                for k, v in st[1].items():
                    if k == e:
                        continue
                    add(k, v)
        for k, v in deps.items():
            if e == 'pe' and k == 'pe':
                continue
            if self.seen[e].get(k, 0) >= v:
                continue
            self.seen[e][k] = v
            self.eng[e].wait_ge(self.sem[k], v)
            self.ninst += 1

    def _record(self, reads, writes, tok):
        for key in reads:
            st = self.state.get(key)
            if st is None:
                st = self.state[key] = [None, {}]
            if st[1].get(tok[0], 0) < tok[1]:
                st[1][tok[0]] = tok[1]
        for key in writes:
            self.state[key] = [tok, {}]
        self.last[tok[0]] = tok[1]

    def op(self, e, fn, reads=(), writes=()):
        self._deps(e, reads, writes)
        self.cnt[e] += 1
        tok = (e, self.cnt[e])
        fn(self.eng[e]).then_inc(self.sem[e], 1)
        self.ninst += 1
        self._record(reads, writes, tok)

    def dma(self, q, out, in_, reads=(), writes=(), **kw):
        n = self.dq[q]
        self.dq[q] += 1
        slot = n % self.NDMA
        val = 16 * (n // self.NDMA + 1)
        key = ('d', q, slot)
        self._deps(q, reads, writes)
        if val > 16 and self.seen[q].get(key, 0) < val - 16:
            self.seen[q][key] = val - 16
            self.eng[q].wait_ge(self.sem[key], val - 16)
        self.eng[q].dma_start(out=out, in_=in_, **kw).then_inc(self.sem[key], 16)
        self.ninst += 1
        self._record(reads, writes, (key, val))

    def allgather(self, in2d, out2d, dummy):
        self.barrier()
        sem = self.ccsems[self.ncc]
        self.ncc += 1
        self.eng['pool'].collective_compute("AllGather", ALU.bypass, replica_groups=[list(range(8))],
                                            ins=[in2d.opt()], outs=[out2d.opt()]).then_inc(sem)
        self.eng['pool'].wait_ge(sem, 1)
        self.op('pool', lambda e: e.memset(dummy[0:1, 0:1], 0.0), writes=['cc_dummy'])
        self.barrier()

    def barrier(self):
        for e in self.eng:
            for k, v in self.last.items():
                if self.seen[e].get(k, 0) >= v:
                    continue
                self.seen[e][k] = v
                self.eng[e].wait_ge(self.sem[k], v)
        self.state = {}


class Ctx:
    pass


USPLIT = 3088


class USplit:
    def __init__(self, u1, u2):
        self.u1, self.u2 = u1, u2

    def __getitem__(self, key):
        rows, cols = key
        a, b = cols.start, cols.stop
        if b <= USPLIT:
            return self.u1[rows, a:b]
        assert a >= USPLIT, (a, b)
        return self.u2[rows, a - USPLIT:b - USPLIT]

    def pieces(self, rows, a, b):
        out = []
        if a < USPLIT:
            e = min(b, USPLIT)
            out.append((self.u1[rows, a:e], 0, e - a))
        if b > USPLIT:
            st = max(a, USPLIT)
            out.append((self.u2[rows, st - USPLIT:b - USPLIT], st - a, b - a))
        return out


def bf(a):
    return np.ascontiguousarray(a).astype(ml_dtypes.bfloat16)


_CONST = None


def host_consts():
    global _CONST
    if _CONST is not None:
        return _CONST
    c = {}
    c['identb'] = bf(np.eye(128))
    c['identf'] = np.eye(128, dtype=np.float32)
    c['jrev'] = np.ascontiguousarray(np.eye(128, dtype=np.float32)[::-1])
    sel = np.zeros((8, 8, 128), np.float32)
    for r in range(8):
        sel[r, r, :] = 1.0
    c['sel'] = sel
    j = np.arange(128)[:, None]
    i = np.arange(128)[None, :]
    c['maskf'] = (S_ML * (j <= i)).astype(np.float32)
    c['maskb'] = (S_ML * (j >= i)).astype(np.float32)
    dct = np.zeros((128, 4, 128), np.float64)
    qwf = np.zeros((128, 4, 128), np.float64)
    qwb = np.zeros((128, 4, 128), np.float64)
    kwf = np.zeros((128, 4), np.float64)
    kwb = np.zeros((128, 4), np.float64)
    for h in range(4):
        gf, gb = RT_LGF[h], RT_LGB[h]
        dct[:, h, :] = S_RT * (np.where(i >= j, np.exp(gf * np.maximum(i - j, 0)), 0.0)
                               + np.where(j >= i, np.exp(gb * np.maximum(j - i, 0)), 0.0))
        qwf[:, h, :] = S_RT * np.exp(gf * (i + 1.0))
        qwb[:, h, :] = S_RT * np.exp(gb * (128.0 - i))
        kwf[:, h] = np.exp(gf * (127.0 - j[:, 0]))
        kwb[:, h] = np.exp(gb * (j[:, 0] * 1.0))
    c['dct'] = dct.astype(np.float32)
    c['qwf'] = qwf.astype(np.float32)
    c['qwb'] = qwb.astype(np.float32)
    c['kwf'] = kwf.astype(np.float32)
    c['kwb'] = kwb.astype(np.float32)
    inv = (np.float32(10000.0) ** (-np.arange(0, 128, 2, dtype=np.float32) / np.float32(128))).astype(np.float32)
    ang = (np.arange(LB, dtype=np.float32)[:, None] * inv[None, :]).astype(np.float32)
    c['rot'] = np.concatenate([np.cos(ang), np.sin(ang)], axis=1).astype(np.float32)
    for name, L, nb in (('a', LA, 1), ('b', LB, 4)):
        t = np.linspace(0.0, 1.0, L, dtype=np.float32)
        w = (np.float32(2.0 * math.pi / L) * np.arange(L, dtype=np.float32)).astype(np.float32)
        bands = np.linspace(1e-4, HY_BANDS - 1, HY_BANDS, dtype=np.float32)
        feats = np.concatenate([t[:, None], np.cos(bands[None, :] * w[:, None]),
                                -np.sin(bands[None, :] * w[:, None])], axis=1).astype(np.float32)
        pieces = list(range(-(nb - 1), nb))
        ft = np.zeros((len(pieces), 2, 33, P), np.float32)
        tn = np.zeros((len(pieces), 2, 128, 32), np.float32)
        for pi, d in enumerate(pieces):
            for half in range(2):
                jj = np.arange(P) + half * P
                m = np.where(jj < P, jj, jj - NF)
                lag = np.abs(d * P + m)
                lag = np.minimum(lag, L - 1)
                ft[pi, half] = feats[lag].T
                tn[pi, half] = (-t[lag]).reshape(32, 128).T
        c['feat_' + name] = ft
        c['ntn_' + name] = tn
    s = np.arange(NF, dtype=np.int64)
    k = np.arange(NKC * 128, dtype=np.int64)
    ph = (s[:, None] * k[None, :]) % NF
    angm = (2.0 * np.pi / NF) * ph
    valid = (k <= NF // 2)[None, :]
    cosm = np.where(valid, np.cos(angm), 0.0)
    sinm = np.where(valid, np.sin(angm), 0.0)
    fc = cosm.reshape(64, 128, NKC, 128)
    fs = (-sinm).reshape(64, 128, NKC, 128)
    ftab = np.stack([fc, fs], axis=0)
    c['fwd'] = bf(ftab.transpose(3, 2, 1, 0, 4))
    wk = np.where((k == 0) | (k == NF // 2), 1.0, 2.0) / NF
    ci = (cosm[:P] * wk[None, :]).T
    si = (-sinm[:P] * wk[None, :]).T
    ci = ci.reshape(NKC, 128, 32, 128)
    si = si.reshape(NKC, 128, 32, 128)
    itab = np.stack([ci, si], axis=0)
    c['inv'] = bf(itab.transpose(3, 2, 1, 0, 4))
    _CONST = c
    return c


CONST_SPECS = {
    'identb': ([128, 128], BF16), 'identf': ([128, 128], F32), 'jrev': ([128, 128], F32),
    'sel': ([8, 8, 128], F32), 'maskf': ([128, 128], F32), 'maskb': ([128, 128], F32),
    'dct': ([128, 4, 128], F32), 'qwf': ([128, 4, 128], F32), 'qwb': ([128, 4, 128], F32),
    'kwf': ([128, 4], F32), 'kwb': ([128, 4], F32), 'rot': ([LB, 128], F32),
    'feat_a': ([1, 2, 33, P], F32), 'ntn_a': ([1, 2, 128, 32], F32),
    'feat_b': ([7, 2, 33, P], F32), 'ntn_b': ([7, 2, 128, 32], F32),
    'fwd': ([NKC, 128, 64, 2, 128], BF16), 'inv': ([32, 128, NKC, 2, 128], BF16),
    'fwd_own': ([5, 128, 64, 2, 128], BF16),
}
WEIGHT_SPECS = {
    'norm_g': [DEPTH, D], 'w_in': [DEPTH, D, INC], 'ml_conv_w': [DEPTH, 3, 1024], 'ml_conv_b': [DEPTH, 1024],
    'ml_gate_b': [DEPTH, 4, 4], 'ml_norm_g': [DEPTH, 512], 'rt_norm_g': [DEPTH, 512],
    'hy_conv_w': [DEPTH, 3, 1536], 'hy_conv_b': [DEPTH, 1536], 'hy_w1': [DEPTH, 33, 64], 'hy_b1': [DEPTH, 64],
    'hy_w2': [DEPTH, 64, 64], 'hy_b2': [DEPTH, 64], 'hy_w3': [DEPTH, 64, 2048], 'hy_freq': [DEPTH, 64],
    'hy_deltas': [DEPTH, 2, 2, 512], 'hy_skip': [DEPTH, 2, 512], 'w_out': [DEPTH, 1536, D], 'final_g': [D],
}


def bcast_rows(ap1d, nparts):
    return ap1d.partition_broadcast(nparts)


_UID = [0]


def sb(nc, es, name, shape, dt):
    _UID[0] += 1
    return es.enter_context(nc.sbuf_tensor('%s_%d' % (name, _UID[0]), shape, dt))


def ps(nc, es, name, shape, dt=F32):
    _UID[0] += 1
    return es.enter_context(nc.psum_tensor('%s_%d' % (name, _UID[0]), shape, dt))


def rms_rows(G, xt, key_x, junk, ss, rstd, keyp):
    Pg = G.P
    Pg.op('act', lambda e: e.activation(out=junk[:], in_=xt, func=AF.Square), reads=[key_x], writes=[keyp + 'junk'])
    Pg.op('dve', lambda e: e.reduce_sum(out=ss[:], in_=junk[:], axis=AX.X), reads=[keyp + 'junk'], writes=[keyp + 'ss'])
    Pg.op('dve', lambda e: e.tensor_scalar(out=rstd[:], in0=ss[:], scalar1=1.0 / D, scalar2=1e-6,
                                           op0=ALU.mult, op1=ALU.add), reads=[keyp + 'ss'], writes=[keyp + 'rstd'])
    Pg.op('act', lambda e: e.sqrt(out=rstd[:], in_=rstd[:]), reads=[keyp + 'rstd'], writes=[keyp + 'rstd'])
    Pg.op('dve', lambda e: e.reciprocal(out=rstd[:], in_=rstd[:]), reads=[keyp + 'rstd'], writes=[keyp + 'rstd'])


def phase_A(G, layer, seq, blk):
    nc, Pg = G.nc, G.P
    xsrc = seq.xin if layer == 0 else seq.X
    t0 = blk * P
    with ExitStack() as es:
        hT = sb(nc, es, 'a_hT', [128, 8, P], BF16)
        gbc = sb(nc, es, 'a_gbc', [128, D], F32)
        xt = [sb(nc, es, 'a_xt%d' % i, [128, D], F32) for i in range(2)]
        junk = sb(nc, es, 'a_junk', [128, D], F32)
        ss = sb(nc, es, 'a_ss', [128, 1], F32)
        rstd = sb(nc, es, 'a_rstd', [128, 1], F32)
        h16 = [sb(nc, es, 'a_h16%d' % i, [128, D], BF16) for i in range(2)]
        wst = [sb(nc, es, 'a_wst%d' % i, [128, 8, 512], F32) for i in range(2)]
        w16 = [sb(nc, es, 'a_w16%d' % i, [128, 8, 512], BF16) for i in range(2)]
        ost = [sb(nc, es, 'a_ost%d' % i, [128, 512], F32) for i in range(4)]
        tp = [ps(nc, es, 'a_tp%d' % i, [128, 512], BF16) for i in range(2)]
        mm = [ps(nc, es, 'a_mm%d' % i, [128, 512], F32) for i in range(4)]
        Pg.dma('sp', gbc[:], bcast_rows(G.w['norm_g'][layer], 128), writes=['a_gbc'])
        for t in range(32):
            b = t % 2
            Pg.dma('sp', xt[b][:], xsrc[t0 + t * 128: t0 + (t + 1) * 128, :], writes=['a_xt%d' % b])
            rms_rows(G, xt[b][:], 'a_xt%d' % b, junk, ss, rstd, 'a_')
            Pg.op('dve', lambda e: e.scalar_tensor_tensor(out=h16[b][:], in0=xt[b][:], scalar=rstd[:, 0:1], in1=gbc[:],
                                                          op0=ALU.mult, op1=ALU.mult),
                  reads=['a_xt%d' % b, 'a_rstd', 'a_gbc'], writes=['a_h16%d' % b])
            for half in range(2):
                for jj in range(4):
                    kc = half * 4 + jj
                    Pg.op('pe', lambda e: e.transpose(tp[half][:, jj * 128:(jj + 1) * 128],
                                                      h16[b][:, kc * 128:(kc + 1) * 128], G.identb[:]),
                          reads=['a_h16%d' % b], writes=['a_tp%d' % half])
                dst = hT[:, half * 4:half * 4 + 4, t * 128:(t + 1) * 128]
                src = tp[half][:].rearrange('p (j t) -> p j t', j=4)
                if half == 0:
                    Pg.op('act', lambda e: e.copy(out=dst, in_=src), reads=['a_tp0'], writes=[('a_hT', t, 0)])
                else:
                    Pg.op('dve', lambda e: e.tensor_copy(out=dst, in_=src), reads=['a_tp1'], writes=[('a_hT', t, 1)])
        groups = [('f', g * 128, 128) for g in range(8)] + [('t', 1024 + g * 512, 512) for g in range(11)] + [('t', 1024 + 11 * 512, 16)]
        oi = 0
        for gi, (kind, c0, wd) in enumerate(groups):
            wb = gi % 2
            wsrc = G.w['w_in'][layer, :, c0:c0 + wd].rearrange('(kc p) c -> p kc c', p=128)
            Pg.dma('sp', wst[wb][:, :, 0:wd], wsrc, writes=['a_wst%d' % wb])
            ceng = 'pool' if gi % 2 == 0 else 'dve'
            Pg.op(ceng, lambda e: e.tensor_copy(out=w16[wb][:, :, 0:wd], in_=wst[wb][:, :, 0:wd]),
                  reads=['a_wst%d' % wb], writes=['a_w16%d' % wb])
            for t in range(32 if kind == 't' else 8):
                pb = oi % 4
                if kind == 't':
                    for kc in range(8):
                        Pg.op('pe', lambda e: e.matmul(mm[pb][:, 0:wd], lhsT=hT[:, kc, t * 128:(t + 1) * 128],
                                                       rhs=w16[wb][:, kc, 0:wd], start=(kc == 0), stop=(kc == 7)),
                              reads=[('a_hT', t, 0), ('a_hT', t, 1), 'a_w16%d' % wb], writes=['a_mm%d' % pb])
                    dst = None
                    ow = wd
                else:
                    hk = [('a_hT', 4 * t + q, hh) for q in range(4) for hh in range(2)]
                    for kc in range(8):
                        Pg.op('pe', lambda e: e.matmul(mm[pb][:, :], lhsT=w16[wb][:, kc, 0:128],
                                                       rhs=hT[:, kc, t * 512:(t + 1) * 512], start=(kc == 0), stop=(kc == 7)),
                              reads=hk + ['a_w16%d' % wb], writes=['a_mm%d' % pb])
                    dst = seq.QKT[c0:c0 + 128, t0 + t * 512:t0 + (t + 1) * 512]
                    ow = 512
                if oi % 2 == 0:
                    Pg.op('act', lambda e: e.copy(out=ost[pb][:, 0:ow], in_=mm[pb][:, 0:ow]),
                          reads=['a_mm%d' % pb], writes=['a_ost%d' % pb])
                else:
                    Pg.op('dve', lambda e: e.tensor_copy(out=ost[pb][:, 0:ow], in_=mm[pb][:, 0:ow]),
                          reads=['a_mm%d' % pb], writes=['a_ost%d' % pb])
                if dst is None:
                    for (dap, o0, o1) in seq.U.pieces(slice(t0 + t * 128, t0 + (t + 1) * 128), c0 - 1024, c0 - 1024 + wd):
                        Pg.dma('pool' if oi % 2 == 0 else 'sp', dap, ost[pb][:, o0:o1], reads=['a_ost%d' % pb])
                else:
                    Pg.dma('pool' if oi % 2 == 0 else 'sp', dst, ost[pb][:, 0:ow], reads=['a_ost%d' % pb])
                oi += 1
    Pg.barrier()


def phase_O(G, layer, seq):
    nc, Pg = G.nc, G.P
    xsrc = seq.xin if layer == 0 else seq.X
    last = layer == DEPTH - 1
    with ExitStack() as es:
        wo = sb(nc, es, 'o_wo', [128, 12, D], BF16)
        wst = [sb(nc, es, 'o_wst%d' % i, [128, D], F32) for i in range(2)]
        mix = [sb(nc, es, 'o_mix%d' % i, [128, 1536], F32) for i in range(2)]
        m16 = [sb(nc, es, 'o_m16%d' % i, [128, 1536], BF16) for i in range(2)]
        mT = [sb(nc, es, 'o_mT%d' % i, [128, 12, 128], BF16) for i in range(2)]
        xt = [sb(nc, es, 'o_xt%d' % i, [128, D], F32) for i in range(2)]
        xn = [sb(nc, es, 'o_xn%d' % i, [128, D], F32) for i in range(2)]
        gbc = sb(nc, es, 'o_gbc', [128, D], F32)
        junk = sb(nc, es, 'o_junk', [128, D], F32)
        ss = sb(nc, es, 'o_ss', [128, 1], F32)
        rstd = sb(nc, es, 'o_rstd', [128, 1], F32)
        yo = [sb(nc, es, 'o_yo%d' % i, [128, D], F32) for i in range(2)]
        tp = [ps(nc, es, 'o_tp%d' % i, [128, 512], BF16) for i in range(3)]
        mm = [ps(nc, es, 'o_mm%d' % i, [128, 512], F32) for i in range(4)]
        for kc in range(12):
            b = kc % 2
            Pg.dma('sp', wst[b][:], G.w['w_out'][layer, kc * 128:(kc + 1) * 128, :], writes=['o_wst%d' % b])
            Pg.op('pool' if b else 'dve', lambda e: e.tensor_copy(out=wo[:, kc, :], in_=wst[b][:]),
                  reads=['o_wst%d' % b], writes=[('o_wo', kc)])
        wokeys = [('o_wo', kc) for kc in range(12)]
        if last:
            Pg.dma('sp', gbc[:], bcast_rows(G.w['final_g'], 128), writes=['o_gbc'])
        for t in range(seq.L // 128):
            b = t % 2
            rows = slice(t * 128, (t + 1) * 128)
            Pg.dma('sp', mix[b][:], seq.MIX[rows, :], writes=['o_mix%d' % b])
            Pg.dma('sp', xt[b][:], xsrc[rows, :], writes=['o_xt%d' % b])
            Pg.op('pool', lambda e: e.tensor_copy(out=m16[b][:], in_=mix[b][:]), reads=['o_mix%d' % b], writes=['o_m16%d' % b])
            for g3 in range(3):
                for jj in range(4):
                    kc = g3 * 4 + jj
                    Pg.op('pe', lambda e: e.transpose(tp[g3][:, jj * 128:(jj + 1) * 128],
                                                      m16[b][:, kc * 128:(kc + 1) * 128], G.identb[:]),
                          reads=['o_m16%d' % b], writes=['o_tp%d' % g3])
                dst = mT[b][:, g3 * 4:g3 * 4 + 4, :]
                src = tp[g3][:].rearrange('p (j t) -> p j t', j=4)
                if g3 == 1:
                    Pg.op('dve', lambda e: e.tensor_copy(out=dst, in_=src), reads=['o_tp%d' % g3], writes=[('o_mT%d' % b, g3)])
                else:
                    Pg.op('act', lambda e: e.copy(out=dst, in_=src), reads=['o_tp%d' % g3], writes=[('o_mT%d' % b, g3)])
            for n in range(2):
                pb = (t * 2 + n) % 4
                for kc in range(12):
                    Pg.op('pe', lambda e: e.matmul(mm[pb][:], lhsT=mT[b][:, kc, :], rhs=wo[:, kc, n * 512:(n + 1) * 512],
                                                   start=(kc == 0), stop=(kc == 11)),
                          reads=[('o_mT%d' % b, g3) for g3 in range(3)] + wokeys, writes=['o_mm%d' % pb])
                Pg.op('dve', lambda e: e.tensor_tensor(out=xn[b][:, n * 512:(n + 1) * 512], in0=mm[pb][:],
                                                       in1=xt[b][:, n * 512:(n + 1) * 512], op=ALU.add),
                      reads=['o_mm%d' % pb, 'o_xt%d' % b], writes=[('o_xn%d' % b, n)])
            xk = [('o_xn%d' % b, 0), ('o_xn%d' % b, 1)]
            if not last:
                Pg.dma('pool', seq.X[rows, :], xn[b][:], reads=xk)
            else:
                Pg.op('pool', lambda e: e.tensor_copy(out=xn[b][:, 0:1], in_=xn[b][:, 0:1]), reads=xk, writes=['o_xnf%d' % b])
                rms_rows(G, xn[b][:], 'o_xnf%d' % b, junk, ss, rstd, 'o_')
                Pg.op('dve', lambda e: e.scalar_tensor_tensor(out=yo[b][:], in0=xn[b][:], scalar=rstd[:, 0:1], in1=gbc[:],
                                                              op0=ALU.mult, op1=ALU.mult),
                      reads=xk + ['o_xnf%d' % b, 'o_rstd', 'o_gbc'], writes=['o_yo%d' % b])
                Pg.dma('pool', seq.yout[rows, :], yo[b][:], reads=['o_yo%d' % b])
    Pg.barrier()


def headnorm_gate(G, pre, h, hkeys, z32, zkey, gbc, gkey, dst, dma_q):
    Pg = G.P
    T = G.hn
    h3 = h[:].rearrange('p (a d) -> p a d', a=4)
    Pg.op('dve', lambda e: e.reduce_sum(out=T['s1'][:], in_=h3, axis=AX.X), reads=hkeys, writes=[pre + 's1'])
    Pg.op('dve', lambda e: e.tensor_scalar(out=T['s1'][:], in0=T['s1'][:], scalar1=1.0 / 128, scalar2=None, op0=ALU.mult),
          reads=[pre + 's1'], writes=[pre + 's1'])
    hc3 = T['hc'][:].rearrange('p (a d) -> p a d', a=4)
    Pg.op('dve', lambda e: e.tensor_tensor(out=hc3, in0=h3, in1=T['s1'][:].unsqueeze(2).to_broadcast([128, 4, 128]),
                                           op=ALU.subtract), reads=hkeys + [pre + 's1'], writes=[pre + 'hc'])
    Pg.op('act', lambda e: e.activation(out=T['sq'][:], in_=T['hc'][:], func=AF.Square), reads=[pre + 'hc'], writes=[pre + 'sq'])
    Pg.op('dve', lambda e: e.reduce_sum(out=T['s2'][:], in_=T['sq'][:].rearrange('p (a d) -> p a d', a=4), axis=AX.X),
          reads=[pre + 'sq'], writes=[pre + 's2'])
    Pg.op('dve', lambda e: e.tensor_scalar(out=T['s2'][:], in0=T['s2'][:], scalar1=1.0 / 128, scalar2=1e-5,
                                           op0=ALU.mult, op1=ALU.add), reads=[pre + 's2'], writes=[pre + 's2'])
    Pg.op('act', lambda e: e.sqrt(out=T['s2'][:], in_=T['s2'][:]), reads=[pre + 's2'], writes=[pre + 's2'])
    Pg.op('dve', lambda e: e.reciprocal(out=T['s2'][:], in_=T['s2'][:]), reads=[pre + 's2'], writes=[pre + 's2'])
    Pg.op('dve', lambda e: e.tensor_tensor(out=hc3, in0=hc3, in1=T['s2'][:].unsqueeze(2).to_broadcast([128, 4, 128]),
                                           op=ALU.mult), reads=[pre + 'hc', pre + 's2'], writes=[pre + 'hc'])
    Pg.op('pool', lambda e: e.tensor_tensor(out=T['hc'][:], in0=T['hc'][:], in1=gbc[:], op=ALU.mult),
          reads=[pre + 'hc', gkey], writes=[pre + 'hc'])
    Pg.op('act', lambda e: e.activation(out=T['sq'][:], in_=z32, func=AF.Silu), reads=[zkey, pre + 'sq'], writes=[pre + 'sq'])
    Pg.op('dve', lambda e: e.tensor_tensor(out=T['out'][:], in0=T['hc'][:], in1=T['sq'][:], op=ALU.mult),
          reads=[pre + 'hc', pre + 'sq'], writes=[pre + 'out'])
    Pg.dma(dma_q, dst, T['out'][:], reads=[pre + 'out'])


def alloc_hn(G, es, pre):
    nc = G.nc
    G.hn = {'s1': sb(nc, es, pre + 's1', [128, 4], F32), 's2': sb(nc, es, pre + 's2', [128, 4], F32),
            'hc': sb(nc, es, pre + 'hc', [128, 512], F32), 'sq': sb(nc, es, pre + 'sq', [128, 512], F32),
            'out': sb(nc, es, pre + 'out', [128, 512], F32)}


def phase_G(G, layer, seq):
    nc, Pg = G.nc, G.P
    L = seq.L
    NC = L // 128
    nseg = L // P
    gb = G.w['ml_gate_b'][layer].rearrange('(t s) h -> s t h', s=2)
    with ExitStack() as es:
        identf = sb(nc, es, 'g_identf', [128, 128], F32)
        jrev = sb(nc, es, 'g_jrev', [128, 128], F32)
        sel = sb(nc, es, 'g_sel', [8, 8, 128], F32)
        ones = sb(nc, es, 'g_ones', [8, P], F32)
        one1 = sb(nc, es, 'g_one1', [8, 1], F32)
        bi8 = sb(nc, es, 'g_bi8', [8, 1], F32)
        bf8 = sb(nc, es, 'g_bf8', [8, 1], F32)
        cr = sb(nc, es, 'g_cr', [8, 1], F32)
        cnb = sb(nc, es, 'g_cnb', [8, 1], F32)
        gtF = sb(nc, es, 'g_gtF', [128, 32, 16], F32)
        gtB = sb(nc, es, 'g_gtB', [128, 32, 16], F32)
        liF = sb(nc, es, 'g_liF', [128, 32, 8], F32)
        liB = sb(nc, es, 'g_liB', [128, 32, 8], F32)
        pfF = sb(nc, es, 'g_pfF', [128, 32, 8], F32)
        pfB = sb(nc, es, 'g_pfB', [128, 32, 8], F32)
        LI = sb(nc, es, 'g_LI', [8, P], F32)
        SP = sb(nc, es, 'g_SP', [8, P], F32)
        NB = sb(nc, es, 'g_NB', [8, P], F32)
        AL = sb(nc, es, 'g_AL', [8, P], F32)
        RR = sb(nc, es, 'g_RR', [8, P], F32)
        EN = sb(nc, es, 'g_EN', [8, P], F32)
        KW = sb(nc, es, 'g_KW', [8, P], F32)
        CC = sb(nc, es, 'g_CC', [8, P], F32)
        RP = sb(nc, es, 'g_RP', [8, 33], F32)
        AA = sb(nc, es, 'g_AA', [8, 32], F32)
        T1s = [sb(nc, es, 'g_T1s%d' % i, [128, 24], F32) for i in range(2)]
        GTF = sb(nc, es, 'g_GTF', [128, 32, 12], F32)
        GTB = sb(nc, es, 'g_GTB', [128, 32, 12], F32)
        ABCs = sb(nc, es, 'g_ABCs', [128, 8, 32], F32)
        li_ps = [ps(nc, es, 'g_lips%d' % i, [8, 512]) for i in range(2)]
        pf_ps = [ps(nc, es, 'g_pfps%d' % i, [8, 512]) for i in range(2)]
        t_ps = [ps(nc, es, 'g_tps%d' % i, [128, 24]) for i in range(2)]
        t2_ps = ps(nc, es, 'g_t2ps', [128, 24])
        bc_ps = ps(nc, es, 'g_bcps', [128, 256])
        Pg.dma('sp', identf[:], G.c['identf'][:, :], writes=['g_identf'])
        Pg.dma('sp', jrev[:], G.c['jrev'][:, :], writes=['g_jrev'])
        Pg.dma('sp', sel[:], G.c['sel'][:, :, :], writes=['g_sel'])
        gbl = G.w['ml_gate_b'][layer]
        Pg.dma('sp', bi8[0:4, :], gbl[0].unsqueeze(1), writes=['g_bi8'])
        Pg.dma('sp', bi8[4:8, :], gbl[2].unsqueeze(1), writes=['g_bi8'])
        Pg.dma('sp', bf8[0:4, :], gbl[1].unsqueeze(1), writes=['g_bf8'])
        Pg.dma('sp', bf8[4:8, :], gbl[3].unsqueeze(1), writes=['g_bf8'])
        Pg.op('dve', lambda e: e.tensor_scalar(out=bf8[:], in0=bf8[:], scalar1=-1.0, scalar2=None, op0=ALU.mult),
              reads=['g_bf8'], writes=['g_bf8'])
        Pg.op('pool', lambda e: e.memset(ones[:], 1.0), writes=['g_ones'])
        Pg.op('pool', lambda e: e.memset(one1[:], 1.0), writes=['g_one1'])
        Pg.op('pool', lambda e: e.memset(cr[:], -1e30), writes=['g_cr'])
        Pg.op('pool', lambda e: e.memset(cnb[:], 0.0), writes=['g_cnb'])
        for t_, nm in ((liF, 'g_liF'), (liB, 'g_liB'), (pfF, 'g_pfF'), (pfB, 'g_pfB')):
            Pg.op('pool', lambda e: e.memset(t_[:], 0.0), writes=[nm])
        for sg in range(nseg):
            c0 = NC - 32 * (sg + 1)
            Pg.dma('sp', gtF[:], seq.U[sg * P:(sg + 1) * P, C_G:C_G + 16].rearrange('(n p) c -> p n c', p=128),
                   writes=['g_gtF'])
            Pg.dma('pool', gtB[:], seq.U[c0 * 128:(c0 + 32) * 128, C_G:C_G + 16].rearrange('(n p) c -> p n c', p=128),
                   writes=['g_gtB'])
            Pg.op('dve', lambda e: e.tensor_copy(out=liF[:, :, 0:4], in_=gtF[:, :, 0:4]), reads=['g_gtF'], writes=['g_liF'])
            Pg.op('dve', lambda e: e.tensor_copy(out=pfF[:, :, 0:4], in_=gtF[:, :, 4:8]), reads=['g_gtF'], writes=['g_pfF'])
            Pg.op('dve', lambda e: e.tensor_copy(out=liB[:, :, 4:8], in_=gtB[:, :, 8:12]), reads=['g_gtB'], writes=['g_liB'])
            Pg.op('dve', lambda e: e.tensor_copy(out=pfB[:, :, 4:8], in_=gtB[:, :, 12:16]), reads=['g_gtB'], writes=['g_pfB'])
            for gq in range(8):
                pb = gq % 2
                for mm_ in range(4):
                    m = gq * 4 + mm_
                    cs = slice(mm_ * 128, (mm_ + 1) * 128)
                    Pg.op('pe', lambda e: e.matmul(li_ps[pb][:, cs], lhsT=liF[:, m, :], rhs=identf[:], start=True, stop=False),
                          reads=['g_liF', 'g_identf'], writes=['g_lips%d' % pb])
                    Pg.op('pe', lambda e: e.matmul(li_ps[pb][:, cs], lhsT=liB[:, 31 - m, :], rhs=jrev[:], start=False, stop=True),
                          reads=['g_liB', 'g_jrev'], writes=['g_lips%d' % pb])
                    Pg.op('pe', lambda e: e.matmul(pf_ps[pb][:, cs], lhsT=pfF[:, m, :], rhs=identf[:], start=True, stop=False),
                          reads=['g_pfF', 'g_identf'], writes=['g_pfps%d' % pb])
                    Pg.op('pe', lambda e: e.matmul(pf_ps[pb][:, cs], lhsT=pfB[:, 31 - m, :], rhs=jrev[:], start=False, stop=True),
                          reads=['g_pfB', 'g_jrev'], writes=['g_pfps%d' % pb])
                gs = slice(gq * 512, (gq + 1) * 512)
                Pg.op('act', lambda e: e.activation(out=LI[:, gs], in_=li_ps[pb][:], func=AF.Identity, bias=bi8[:, 0:1], scale=1.0),
                      reads=['g_lips%d' % pb, 'g_bi8'], writes=['g_LI'])
                Pg.op('act', lambda e: e.activation(out=SP[:, gs], in_=pf_ps[pb][:], func=AF.Exp, bias=bf8[:, 0:1], scale=-1.0),
                      reads=['g_pfps%d' % pb, 'g_bf8'], writes=['g_SP'])
            Pg.op('act', lambda e: e.activation(out=SP[:], in_=SP[:], func=AF.Ln, bias=one1[:, 0:1], scale=1.0),
                  reads=['g_SP', 'g_one1'], writes=['g_SP'])
            Pg.op('dve', lambda e: e.tensor_tensor_scan(out=NB[:], data0=ones[:], data1=SP[:], initial=cnb[:, 0:1],
                                                        op0=ALU.mult, op1=ALU.add),
                  reads=['g_ones', 'g_SP', 'g_cnb'], writes=['g_NB'])
            Pg.op('dve', lambda e: e.tensor_tensor(out=AL[:], in0=LI[:], in1=NB[:], op=ALU.add),
                  reads=['g_LI', 'g_NB'], writes=['g_AL'])
            Pg.op('dve', lambda e: e.tensor_tensor_scan(out=RR[:], data0=ones[:], data1=AL[:], initial=cr[:, 0:1],
                                                        op0=ALU.mult, op1=ALU.max),
                  reads=['g_ones', 'g_AL', 'g_cr'], writes=['g_RR'])
            Pg.op('dve', lambda e: e.tensor_tensor(out=EN[:], in0=RR[:], in1=NB[:], op=ALU.subtract),
                  reads=['g_RR', 'g_NB'], writes=['g_EN'])
            Pg.op('act', lambda e: e.activation(out=EN[:], in_=EN[:], func=AF.Exp, scale=-1.0), reads=['g_EN'], writes=['g_EN'])
            Pg.op('dve', lambda e: e.tensor_copy(out=RP[:, 0:1], in_=cr[:]), reads=['g_cr'], writes=['g_RP'])
            Pg.op('dve', lambda e: e.tensor_copy(out=RP[:, 1:33], in_=RR[:].rearrange('p (n t) -> p n t', t=128)[:, :, 127]),
                  reads=['g_RR', 'g_RP'], writes=['g_RP'])
            Pg.op('dve', lambda e: e.tensor_tensor(out=AA[:], in0=RP[:, 0:32], in1=RP[:, 1:33], op=ALU.subtract),
                  reads=['g_RP'], writes=['g_AA'])
            Pg.op('act', lambda e: e.activation(out=AA[:], in_=AA[:], func=AF.Exp), reads=['g_AA'], writes=['g_AA'])
            rb = RP[:, 1:33].unsqueeze(2).to_broadcast([8, 32, 128])
            Pg.op('dve', lambda e: e.tensor_tensor(out=KW[:].rearrange('p (n t) -> p n t', t=128),
                                                   in0=AL[:].rearrange('p (n t) -> p n t', t=128), in1=rb, op=ALU.subtract),
                  reads=['g_AL', 'g_RP'], writes=['g_KW'])
            Pg.op('act', lambda e: e.activation(out=KW[:], in_=KW[:], func=AF.Exp), reads=['g_KW'], writes=['g_KW'])
            Pg.op('dve', lambda e: e.tensor_tensor(out=CC[:].rearrange('p (n t) -> p n t', t=128), in0=rb,
                                                   in1=RR[:].rearrange('p (n t) -> p n t', t=128), op=ALU.subtract),
                  reads=['g_RR', 'g_RP'], writes=['g_CC'])
            Pg.op('act', lambda e: e.activation(out=CC[:], in_=CC[:], func=AF.Exp), reads=['g_CC'], writes=['g_CC'])
            Pg.op('dve', lambda e: e.tensor_copy(out=cr[:], in_=RR[:, P - 1:P]), reads=['g_RR', 'g_RP'], writes=['g_cr'])
            Pg.op('dve', lambda e: e.tensor_copy(out=cnb[:], in_=NB[:, P - 1:P]), reads=['g_NB'], writes=['g_cnb'])
            for m in range(32):
                tb = m % 2
                cs = slice(m * 128, (m + 1) * 128)
                for q, QT in enumerate((KW, CC, EN)):
                    Pg.op('pe', lambda e: e.matmul(t_ps[tb][:, q * 8:(q + 1) * 8], lhsT=QT[:, cs], rhs=identf[0:8, 0:8],
                                                   start=True, stop=True),
                          reads=['g_KW', 'g_CC', 'g_EN', 'g_identf'], writes=['g_tps%d' % tb])
                Pg.op('act', lambda e: e.copy(out=T1s[tb][:], in_=t_ps[tb][:]), reads=['g_tps%d' % tb], writes=['g_T1s%d' % tb])
                Pg.op('pe', lambda e: e.matmul(t2_ps[:], lhsT=jrev[:], rhs=T1s[tb][:], start=True, stop=True),
                      reads=['g_T1s%d' % tb, 'g_jrev'], writes=['g_t2ps'])
                Pg.op('dve', lambda e: e.tensor_copy(out=GTF[:, m, :].rearrange('p (q r) -> p q r', q=3),
                                                     in_=T1s[tb][:].rearrange('p (q r) -> p q r', q=3)[:, :, 0:4]),
                      reads=['g_T1s%d' % tb], writes=['g_GTF'])
                Pg.op('dve', lambda e: e.tensor_copy(out=GTB[:, 31 - m, :].rearrange('p (q r) -> p q r', q=3),
                                                     in_=t2_ps[:].rearrange('p (q r) -> p q r', q=3)[:, :, 4:8]),
                      reads=['g_t2ps'], writes=['g_GTB'])
            for r in range(8):
                Pg.op('pe', lambda e: e.matmul(bc_ps[:, r * 32:(r + 1) * 32], lhsT=sel[:, r, :], rhs=AA[:], start=True, stop=True),
                      reads=['g_sel', 'g_AA'], writes=['g_bcps'])
            Pg.op('act', lambda e: e.copy(out=ABCs[:].rearrange('p r n -> p (r n)'), in_=bc_ps[:]), reads=['g_bcps'], writes=['g_ABCs'])
            Pg.dma('sp', seq.GTF[sg * P:(sg + 1) * P, :].rearrange('(n p) c -> p n c', p=128), GTF[:], reads=['g_GTF'])
            Pg.dma('sp', seq.GTB[c0 * 128:(c0 + 32) * 128, :].rearrange('(n p) c -> p n c', p=128), GTB[:], reads=['g_GTB'])
            Pg.dma('sp', seq.ABC[:, :, sg * 32:(sg + 1) * 32], ABCs[:], reads=['g_ABCs'])
    Pg.barrier()


def phase_B(G, layer, seq):
    nc, Pg = G.nc, G.P
    L = seq.L
    NC = L // 128
    NG = L // 512
    with ExitStack() as es:
        cw = sb(nc, es, 'b_cw', [128, 8, 3], F32)
        cb = sb(nc, es, 'b_cb', [128, 8], F32)
        maskf = sb(nc, es, 'b_maskf', [128, 128], F32)
        maskb = sb(nc, es, 'b_maskb', [128, 128], F32)
        abc = sb(nc, es, 'b_abc', [128, 8, 128], F32)
        gbc = sb(nc, es, 'b_gbc', [128, 512], F32)
        win = [sb(nc, es, 'b_win%d' % i, [128, 8, 514], F32) for i in range(2)]
        tmp = [sb(nc, es, 'b_tmp%d' % i, [128, 512], F32) for i in range(2)]
        qk16 = [sb(nc, es, 'b_qk16%d' % i, [128, 8, 512], BF16) for i in range(2)]
        v32 = [sb(nc, es, 'b_v32%d' % i, [128, 512], F32) for i in range(2)]
        vaug = [sb(nc, es, 'b_vaug%d' % i, [128, 4, 129], BF16) for i in range(2)]
        gt = [sb(nc, es, 'b_gt%d' % i, [128, 12], F32) for i in range(2)]
        S32 = sb(nc, es, 'b_S32', [128, 4, 129], F32)
        S16 = sb(nc, es, 'b_S16', [128, 4, 129], BF16)
        kw16 = [sb(nc, es, 'b_kw16%d' % i, [128, 128], BF16) for i in range(2)]
        pT = [sb(nc, es, 'b_pT%d' % i, [128, 128], BF16) for i in range(2)]
        hout = [sb(nc, es, 'b_hout%d' % i, [128, 512], F32) for i in range(2)]
        hf = [sb(nc, es, 'b_hf%d' % i, [128, 512], F32) for i in range(2)]
        o32 = [sb(nc, es, 'b_o32%d' % i, [128, 512], F32) for i in range(2)]
        z32 = [sb(nc, es, 'b_z32%d' % i, [128, 512], F32) for i in range(2)]
        dd = [sb(nc, es, 'b_dd%d' % i, [128, 4], F32) for i in range(2)]
        alloc_hn(G, es, 'bn_')
        kt_ps = [ps(nc, es, 'b_ktps%d' % i, [128, 128], BF16) for i in range(2)]
        sT = [ps(nc, es, 'b_sT%d' % i, [128, 128]) for i in range(2)]
        o_ps = [ps(nc, es, 'b_ops%d' % i, [128, 129]) for i in range(2)]
        kv_ps = [ps(nc, es, 'b_kvps%d' % i, [128, 129]) for i in range(2)]
        for k3 in range(3):
            Pg.dma('sp', cw[:, :, k3], G.w['ml_conv_w'][layer, k3].rearrange('(g p) -> p g', p=128), writes=['b_cw'],
                   allow_slow_non_contiguous=True)
        Pg.dma('sp', cb[:], G.w['ml_conv_b'][layer].rearrange('(g p) -> p g', p=128), writes=['b_cb'],
               allow_slow_non_contiguous=True)
        Pg.dma('sp', maskf[:], G.c['maskf'][:, :], writes=['b_maskf'])
        Pg.dma('sp', maskb[:], G.c['maskb'][:, :], writes=['b_maskb'])
        Pg.dma('sp', abc[:, :, 0:NC], seq.ABC[:, :, 0:NC], writes=['b_abc'])
        Pg.dma('sp', gbc[:], bcast_rows(G.w['ml_norm_g'][layer], 128), writes=['b_gbc'])
        for i in range(2):
            Pg.op('pool', lambda e: e.memset(vaug[i][:], 1.0), writes=['b_vaug%d' % i])
        it = 0
        for sweep in range(2):
            mask = maskf if sweep == 0 else maskb
            Pg.op('pool', lambda e: e.memset(S32[:], 0.0), reads=['b_S16'], writes=['b_S32'])
            gorder = range(NG) if sweep == 0 else range(NG - 1, -1, -1)
            for gi, g in enumerate(gorder):
                wb = gi % 2
                t0 = g * 512
                lo = max(t0 - 1, 0)
                hi = min(t0 + 513, L)
                wlo = lo - (t0 - 1)
                whi = wlo + (hi - lo)
                if wlo > 0:
                    Pg.op('pool', lambda e: e.memset(win[wb][:, :, 0:1], 0.0), writes=['b_win%d' % wb])
                if whi < 514:
                    Pg.op('pool', lambda e: e.memset(win[wb][:, :, 513:514], 0.0), writes=['b_win%d' % wb])
                for q8 in range(8):
                    Pg.dma('sp' if q8 % 2 == 0 else 'act', win[wb][:, q8, wlo:whi], seq.QKT[q8 * 128:(q8 + 1) * 128, lo:hi],
                           writes=[('b_win%d' % wb, q8)], reads=['b_win%d' % wb])
                for q8 in range(8):
                    ce = 'dve'
                    tb = q8 % 2
                    Pg.op(ce, lambda e: e.tensor_scalar(out=tmp[tb][:], in0=win[wb][:, q8, 0:512], scalar1=cw[:, q8, 0:1],
                                                        scalar2=None, op0=ALU.mult),
                          reads=[('b_win%d' % wb, q8), 'b_win%d' % wb, 'b_cw'], writes=['b_tmp%d' % tb])
                    for kk in (1, 2):
                        Pg.op(ce, lambda e: e.scalar_tensor_tensor(out=tmp[tb][:], in0=win[wb][:, q8, kk:kk + 512],
                                                                   scalar=cw[:, q8, kk:kk + 1], in1=tmp[tb][:],
                                                                   op0=ALU.mult, op1=ALU.add),
                              reads=[('b_win%d' % wb, q8), 'b_tmp%d' % tb], writes=['b_tmp%d' % tb])
                    Pg.op('act', lambda e: e.activation(out=qk16[wb][:, q8, :], in_=tmp[tb][:], func=AF.Silu,
                                                        bias=cb[:, q8:q8 + 1], scale=1.0),
                          reads=['b_tmp%d' % tb, 'b_cb'], writes=[('b_qk16%d' % wb, q8)])
                corder = range(4) if sweep == 0 else range(3, -1, -1)
                for cc_ in corder:
                    n = g * 4 + cc_
                    M = n if sweep == 0 else NC - 1 - n
                    cb_ = it % 2
                    it += 1
                    rows = slice(n * 128, (n + 1) * 128)
                    csl = slice(cc_ * 128, (cc_ + 1) * 128)
                    Pg.dma('sp', v32[cb_][:], seq.U[rows, C_MLV:C_MLV + 512], writes=['b_v32%d' % cb_])
                    Pg.dma('sp', gt[cb_][:], (seq.GTF if sweep == 0 else seq.GTB)[rows, :], writes=['b_gt%d' % cb_])
                    if sweep == 1:
                        Pg.dma('act', hf[cb_][:], seq.HF[rows, :], writes=['b_hf%d' % cb_])
                        Pg.dma('act', o32[cb_][:], seq.U[rows, C_MLO:C_MLO + 512], writes=['b_o32%d' % cb_])
                        Pg.dma('act', z32[cb_][:], seq.U[rows, C_MLZ:C_MLZ + 512], writes=['b_z32%d' % cb_])
                    Pg.op('pool', lambda e: e.tensor_copy(out=vaug[cb_][:, :, 0:128],
                                                          in_=v32[cb_][:].rearrange('p (a d) -> p a d', a=4)),
                          reads=['b_v32%d' % cb_], writes=['b_vaug%d' % cb_])
                    for h in range(4):
                        hb = h % 2
                        r = h + 4 * sweep
                        kT = qk16[wb][:, 4 + h, csl]
                        qT = qk16[wb][:, h, csl]
                        kwt = gt[cb_][:, h:h + 1]
                        ccol = gt[cb_][:, 4 + h:5 + h]
                        encol = gt[cb_][:, 8 + h:9 + h]
                        Pg.op('pe', lambda e: e.transpose(kt_ps[hb][:], kT, G.identb[:]),
                              reads=[('b_qk16%d' % wb, 4 + h)], writes=['b_ktps%d' % hb])
                        Pg.op('act', lambda e: e.mul(out=kw16[hb][:], in_=kt_ps[hb][:], mul=kwt),
                              reads=['b_ktps%d' % hb, 'b_gt%d' % cb_], writes=['b_kw16%d' % hb])
                        Pg.op('pe', lambda e: e.matmul(sT[hb][:], lhsT=kT, rhs=qT, start=True, stop=True),
                              reads=[('b_qk16%d' % wb, 4 + h), ('b_qk16%d' % wb, h)], writes=['b_sT%d' % hb])
                        Pg.op('dve', lambda e: e.scalar_tensor_tensor(out=pT[hb][:], in0=sT[hb][:], scalar=kwt, in1=mask[:],
                                                                      op0=ALU.mult, op1=ALU.mult),
                              reads=['b_sT%d' % hb, 'b_gt%d' % cb_, 'b_maskf', 'b_maskb'], writes=['b_pT%d' % hb])
                        Pg.op('dve', lambda e: e.tensor_scalar(out=S32[:, h, :], in0=S32[:, h, :], scalar1=abc[:, r, M:M + 1],
                                                               scalar2=None, op0=ALU.mult),
                              reads=[('b_S32', h), 'b_S32', 'b_abc'], writes=[('b_S32', h)])
                        Pg.op('act', lambda e: e.mul(out=S16[:, h, :], in_=S32[:, h, :], mul=S_ML),
                              reads=[('b_S32', h)], writes=[('b_S16', h)])
                        Pg.op('pe', lambda e: e.matmul(o_ps[hb][:], lhsT=pT[hb][:], rhs=vaug[cb_][:, h, :], start=True, stop=False),
                              reads=['b_pT%d' % hb, 'b_vaug%d' % cb_], writes=['b_ops%d' % hb])
                        Pg.op('pe', lambda e: e.matmul(o_ps[hb][:], lhsT=qT, rhs=S16[:, h, :], start=False, stop=True),
                              reads=[('b_qk16%d' % wb, h), ('b_S16', h)], writes=['b_ops%d' % hb])
                        Pg.op('pe', lambda e: e.matmul(kv_ps[hb][:], lhsT=kw16[hb][:], rhs=vaug[cb_][:, h, :], start=True, stop=True),
                              reads=['b_kw16%d' % hb, 'b_vaug%d' % cb_], writes=['b_kvps%d' % hb])
                        Pg.op('dve', lambda e: e.tensor_tensor(out=S32[:, h, :], in0=S32[:, h, :], in1=kv_ps[hb][:], op=ALU.add),
                              reads=[('b_S32', h), 'b_kvps%d' % hb], writes=[('b_S32', h)])
                        d = dd[hb]
                        dk = 'b_dd%d' % hb
                        Pg.op('act', lambda e: e.activation(out=d[:, 0:1], in_=o_ps[hb][:, 128:129], func=AF.Abs, scale=ccol),
                              reads=['b_ops%d' % hb, 'b_gt%d' % cb_], writes=[dk])
                        Pg.op('dve', lambda e: e.tensor_tensor(out=d[:, 1:2], in0=d[:, 0:1], in1=encol, op=ALU.max),
                              reads=[dk, 'b_gt%d' % cb_], writes=[dk])
                        Pg.op('dve', lambda e: e.reciprocal(out=d[:, 2:3], in_=d[:, 1:2]), reads=[dk], writes=[dk])
                        Pg.op('dve', lambda e: e.tensor_tensor(out=d[:, 3:4], in0=d[:, 2:3], in1=ccol, op=ALU.mult),
                              reads=[dk, 'b_gt%d' % cb_], writes=[dk])
                        Pg.op('act', lambda e: e.mul(out=hout[cb_][:, h * 128:(h + 1) * 128], in_=o_ps[hb][:, 0:128], mul=d[:, 3:4]),
                              reads=['b_ops%d' % hb, dk], writes=[('b_hout%d' % cb_, h)])
                    hk = [('b_hout%d' % cb_, h) for h in range(4)]
                    if sweep == 0:
                        Pg.dma('pool', seq.HF[rows, :], hout[cb_][:], reads=hk)
                    else:
                        Pg.op('dve', lambda e: e.tensor_tensor(out=hout[cb_][:], in0=hout[cb_][:], in1=hf[cb_][:], op=ALU.add),
                              reads=hk + ['b_hf%d' % cb_], writes=['b_hsum%d' % cb_])
                        Pg.op('act', lambda e: e.activation(out=o32[cb_][:], in_=o32[cb_][:], func=AF.Sigmoid),
                              reads=['b_o32%d' % cb_], writes=['b_o32%d' % cb_])
                        Pg.op('dve', lambda e: e.tensor_tensor(out=hout[cb_][:], in0=hout[cb_][:], in1=o32[cb_][:], op=ALU.mult),
                              reads=['b_hsum%d' % cb_, 'b_o32%d' % cb_], writes=['b_hsum%d' % cb_])
                        headnorm_gate(G, 'bn_', hout[cb_], ['b_hsum%d' % cb_], z32[cb_][:], 'b_z32%d' % cb_, gbc, 'b_gbc',
                                      seq.MIX[rows, 0:512], 'pool')
            Pg.barrier()
    Pg.barrier()


def phase_R(G, layer, seq):
    nc, Pg = G.nc, G.P
    L = seq.L
    NC = L // 128
    g128f = [math.exp(128.0 * x) for x in RT_LGF]
    g128b = [math.exp(128.0 * x) for x in RT_LGB]
    with ExitStack() as es:
        dct = sb(nc, es, 'r_dct', [128, 4, 128], F32)
        qwf = sb(nc, es, 'r_qwf', [128, 4, 128], F32)
        qwb = sb(nc, es, 'r_qwb', [128, 4, 128], F32)
        kwf = sb(nc, es, 'r_kwf', [128, 4], F32)
        kwb = sb(nc, es, 'r_kwb', [128, 4], F32)
        gbc = sb(nc, es, 'r_gbc', [128, 512], F32)
        qkv = [sb(nc, es, 'r_qkv%d' % i, [128, 1536], F32) for i in range(2)]
        rot = [sb(nc, es, 'r_rot%d' % i, [128, 128], F32) for i in range(2)]
        ta = [sb(nc, es, 'r_ta%d' % i, [128, 256], F32) for i in range(2)]
        tb = [sb(nc, es, 'r_tb%d' % i, [128, 256], F32) for i in range(2)]
        kr32 = [sb(nc, es, 'r_kr32%d' % i, [128, 512], F32) for i in range(2)]
        q16 = [sb(nc, es, 'r_q16%d' % i, [128, 512], BF16) for i in range(2)]
        k16 = [sb(nc, es, 'r_k16%d' % i, [128, 512], BF16) for i in range(2)]
        v16 = [sb(nc, es, 'r_v16%d' % i, [128, 512], BF16) for i in range(2)]
        qT = [sb(nc, es, 'r_qT%d' % i, [128, 4, 128], BF16) for i in range(2)]
        kT = [sb(nc, es, 'r_kT%d' % i, [128, 4, 128], BF16) for i in range(2)]
        pT = [sb(nc, es, 'r_pT%d' % i, [128, 128], BF16) for i in range(2)]
        qw16 = [sb(nc, es, 'r_qw16%d' % i, [128, 128], BF16) for i in range(2)]
        kw16 = [sb(nc, es, 'r_kw16%d' % i, [128, 128], BF16) for i in range(2)]
        R32 = sb(nc, es, 'r_R32', [128, 4, 128], F32)
        R16 = sb(nc, es, 'r_R16', [128, 4, 128], BF16)
        outt = [sb(nc, es, 'r_out%d' % i, [128, 512], F32) for i in range(2)]
        rf = [sb(nc, es, 'r_rf%d' % i, [128, 512], F32) for i in range(2)]
        z32 = [sb(nc, es, 'r_z32%d' % i, [128, 512], F32) for i in range(2)]
        alloc_hn(G, es, 'rn_')
        qt_ps = ps(nc, es, 'r_qtps', [128, 512], BF16)
        kt_ps = ps(nc, es, 'r_ktps', [128, 512], BF16)
        sT = [ps(nc, es, 'r_sT%d' % i, [128, 128]) for i in range(2)]
        o_ps = [ps(nc, es, 'r_ops%d' % i, [128, 512]) for i in range(2)]
        kv_ps = [ps(nc, es, 'r_kvps%d' % i, [128, 128]) for i in range(2)]
        for nm, t_ in (('dct', dct), ('qwf', qwf), ('qwb', qwb)):
            Pg.dma('sp', t_[:], G.c[nm][:, :, :], writes=['r_' + nm])
        Pg.dma('sp', kwf[:], G.c['kwf'][:, :], writes=['r_kwf'])
        Pg.dma('sp', kwb[:], G.c['kwb'][:, :], writes=['r_kwb'])
        Pg.dma('sp', gbc[:], bcast_rows(G.w['rt_norm_g'][layer], 128), writes=['r_gbc'])
        it = 0
        for sweep in range(2):
            Pg.op('pool', lambda e: e.memset(R32[:], 0.0), writes=['r_R32'])
            Pg.op('pool', lambda e: e.memset(R16[:], 0.0), reads=['r_R16'], writes=[('r_R16', h) for h in range(4)] + ['r_R16'])
            order = range(NC) if sweep == 0 else range(NC - 1, -1, -1)
            for n in order:
                b = it % 2
                it += 1
                rows = slice(n * 128, (n + 1) * 128)
                Pg.dma('sp', qkv[b][:], seq.U[rows, C_RQ:C_RQ + 1536], writes=['r_qkv%d' % b])
                Pg.dma('act', rot[b][:], G.c['rot'][rows, :], writes=['r_rot%d' % b])
                if sweep == 1:
                    Pg.dma('act', rf[b][:], seq.HF[rows, :], writes=['r_rf%d' % b])
                    Pg.dma('act', z32[b][:], seq.U[rows, C_RZ:C_RZ + 512], writes=['r_z32%d' % b])
                cosb = rot[b][:, 0:64].unsqueeze(1).to_broadcast([128, 4, 64])
                sinb = rot[b][:, 64:128].unsqueeze(1).to_broadcast([128, 4, 64])
                for qi, (eng, src0, dst, dkey) in enumerate((('dve', 0, q16[b], 'r_q16%d' % b), ('pool', 512, kr32[b], 'r_kr32%d' % b))):
                    xv = qkv[b][:, src0:src0 + 512].rearrange('p (a t d) -> p a t d', a=4, t=2)
                    dv = dst[:].rearrange('p (a t d) -> p a t d', a=4, t=2)
                    A_ = ta[qi][:].rearrange('p (a d) -> p a d', a=4)
                    B_ = tb[qi][:].rearrange('p (a d) -> p a d', a=4)
                    ka, kb = 'r_ta%d' % qi, 'r_tb%d' % qi
                    rk = ['r_qkv%d' % b, 'r_rot%d' % b]
                    Pg.op(eng, lambda e: e.tensor_tensor(out=A_, in0=xv[:, :, 0, :], in1=cosb, op=ALU.mult), reads=rk, writes=[ka])
                    Pg.op(eng, lambda e: e.tensor_tensor(out=B_, in0=xv[:, :, 1, :], in1=sinb, op=ALU.mult), reads=rk, writes=[kb])
                    Pg.op(eng, lambda e: e.tensor_tensor(out=dv[:, :, 0, :], in0=A_, in1=B_, op=ALU.subtract),
                          reads=[ka, kb], writes=[dkey])
                    Pg.op(eng, lambda e: e.tensor_tensor(out=A_, in0=xv[:, :, 0, :], in1=sinb, op=ALU.mult), reads=rk + [dkey], writes=[ka])
                    Pg.op(eng, lambda e: e.tensor_tensor(out=B_, in0=xv[:, :, 1, :], in1=cosb, op=ALU.mult), reads=rk + [dkey], writes=[kb])
                    Pg.op(eng, lambda e: e.tensor_tensor(out=dv[:, :, 1, :], in0=A_, in1=B_, op=ALU.add),
                          reads=[ka, kb], writes=[dkey])
                Pg.op('act', lambda e: e.copy(out=k16[b][:], in_=kr32[b][:]), reads=['r_kr32%d' % b], writes=['r_k16%d' % b])
                Pg.op('act', lambda e: e.copy(out=v16[b][:], in_=qkv[b][:, 1024:1536]), reads=['r_qkv%d' % b], writes=['r_v16%d' % b])
                for h in range(4):
                    Pg.op('pe', lambda e: e.transpose(qt_ps[:, h * 128:(h + 1) * 128], q16[b][:, h * 128:(h + 1) * 128], G.identb[:]),
                          reads=['r_q16%d' % b], writes=['r_qtps'])
                    Pg.op('pe', lambda e: e.transpose(kt_ps[:, h * 128:(h + 1) * 128], k16[b][:, h * 128:(h + 1) * 128], G.identb[:]),
                          reads=['r_k16%d' % b], writes=['r_ktps'])
                Pg.op('act', lambda e: e.copy(out=qT[b][:].rearrange('p a t -> p (a t)'), in_=qt_ps[:]), reads=['r_qtps'], writes=['r_qT%d' % b])
                Pg.op('dve', lambda e: e.tensor_copy(out=kT[b][:].rearrange('p a t -> p (a t)'), in_=kt_ps[:]), reads=['r_ktps'], writes=['r_kT%d' % b])
                ob = o_ps[b]
                for h in range(4):
                    hb = h % 2
                    hs = slice(h * 128, (h + 1) * 128)
                    qw = qwf if sweep == 0 else qwb
                    kwc = (kwf if sweep == 0 else kwb)[:, h:h + 1]
                    g128 = (g128f if sweep == 0 else g128b)[h]
                    Pg.op('pool', lambda e: e.tensor_tensor(out=qw16[hb][:], in0=qT[b][:, h, :], in1=qw[:, h, :], op=ALU.mult),
                          reads=['r_qT%d' % b, 'r_qwf', 'r_qwb'], writes=['r_qw16%d' % hb])
                    if sweep == 0:
                        Pg.op('pe', lambda e: e.matmul(sT[hb][:], lhsT=kT[b][:, h, :], rhs=qT[b][:, h, :], start=True, stop=True),
                              reads=['r_kT%d' % b, 'r_qT%d' % b], writes=['r_sT%d' % hb])
                        Pg.op('dve', lambda e: e.tensor_tensor(out=pT[hb][:], in0=sT[hb][:], in1=dct[:, h, :], op=ALU.mult),
                              reads=['r_sT%d' % hb, 'r_dct'], writes=['r_pT%d' % hb])
                        Pg.op('pe', lambda e: e.matmul(ob[:, hs], lhsT=pT[hb][:], rhs=v16[b][:, hs], start=True, stop=False),
                              reads=['r_pT%d' % hb, 'r_v16%d' % b], writes=['r_ops%d' % b])
                    Pg.op('pe', lambda e: e.matmul(ob[:, hs], lhsT=qw16[hb][:], rhs=R16[:, h, :], start=(sweep == 1), stop=True),
                          reads=['r_qw16%d' % hb, ('r_R16', h)], writes=['r_ops%d' % b])
                    Pg.op('act', lambda e: e.mul(out=kw16[hb][:], in_=kr32[b][:, hs], mul=kwc),
                          reads=['r_kr32%d' % b, 'r_kwf', 'r_kwb'], writes=['r_kw16%d' % hb])
                    Pg.op('pe', lambda e: e.matmul(kv_ps[hb][:], lhsT=kw16[hb][:], rhs=v16[b][:, hs], start=True, stop=True),
                          reads=['r_kw16%d' % hb, 'r_v16%d' % b], writes=['r_kvps%d' % hb])
                    Pg.op('dve', lambda e: e.scalar_tensor_tensor(out=R32[:, h, :], in0=R32[:, h, :], scalar=g128, in1=kv_ps[hb][:],
                                                                  op0=ALU.mult, op1=ALU.add),
                          reads=['r_R32', ('r_R32', h), 'r_kvps%d' % hb], writes=[('r_R32', h)])
                    Pg.op('act', lambda e: e.copy(out=R16[:, h, :], in_=R32[:, h, :]), reads=[('r_R32', h)], writes=[('r_R16', h)])
                if sweep == 0:
                    Pg.op('act', lambda e: e.copy(out=outt[b][:], in_=ob[:]), reads=['r_ops%d' % b], writes=['r_out%d' % b])
                    Pg.dma('pool', seq.HF[rows, :], outt[b][:], reads=['r_out%d' % b])
                else:
                    Pg.op('dve', lambda e: e.tensor_tensor(out=outt[b][:], in0=ob[:], in1=rf[b][:], op=ALU.add),
                          reads=['r_ops%d' % b, 'r_rf%d' % b], writes=['r_out%d' % b])
                    headnorm_gate(G, 'rn_', outt[b], ['r_out%d' % b], z32[b][:], 'r_z32%d' % b, gbc, 'r_gbc',
                                  seq.MIX[rows, 512:1024], 'pool')
            Pg.barrier()
    Pg.barrier()


def conv3_tile(G, pre, seq, col0, stream, t, cwb, cbb, um, u0, up, acc, q):
    Pg = G.P
    L = seq.L
    r0 = t * 128
    ks = [pre + 'um', pre + 'u0', pre + 'up']
    if r0 == 0:
        Pg.op('pool', lambda e: e.memset(um[:], 0.0), writes=[ks[0]])
        Pg.dma(q, um[1:128, :], seq.U[0:127, col0:col0 + 512], reads=[ks[0]], writes=[ks[0] + 'd'])
    else:
        Pg.dma(q, um[:], seq.U[r0 - 1:r0 + 127, col0:col0 + 512], writes=[ks[0], ks[0] + 'd'])
    Pg.dma(q, u0[:], seq.U[r0:r0 + 128, col0:col0 + 512], writes=[ks[1]])
    if r0 + 128 == L:
        Pg.op('pool', lambda e: e.memset(up[:], 0.0), writes=[ks[2]])
        Pg.dma(q, up[0:127, :], seq.U[r0 + 1:L, col0:col0 + 512], reads=[ks[2]], writes=[ks[2] + 'd'])
    else:
        Pg.dma(q, up[:], seq.U[r0 + 1:r0 + 129, col0:col0 + 512], writes=[ks[2], ks[2] + 'd'])
    ak = pre + 'acc'
    Pg.op('dve', lambda e: e.tensor_tensor(out=acc[:], in0=um[:], in1=cwb[:, stream, 0, :], op=ALU.mult),
          reads=[ks[0], ks[0] + 'd', 'h_cwb'], writes=[ak])
    Pg.op('pool', lambda e: e.tensor_tensor(out=u0[:], in0=u0[:], in1=cwb[:, stream, 1, :], op=ALU.mult),
          reads=[ks[1], 'h_cwb'], writes=[ks[1]])
    Pg.op('pool', lambda e: e.tensor_tensor(out=up[:], in0=up[:], in1=cwb[:, stream, 2, :], op=ALU.mult),
          reads=[ks[2], ks[2] + 'd', 'h_cwb'], writes=[ks[2]])
    Pg.op('dve', lambda e: e.tensor_tensor(out=acc[:], in0=acc[:], in1=u0[:], op=ALU.add), reads=[ak, ks[1]], writes=[ak])
    Pg.op('dve', lambda e: e.tensor_tensor(out=acc[:], in0=acc[:], in1=up[:], op=ALU.add), reads=[ak, ks[2]], writes=[ak])
    Pg.op('dve', lambda e: e.tensor_tensor(out=acc[:], in0=acc[:], in1=cbb[:, stream, :], op=ALU.add), reads=[ak, 'h_cbb'], writes=[ak])


def phase_H(G, layer, seq):
    nc, Pg = G.nc, G.P
    L, nb = seq.L, seq.nb
    npc = 2 * nb - 1
    feat = G.c['feat_' + seq.name]
    ntn = G.c['ntn_' + seq.name]
    HPI = math.pi / 2
    with ExitStack() as es0:
        cwb = sb(nc, es0, 'h_cwb', [128, 3, 3, 512], F32)
        cbb = sb(nc, es0, 'h_cbb', [128, 3, 512], F32)
        skb = sb(nc, es0, 'h_skb', [128, 2, 512], F32)
        RNt = sb(nc, es0, 'h_RNt', [128, 512], F32)
        for st in range(3):
            for k3 in range(3):
                Pg.dma('sp', cwb[:, st, k3, :], bcast_rows(G.w['hy_conv_w'][layer, k3, st * 512:(st + 1) * 512], 128), writes=['h_cwb'])
            Pg.dma('sp', cbb[:, st, :], bcast_rows(G.w['hy_conv_b'][layer, st * 512:(st + 1) * 512], 128), writes=['h_cbb'])
        for o in range(2):
            Pg.dma('sp', skb[:, o, :], bcast_rows(G.w['hy_skip'][layer, o], 128), writes=['h_skb'])
        Pg.barrier()
        for o in range(2):
            with ExitStack() as es:
                w1 = sb(nc, es, 's_w1', [33, 64], F32)
                w2 = sb(nc, es, 's_w2', [64, 64], F32)
                w3 = sb(nc, es, 's_w3', [64, 2, 512], F32)
                fr = sb(nc, es, 's_fr', [64, 1], F32)
                fb1 = sb(nc, es, 's_fb1', [64, 1], F32)
                fb2 = sb(nc, es, 's_fb2', [64, 1], F32)
                absd = sb(nc, es, 's_absd', [128, 2, 512], F32)
                ones = sb(nc, es, 's_ones', [128, 128], F32)
                ftT = [sb(nc, es, 's_ftT%d' % i, [33, 512], F32) for i in range(2)]
                ntt = sb(nc, es, 's_ntt', [128, 32], F32)
                zt = sb(nc, es, 's_zt', [64, 512], F32)
                ct = sb(nc, es, 's_ct', [64, 512], F32)
                hid1 = sb(nc, es, 's_hid1', [64, 512], F32)
                hid2 = sb(nc, es, 's_hid2', [64, 512], F32)
                Et = [sb(nc, es, 's_E%d' % i, [128, 512], F32) for i in range(2)]
                g32 = [sb(nc, es, 's_g32%d' % i, [128, 512], F32) for i in range(2)]
                gab = [sb(nc, es, 's_gab%d' % i, [128, 512], F32) for i in range(2)]
                GT16 = sb(nc, es, 's_GT16', [128, 64, 512], BF16)
                FT = [sb(nc, es, 's_FT%d' % i, [128, 64, 2, 128], BF16) for i in range(2)]
                go16 = [sb(nc, es, 's_go16%d' % i, [128, 2, 512], BF16) for i in range(2)]
                m_ps = ps(nc, es, 's_mps', [64, 512])
                f_ps = [ps(nc, es, 's_fps%d' % i, [128, 512]) for i in range(2)]
                b0_ps = ps(nc, es, 's_b0ps', [1, 512])
                n_ps = ps(nc, es, 's_nps', [128, 512])
                x_ps = [ps(nc, es, 's_xps%d' % i, [128, 512]) for i in range(2)]
                Pg.dma('sp', w1[:], G.w['hy_w1'][layer], writes=['s_w1'])
                Pg.dma('sp', w2[:], G.w['hy_w2'][layer], writes=['s_w2'])
                Pg.dma('sp', w3[:], G.w['hy_w3'][layer, :, o * 1024:(o + 1) * 1024].rearrange('k (d c) -> k d c', d=2), writes=['s_w3'])
                Pg.dma('sp', fr[:], G.w['hy_freq'][layer].unsqueeze(1), writes=['s_fr'])
                Pg.dma('sp', fb1[:], G.w['hy_b1'][layer].unsqueeze(1), writes=['s_fb1'])
                Pg.dma('sp', fb2[:], G.w['hy_b2'][layer].unsqueeze(1), writes=['s_fb2'])
                for dr in range(2):
                    Pg.dma('sp', absd[:, dr, :], bcast_rows(G.w['hy_deltas'][layer, o, dr], 128), writes=['s_absd'])
                Pg.op('act', lambda e: e.activation(out=absd[:].rearrange('p a c -> p (a c)'), in_=absd[:].rearrange('p a c -> p (a c)'),
                                                    func=AF.Abs), reads=['s_absd'], writes=['s_absd'])
                Pg.op('dve', lambda e: e.tensor_tensor(out=fb1[:], in0=fb1[:], in1=fr[:], op=ALU.mult), reads=['s_fb1', 's_fr'], writes=['s_fb1'])
                Pg.op('dve', lambda e: e.tensor_tensor(out=fb2[:], in0=fb2[:], in1=fr[:], op=ALU.mult), reads=['s_fb2', 's_fr'], writes=['s_fb2'])
                Pg.op('pool', lambda e: e.memset(ones[:], 1.0), writes=['s_ones'])
                norm_tiles = [(pi, hf) for pi in range(npc) for hf in range(2) if hf == 0 or pi == 0]
                nleft = len(norm_tiles) * 32
                ncount = 0
                ti = 0
                xi_ = 0
                for pi in range(npc):
                    d = pi - (nb - 1)
                    for hf in range(2):
                        dr = 0 if ((hf == 0 and d >= 0) or (hf == 1 and d >= 1)) else 1
                        Pg.dma('act', ntt[:], ntn[pi, hf], writes=['s_ntt'])
                        for grp in range(8):
                            fb_ = grp % 2
                            Pg.dma('act', ftT[fb_][:], feat[pi, hf, :, grp * 512:(grp + 1) * 512], writes=['s_ftT%d' % fb_])
                            src_keys = ['s_ftT%d' % fb_]
                            rhs_ = ftT[fb_]
                            for (wm, fbm, hid, wk, hk) in ((w1, fb1, hid1, 's_w1', 's_hid1'), (w2, fb2, hid2, 's_w2', 's_hid2')):
                                Pg.op('pe', lambda e: e.matmul(m_ps[:], lhsT=wm[:], rhs=rhs_[:], start=True, stop=True),
                                      reads=src_keys + [wk], writes=['s_mps'])
                                Pg.op('dve', lambda e: e.tensor_scalar(out=zt[:], in0=m_ps[:], scalar1=fr[:, 0:1], scalar2=fbm[:, 0:1],
                                                                       op0=ALU.mult, op1=ALU.add),
                                      reads=['s_mps', 's_fr', 's_fb1', 's_fb2'], writes=['s_zt'])
                                for _ in range(2):
                                    Pg.op('dve', lambda e: e.tensor_scalar(out=ct[:], in0=zt[:], scalar1=-HPI, scalar2=HPI,
                                                                           op0=ALU.max, op1=ALU.min), reads=['s_zt'], writes=['s_ct'])
                                    Pg.op('dve', lambda e: e.scalar_tensor_tensor(out=zt[:], in0=ct[:], scalar=2.0, in1=zt[:],
                                                                                  op0=ALU.mult, op1=ALU.subtract),
                                          reads=['s_ct', 's_zt'], writes=['s_zt'])
                                Pg.op('act', lambda e: e.activation(out=hid[:], in_=zt[:], func=AF.Sin), reads=['s_zt'], writes=[hk])
                                src_keys = [hk]
                                rhs_ = hid
                            for tt in range(4):
                                tile_i = grp * 4 + tt
                                b2 = ti % 2
                                ti += 1
                                Pg.op('pe', lambda e: e.matmul(f_ps[b2][:], lhsT=hid2[:, tt * 128:(tt + 1) * 128], rhs=w3[:, dr, :],
                                                               start=True, stop=True), reads=['s_hid2', 's_w3'], writes=['s_fps%d' % b2])
                                Pg.op('act', lambda e: e.activation(out=Et[b2][:], in_=absd[:, dr, :], func=AF.Exp,
                                                                    scale=ntt[:, tile_i:tile_i + 1]),
                                      reads=['s_absd', 's_ntt'], writes=['s_E%d' % b2])
                                Pg.op('dve', lambda e: e.tensor_tensor(out=g32[b2][:], in0=f_ps[b2][:], in1=Et[b2][:], op=ALU.mult),
                                      reads=['s_fps%d' % b2, 's_E%d' % b2], writes=['s_g32%d' % b2])
                                if d == 0 and hf == 0 and tile_i == 0:
                                    Pg.op('pe', lambda e: e.matmul(b0_ps[:], lhsT=hid2[:, 0:1], rhs=w3[:, 1, :], start=True, stop=True),
                                          reads=['s_hid2', 's_w3'], writes=['s_b0ps'])
                                    Pg.op('dve', lambda e: e.tensor_tensor(out=g32[b2][0:1, :], in0=g32[b2][0:1, :], in1=b0_ps[:], op=ALU.add),
                                          reads=['s_g32%d' % b2, 's_b0ps'], writes=['s_g32%d' % b2])
                                if hf == 1 and tile_i == 0:
                                    Pg.op('dve', lambda e: e.memset(g32[b2][0:1, :], 0.0), reads=['s_g32%d' % b2], writes=['s_g32%d' % b2])
                                if (pi, hf) in norm_tiles:
                                    Pg.op('act', lambda e: e.activation(out=gab[b2][:], in_=g32[b2][:], func=AF.Abs),
                                          reads=['s_g32%d' % b2], writes=['s_gab%d' % b2])
                                    Pg.op('pe', lambda e: e.matmul(n_ps[:], lhsT=ones[:], rhs=gab[b2][:], start=(ncount == 0),
                                                                   stop=(ncount == nleft - 1)), reads=['s_gab%d' % b2, 's_ones'], writes=['s_nps'])
                                    ncount += 1
                                Pg.op('pool', lambda e: e.tensor_copy(out=GT16[:, hf * 32 + tile_i, :], in_=g32[b2][:]),
                                      reads=['s_g32%d' % b2], writes=[('s_GT16', hf * 32 + tile_i)])
                    gkeys = [('s_GT16', i) for i in range(64)]
                    for kc in range(5):
                        fb_ = xi_ % 2
                        xi_ += 1
                        Pg.dma('sp' if kc % 2 == 0 else 'pool', FT[fb_][:], G.c['fwd_own'][kc], writes=['s_FT%d' % fb_])
                        for ri in range(2):
                            for sc in range(64):
                                Pg.op('pe', lambda e: e.matmul(x_ps[ri][:], lhsT=FT[fb_][:, sc, ri, :], rhs=GT16[:, sc, :],
                                                               start=(sc == 0), stop=(sc == 63)),
                                      reads=['s_FT%d' % fb_] + (gkeys if sc in (0, 63) else []), writes=['s_xps%d' % ri])
                        Pg.op('act', lambda e: e.copy(out=go16[fb_][:, 0, :], in_=x_ps[0][:]), reads=['s_xps0'], writes=[('s_go%d' % fb_, 0)])
                        Pg.op('dve', lambda e: e.tensor_copy(out=go16[fb_][:, 1, :], in_=x_ps[1][:]), reads=['s_xps1'], writes=[('s_go%d' % fb_, 1)])
                        if nb == 1:
                            gdst = G.GSPa[kc * 128:(kc + 1) * 128, :].rearrange('p (r c) -> p r c', r=2)
                        else:
                            gdst = seq.GS[pi, kc]
                        Pg.dma('act', gdst, go16[fb_][:], reads=[('s_go%d' % fb_, 0), ('s_go%d' % fb_, 1)])
                Pg.op('dve', lambda e: e.reciprocal(out=RNt[:], in_=n_ps[:]), reads=['s_nps'], writes=['h_RNt'])
            if nb == 1:
                Pg.allgather(G.GSPa, G.GSAa, G.ccdummy)
            Pg.barrier()
            with ExitStack() as es:
                um = sb(nc, es, 'f_um', [128, 512], F32)
                u0 = sb(nc, es, 'f_u0', [128, 512], F32)
                up = sb(nc, es, 'f_up', [128, 512], F32)
                acc = sb(nc, es, 'f_acc', [128, 512], F32)
                v16 = sb(nc, es, 'f_v16', [128, 32, 512], BF16)
                FT = [sb(nc, es, 'f_FT%d' % i, [128, 32, 2, 128], BF16) for i in range(2)]
                xo16 = [sb(nc, es, 'f_xo16%d' % i, [128, 2, 512], BF16) for i in range(2)]
                x_ps = [ps(nc, es, 'f_xps%d' % i, [128, 512]) for i in range(2)]
                xi_ = 0
                for b in range(nb):
                    for t in range(32):
                        tg = b * 32 + t
                        if o == 0:
                            conv3_tile(G, 'f_', seq, C_HV, 0, tg, cwb, cbb, um, u0, up, acc, 'sp')
                        else:
                            Pg.dma('sp', acc[:], seq.Z1[tg * 128:(tg + 1) * 128, :], writes=['f_acc'])
                        Pg.op('act', lambda e: e.copy(out=v16[:, t, :], in_=acc[:]), reads=['f_acc'], writes=[('f_v16', t)])
                    vkeys = [('f_v16', t) for t in range(32)]
                    for kc in range(NKC if nb == 1 else 5):
                        fb_ = xi_ % 2
                        xi_ += 1
                        Pg.dma('sp' if kc % 2 == 0 else 'pool', FT[fb_][:], (G.c['fwd'] if nb == 1 else G.c['fwd_own'])[kc, :, 0:32],
                               writes=['f_FT%d' % fb_])
                        for ri in range(2):
                            for sc in range(32):
                                Pg.op('pe', lambda e: e.matmul(x_ps[ri][:], lhsT=FT[fb_][:, sc, ri, :], rhs=v16[:, sc, :],
                                                               start=(sc == 0), stop=(sc == 31)),
                                      reads=['f_FT%d' % fb_] + (vkeys if sc in (0, 31) else []), writes=['f_xps%d' % ri])
                        Pg.op('act', lambda e: e.copy(out=xo16[fb_][:, 0, :], in_=x_ps[0][:]), reads=['f_xps0'], writes=[('f_xo%d' % fb_, 0)])
                        Pg.op('dve', lambda e: e.tensor_copy(out=xo16[fb_][:, 1, :], in_=x_ps[1][:]), reads=['f_xps1'], writes=[('f_xo%d' % fb_, 1)])
                        Pg.dma('act', seq.XS[b, kc], xo16[fb_][:], reads=[('f_xo%d' % fb_, 0), ('f_xo%d' % fb_, 1)])
            Pg.barrier()
            with ExitStack() as es:
                Y16 = sb(nc, es, 'i_Y16', [128, NKC, 2, 512], BF16)
                Xt = [sb(nc, es, 'i_X%d' % i, [128, 2, 512], BF16) for i in range(2)]
                Gt = [sb(nc, es, 'i_G%d' % i, [128, 2, 512], BF16) for i in range(2)]
                Yr = sb(nc, es, 'i_Yr', [128, 512], F32)
                Yi = sb(nc, es, 'i_Yi', [128, 512], F32)
                t1 = sb(nc, es, 'i_t1', [128, 512], F32)
                t2 = sb(nc, es, 'i_t2', [128, 512], F32)
                t3 = sb(nc, es, 'i_t3', [128, 512], F32)
                t4 = sb(nc, es, 'i_t4', [128, 512], F32)
                IT = [sb(nc, es, 'i_IT%d' % i, [128, NKC, 2, 128], BF16) for i in range(2)]
                um = sb(nc, es, 'i_um', [128, 512], F32)
                u0 = sb(nc, es, 'i_u0', [128, 512], F32)
                up = sb(nc, es, 'i_up', [128, 512], F32)
                vacc = sb(nc, es, 'i_vacc', [128, 512], F32)
                xacc = sb(nc, es, 'i_xacc', [128, 512], F32)
                yt = sb(nc, es, 'i_yt', [128, 512], F32)
                zt_ = sb(nc, es, 'i_zt', [128, 512], F32)
                hz = sb(nc, es, 'i_hz', [128, 512], F32)
                y_ps = [ps(nc, es, 'i_yps%d' % i, [128, 512]) for i in range(2)]
                pi_ = 0
                if nb > 1:
                    Yp = [sb(nc, es, 'i_Yp%d' % i, [128, 2, 512], BF16) for i in range(2)]
                    for a in range(nb):
                        for kc in range(5):
                            for b in range(nb):
                                xb_ = pi_ % 2
                                pi_ += 1
                                Pg.dma('sp', Xt[xb_][:], seq.XS[b, kc], writes=['i_X%d' % xb_])
                                Pg.dma('act', Gt[xb_][:], seq.GS[a - b + nb - 1, kc], writes=['i_G%d' % xb_])
                                rk = ['i_X%d' % xb_, 'i_G%d' % xb_]
                                first = b == 0
                                Pg.op('dve', lambda e: e.tensor_tensor(out=(Yr if first else t1)[:], in0=Xt[xb_][:, 0, :], in1=Gt[xb_][:, 0, :], op=ALU.mult),
                                      reads=rk, writes=['i_Yr' if first else 'i_t1'])
                                Pg.op('pool', lambda e: e.tensor_tensor(out=t2[:], in0=Xt[xb_][:, 1, :], in1=Gt[xb_][:, 1, :], op=ALU.mult),
                                      reads=rk, writes=['i_t2'])
                                Pg.op('dve', lambda e: e.tensor_tensor(out=(Yi if first else t3)[:], in0=Xt[xb_][:, 0, :], in1=Gt[xb_][:, 1, :], op=ALU.mult),
                                      reads=rk, writes=['i_Yi' if first else 'i_t3'])
                                Pg.op('pool', lambda e: e.tensor_tensor(out=t4[:], in0=Xt[xb_][:, 1, :], in1=Gt[xb_][:, 0, :], op=ALU.mult),
                                      reads=rk, writes=['i_t4'])
                                if not first:
                                    Pg.op('dve', lambda e: e.tensor_tensor(out=Yr[:], in0=Yr[:], in1=t1[:], op=ALU.add), reads=['i_Yr', 'i_t1'], writes=['i_Yr'])
                                    Pg.op('pool', lambda e: e.tensor_tensor(out=Yi[:], in0=Yi[:], in1=t3[:], op=ALU.add), reads=['i_Yi', 'i_t3'], writes=['i_Yi'])
                                Pg.op('dve', lambda e: e.tensor_tensor(out=Yr[:], in0=Yr[:], in1=t2[:], op=ALU.subtract), reads=['i_Yr', 'i_t2'], writes=['i_Yr'])
                                Pg.op('pool', lambda e: e.tensor_tensor(out=Yi[:], in0=Yi[:], in1=t4[:], op=ALU.add), reads=['i_Yi', 'i_t4'], writes=['i_Yi'])
                            yb_ = (a * 5 + kc) % 2
                            Pg.op('act', lambda e: e.copy(out=Yp[yb_][:, 0, :], in_=Yr[:]), reads=['i_Yr'], writes=[('i_Yp%d' % yb_, 0)])
                            Pg.op('act', lambda e: e.copy(out=Yp[yb_][:, 1, :], in_=Yi[:]), reads=['i_Yi'], writes=[('i_Yp%d' % yb_, 1)])
                            r0_ = (a * 5 + kc) * 128
                            Pg.dma('sp', G.YP[r0_:r0_ + 128, :].rearrange('p (r c) -> p r c', r=2), Yp[yb_][:],
                                   reads=[('i_Yp%d' % yb_, 0), ('i_Yp%d' % yb_, 1)])
                    Pg.allgather(G.YP, G.YA, G.ccdummy)
                for a in range(nb):
                    for kc in range(NKC):
                        if nb > 1:
                            r0_ = ((kc % 8) * 20 + a * 5 + kc // 8) * 128
                            Pg.dma('sp' if kc % 2 == 0 else 'act', Y16[:, kc, :, :],
                                   G.YA[r0_:r0_ + 128, :].rearrange('p (r c) -> p r c', r=2),
                                   writes=[('i_Y16', kc, 0), ('i_Y16', kc, 1)])
                            continue
                        for b in range(nb):
                            xb_ = pi_ % 2
                            pi_ += 1
                            Pg.dma('sp', Xt[xb_][:], seq.XS[b, kc], writes=['i_X%d' % xb_])
                            g0_ = ((kc % 8) * 5 + kc // 8) * 128
                            Pg.dma('act', Gt[xb_][:], G.GSAa[g0_:g0_ + 128, :].rearrange('p (r c) -> p r c', r=2), writes=['i_G%d' % xb_])
                            rk = ['i_X%d' % xb_, 'i_G%d' % xb_]
                            first = b == 0
                            Pg.op('dve', lambda e: e.tensor_tensor(out=(Yr if first else t1)[:], in0=Xt[xb_][:, 0, :], in1=Gt[xb_][:, 0, :], op=ALU.mult),
                                  reads=rk, writes=['i_Yr' if first else 'i_t1'])
                            Pg.op('pool', lambda e: e.tensor_tensor(out=t2[:], in0=Xt[xb_][:, 1, :], in1=Gt[xb_][:, 1, :], op=ALU.mult),
                                  reads=rk, writes=['i_t2'])
                            Pg.op('dve', lambda e: e.tensor_tensor(out=(Yi if first else t3)[:], in0=Xt[xb_][:, 0, :], in1=Gt[xb_][:, 1, :], op=ALU.mult),
                                  reads=rk, writes=['i_Yi' if first else 'i_t3'])
                            Pg.op('pool', lambda e: e.tensor_tensor(out=t4[:], in0=Xt[xb_][:, 1, :], in1=Gt[xb_][:, 0, :], op=ALU.mult),
                                  reads=rk, writes=['i_t4'])
                            if not first:
                                Pg.op('dve', lambda e: e.tensor_tensor(out=Yr[:], in0=Yr[:], in1=t1[:], op=ALU.add), reads=['i_Yr', 'i_t1'], writes=['i_Yr'])
                                Pg.op('pool', lambda e: e.tensor_tensor(out=Yi[:], in0=Yi[:], in1=t3[:], op=ALU.add), reads=['i_Yi', 'i_t3'], writes=['i_Yi'])
                            Pg.op('dve', lambda e: e.tensor_tensor(out=Yr[:], in0=Yr[:], in1=t2[:], op=ALU.subtract), reads=['i_Yr', 'i_t2'], writes=['i_Yr'])
                            Pg.op('pool', lambda e: e.tensor_tensor(out=Yi[:], in0=Yi[:], in1=t4[:], op=ALU.add), reads=['i_Yi', 'i_t4'], writes=['i_Yi'])
                        Pg.op('act', lambda e: e.copy(out=Y16[:, kc, 0, :], in_=Yr[:]), reads=['i_Yr'], writes=[('i_Y16', kc, 0)])
                        Pg.op('act', lambda e: e.copy(out=Y16[:, kc, 1, :], in_=Yi[:]), reads=['i_Yi'], writes=[('i_Y16', kc, 1)])
                    ykeys = [('i_Y16', kc, ri) for kc in range(NKC) for ri in range(2)]
                    for sc in range(32):
                        ib = sc % 2
                        tg = a * 32 + sc
                        rows = slice(tg * 128, (tg + 1) * 128)
                        Pg.dma('sp' if sc % 2 == 0 else 'pool', IT[ib][:], G.c['inv'][sc], writes=['i_IT%d' % ib])
                        n_mm = 0
                        for kc in range(NKC):
                            for ri in range(2):
                                Pg.op('pe', lambda e: e.matmul(y_ps[ib][:], lhsT=IT[ib][:, kc, ri, :], rhs=Y16[:, kc, ri, :],
                                                               start=(n_mm == 0), stop=(n_mm == 2 * NKC - 1)),
                                      reads=['i_IT%d' % ib] + (ykeys if n_mm in (0, 2 * NKC - 1) else []), writes=['i_yps%d' % ib])
                                n_mm += 1
                        if o == 0:
                            conv3_tile(G, 'i_', seq, C_HV, 0, tg, cwb, cbb, um, u0, up, vacc, 'act')
                            vk = 'i_acc'
                            vt = vacc
                        else:
                            Pg.dma('act', vacc[:], seq.Z1[rows, :], writes=['i_vz'])
                            vk = 'i_vz'
                            vt = vacc
                        Pg.op('dve', lambda e: e.tensor_tensor(out=yt[:], in0=y_ps[ib][:], in1=RNt[:], op=ALU.mult),
                              reads=['i_yps%d' % ib, 'h_RNt'], writes=['i_yt'])
                        Pg.op('pool', lambda e: e.tensor_tensor(out=zt_[:], in0=vt[:], in1=skb[:, o, :], op=ALU.mult),
                              reads=[vk, 'h_skb'], writes=['i_zt'])
                        Pg.op('dve', lambda e: e.tensor_tensor(out=yt[:], in0=yt[:], in1=zt_[:], op=ALU.add), reads=['i_yt', 'i_zt'], writes=['i_yt'])
                        Pg.op('pool', lambda e: e.tensor_copy(out=zt_[:, 0:1], in_=zt_[:, 0:1]), reads=['i_yt', vk, 'i_acc', 'i_vz'], writes=['i_zt'])
                        conv3_tile(G, 'i_', seq, C_H1 if o == 0 else C_H2, 1 + o, tg, cwb, cbb, um, u0, up, xacc, 'act')
                        Pg.op('dve', lambda e: e.tensor_tensor(out=yt[:], in0=yt[:], in1=xacc[:], op=ALU.mult), reads=['i_yt', 'i_acc'], writes=['i_yt'])
                        if o == 0:
                            Pg.dma('pool', seq.Z1[rows, :], yt[:], reads=['i_yt'])
                        else:
                            Pg.dma('act', hz[:], seq.U[rows, C_HZ:C_HZ + 512], writes=['i_hz'])
                            Pg.op('act', lambda e: e.activation(out=hz[:], in_=hz[:], func=AF.Silu), reads=['i_hz'], writes=['i_hz'])
                            Pg.op('dve', lambda e: e.tensor_tensor(out=yt[:], in0=yt[:], in1=hz[:], op=ALU.mult), reads=['i_yt', 'i_hz'], writes=['i_yt'])
                            Pg.dma('pool', seq.MIX[rows, 1024:1536], yt[:], reads=['i_yt'])
            Pg.barrier()
    Pg.barrier()


def build(dbg=None, mixers=None):
    dbg = dbg or {}
    nlayers = dbg.get('nlayers', DEPTH)
    nc = bass.Bass("TRN2", target_bir_lowering=False)
    G = Ctx()
    G.nc = nc
    G.dbg = dbg
    G.w = {k: nc.dram_tensor(k, s, F32, kind="ExternalInput").ap() for k, s in WEIGHT_SPECS.items()}
    G.c = {k: nc.dram_tensor('c_' + k, s, dt, kind="ExternalInput").ap() for k, (s, dt) in CONST_SPECS.items()}
    xa = nc.dram_tensor('xa', [LA, D], F32, kind="ExternalInput").ap()
    xb = nc.dram_tensor('xb', [LB, D], F32, kind="ExternalInput").ap()
    ya = nc.dram_tensor('ya', [LA, D], F32, kind="ExternalOutput").ap()
    yb = nc.dram_tensor('yb', [LB, D], F32, kind="ExternalOutput").ap()
    sk = "ExternalOutput" if dbg.get('dump') else "Internal"
    U = USplit(nc.dram_tensor('s_U1', [LB, USPLIT], F32, kind="Internal").ap(),
               nc.dram_tensor('s_U2', [LB, UT - USPLIT], F32, kind="Internal").ap())
    QKT = nc.dram_tensor('s_QKT', [1024, LB], F32, kind=sk).ap()
    MIX = nc.dram_tensor('s_MIX', [LB, 1536], F32, kind=sk).ap()
    Xa = nc.dram_tensor('s_Xa', [LA, D], F32, kind=sk).ap()
    Xb = nc.dram_tensor('s_Xb', [LB, D], F32, kind="Internal").ap()
    GTFd = nc.dram_tensor('s_GTF', [LB, 12], F32, kind=sk).ap()
    GTBd = nc.dram_tensor('s_GTB', [LB, 12], F32, kind=sk).ap()
    ABCd = nc.dram_tensor('s_ABC', [128, 8, 128], F32, kind=sk).ap()
    HFd = nc.dram_tensor('s_HF', [LB, 512], F32, kind="Internal").ap()
    XSd = nc.dram_tensor('s_XS', [4, NKC, 128, 2, 512], BF16, kind="Internal").ap()
    GSd = nc.dram_tensor('s_GS', [7, NKC, 128, 2, 512], BF16, kind="Internal").ap()
    Z1d = nc.dram_tensor('s_Z1', [LB, 512], F32, kind="Internal").ap()
    G.GSPa = nc.dram_tensor('s_GSPa', [5 * 128, 1024], BF16, kind="Internal").ap()
    G.GSAa = nc.dram_tensor('s_GSAa', [8 * 5 * 128, 1024], BF16, kind="Internal").ap()
    G.YP = nc.dram_tensor('s_YP', [20 * 128, 1024], BF16, kind="Internal").ap()
    G.YA = nc.dram_tensor('s_YA', [8 * 20 * 128, 1024], BF16, kind="Internal").ap()
    seqs = []
    for name, L, xin, X, yout in (('a', LA, xa, Xa, ya), ('b', LB, xb, Xb, yb)):
        s = Ctx()
        s.name, s.L, s.nb, s.xin, s.X, s.yout = name, L, L // P, xin, X, yout
        s.U, s.QKT, s.MIX = U, QKT, MIX
        s.GTF, s.GTB, s.ABC, s.HF = GTFd, GTBd, ABCd, HFd
        s.XS, s.GS, s.Z1 = XSd, GSd, Z1d
        seqs.append(s)
    if dbg.get('only_a'):
        seqs = seqs[:1]
    with ExitStack() as es:
        G.mixers = mixers or all_mixers
        G.P = Prog(nc, es)
        G.identb = sb(nc, es, 'identb', [128, 128], BF16)
        G.ccdummy = sb(nc, es, 'ccdummy', [1, 8], F32)
        G.P.dma('sp', G.identb[:], G.c['identb'][:, :], writes=['identb'])
        G.P.barrier()
        for layer in range(nlayers):
            for seq in seqs:
                for blk in range(seq.nb):
                    phase_A(G, layer, seq, blk)
                G.mixers(G, layer, seq)
                if not dbg.get('skip_O'):
                    phase_O(G, layer, seq)
        G.P.barrier()
    return nc, G


def no_mixers(G, layer, seq):
    pass


def all_mixers(G, layer, seq):
    phase_G(G, layer, seq)
    phase_B(G, layer, seq)
    phase_R(G, layer, seq)
    phase_H(G, layer, seq)


def hy_mixers(G, layer, seq):
    phase_H(G, layer, seq)


def rt_mixers(G, layer, seq):
    phase_R(G, layer, seq)


def ml_mixers(G, layer, seq):
    phase_G(G, layer, seq)
    phase_B(G, layer, seq)


def build_with(dbg, mixers):
    return build(dbg, mixers)


_NC = None


def kernel(**inputs):
    global _NC
    if _NC is None:
        _NC = build()[0]
    nc = _NC
    c = host_consts()
    maps = []
    xs = np.ascontiguousarray(np.asarray(inputs['x_sample'], dtype=np.float32)[0])
    for i in range(8):
        m = {k: np.ascontiguousarray(np.asarray(inputs[k], dtype=np.float32)) for k in WEIGHT_SPECS}
        for k in CONST_SPECS:
            if k != 'fwd_own':
                m['c_' + k] = c[k]
        own = np.zeros((5, 128, 64, 2, 128), dtype=c['fwd'].dtype)
        for j in range(5):
            if i + 8 * j < NKC:
                own[j] = c['fwd'][i + 8 * j]
        m['c_fwd_own'] = own
        m['xa'] = np.ascontiguousarray(np.asarray(inputs['x_prompt'], dtype=np.float32)[i])
        m['xb'] = xs
        maps.append(m)
    res = run_bass_kernel_spmd(nc, maps, core_ids=list(range(8)))
    y_prompt = np.stack([np.asarray(r['ya'], dtype=np.float32) for r in res.results], axis=0)
    y_sample = np.asarray(res.results[0]['yb'], dtype=np.float32)[None]
    return (y_prompt, y_sample)
```

```python
import math
from contextlib import ExitStack
import numpy as np
import ml_dtypes
import concourse.bass as bass
import concourse.mybir as mybir
from concourse.bass_utils import run_bass_kernel_spmd

F32 = mybir.dt.float32
BF16 = mybir.dt.bfloat16
ALU = mybir.AluOpType
AF = mybir.ActivationFunctionType
AX = mybir.AxisListType

D = 1024
DEPTH = 4
LA = 4096
LB = 16384
P = 4096
NF = 8192
NKC = 33
INC = 6672
UT = 5648
C_MLV, C_MLO, C_MLZ, C_G, C_RQ, C_RK, C_RV, C_RZ, C_HV, C_H1, C_H2, C_HZ = (
    0, 512, 1024, 1536, 1552, 2064, 2576, 3088, 3600, 4112, 4624, 5136)
RT_LGF = [math.log(1.0 - 2.0 ** (-5.0 - h)) for h in range(4)]
RT_LGB = [math.log(1.0 - 2.0 ** (-5.5 - h)) for h in range(4)]
HY_BANDS = 16
S_ML = 128 ** -0.5
S_RT = 128 ** -0.5
TWO_PI = 2.0 * math.pi


class Prog:
    NDMA = 6

    def __init__(self, nc, es):
        self.nc = nc
        self.eng = {'pe': nc.tensor, 'act': nc.scalar, 'dve': nc.vector, 'pool': nc.gpsimd, 'sp': nc.sync}
        self.sem = {}
        for k in ['pe', 'act', 'dve', 'pool']:
            self.sem[k] = es.enter_context(nc.semaphore('s_' + k))
        self.dq = {}
        for q in ['sp', 'pool', 'act']:
            self.dq[q] = 0
            for i in range(self.NDMA):
                self.sem[('d', q, i)] = es.enter_context(nc.semaphore('d_%s%d' % (q, i)))
        self.ccsems = [es.enter_context(nc.semaphore('cc_%d' % i)) for i in range(24)]
        self.ncc = 0
        self.cnt = {k: 0 for k in ['pe', 'act', 'dve', 'pool']}
        self.last = {}
        self.seen = {k: {} for k in self.eng}
        self.state = {}
        self.ninst = 0

    def _deps(self, e, reads, writes):
        deps = {}

        def add(k, v):
            if deps.get(k, 0) < v:
                deps[k] = v
        for key in reads:
            st = self.state.get(key)
            if st and st[0]:
                add(*st[0])
        for key in writes:
            st = self.state.get(key)
            if st:
                if st[0]:
                    add(*st[0])
                for k, v in st[1].items():
                    if k == e:
                        continue
                    add(k, v)
        for k, v in deps.items():
            if e == 'pe' and k == 'pe':
                continue
            if self.seen[e].get(k, 0) >= v:
                continue
            self.seen[e][k] = v
            self.eng[e].wait_ge(self.sem[k], v)
            self.ninst += 1

    def _record(self, reads, writes, tok):
        for key in reads:
            st = self.state.get(key)
            if st is None:
                st = self.state[key] = [None, {}]
            if st[1].get(tok[0], 0) < tok[1]:
                st[1][tok[0]] = tok[1]
        for key in writes:
            self.state[key] = [tok, {}]
        self.last[tok[0]] = tok[1]

    def op(self, e, fn, reads=(), writes=()):
        self._deps(e, reads, writes)
        self.cnt[e] += 1
        tok = (e, self.cnt[e])
        fn(self.eng[e]).then_inc(self.sem[e], 1)
        self.ninst += 1
        self._record(reads, writes, tok)

    def dma(self, q, out, in_, reads=(), writes=(), **kw):
        n = self.dq[q]
        self.dq[q] += 1
        slot = n % self.NDMA
        val = 16 * (n // self.NDMA + 1)
        key = ('d', q, slot)
        self._deps(q, reads, writes)
        if val > 16 and self.seen[q].get(key, 0) < val - 16:
            self.seen[q][key] = val - 16
            self.eng[q].wait_ge(self.sem[key], val - 16)
        self.eng[q].dma_start(out=out, in_=in_, **kw).then_inc(self.sem[key], 16)
        self.ninst += 1
        self._record(reads, writes, (key, val))

    def allgather(self, in2d, out2d, dummy):
        self.barrier()
        sem = self.ccsems[self.ncc]
        self.ncc += 1
        self.eng['pool'].collective_compute("AllGather", ALU.bypass, replica_groups=[list(range(8))],
                                            ins=[in2d.opt()], outs=[out2d.opt()]).then_inc(sem)
        self.eng['pool'].wait_ge(sem, 1)
        self.op('pool', lambda e: e.memset(dummy[0:1, 0:1], 0.0), writes=['cc_dummy'])
        self.barrier()

    def barrier(self):
        for e in self.eng:
            for k, v in self.last.items():
                if self.seen[e].get(k, 0) >= v:
                    continue
                self.seen[e][k] = v
                self.eng[e].wait_ge(self.sem[k], v)
        self.state = {}


class Ctx:
    pass


USPLIT = 3088
USE_CC = True


class USplit:
    def __init__(self, u1, u2):
        self.u1, self.u2 = u1, u2

    def __getitem__(self, key):
        rows, cols = key
        a, b = cols.start, cols.stop
        if b <= USPLIT:
            return self.u1[rows, a:b]
        assert a >= USPLIT, (a, b)
        return self.u2[rows, a - USPLIT:b - USPLIT]

    def pieces(self, rows, a, b):
        out = []
        if a < USPLIT:
            e = min(b, USPLIT)
            out.append((self.u1[rows, a:e], 0, e - a))
        if b > USPLIT:
            st = max(a, USPLIT)
            out.append((self.u2[rows, st - USPLIT:b - USPLIT], st - a, b - a))
        return out


def bf(a):
    return np.ascontiguousarray(a).astype(ml_dtypes.bfloat16)


_CONST = None


def host_consts():
    global _CONST
    if _CONST is not None:
        return _CONST
    c = {}
    c['identb'] = bf(np.eye(128))
    c['identf'] = np.eye(128, dtype=np.float32)
    c['jrev'] = np.ascontiguousarray(np.eye(128, dtype=np.float32)[::-1])
    sel = np.zeros((8, 8, 128), np.float32)
    for r in range(8):
        sel[r, r, :] = 1.0
    c['sel'] = sel
    j = np.arange(128)[:, None]
    i = np.arange(128)[None, :]
    c['maskf'] = (S_ML * (j <= i)).astype(np.float32)
    c['maskb'] = (S_ML * (j >= i)).astype(np.float32)
    dct = np.zeros((128, 4, 128), np.float64)
    qwf = np.zeros((128, 4, 128), np.float64)
    qwb = np.zeros((128, 4, 128), np.float64)
    kwf = np.zeros((128, 4), np.float64)
    kwb = np.zeros((128, 4), np.float64)
    for h in range(4):
        gf, gb = RT_LGF[h], RT_LGB[h]
        dct[:, h, :] = S_RT * (np.where(i >= j, np.exp(gf * np.maximum(i - j, 0)), 0.0)
                               + np.where(j >= i, np.exp(gb * np.maximum(j - i, 0)), 0.0))
        qwf[:, h, :] = S_RT * np.exp(gf * (i + 1.0))
        qwb[:, h, :] = S_RT * np.exp(gb * (128.0 - i))
        kwf[:, h] = np.exp(gf * (127.0 - j[:, 0]))
        kwb[:, h] = np.exp(gb * (j[:, 0] * 1.0))
    c['dct'] = dct.astype(np.float32)
    c['qwf'] = qwf.astype(np.float32)
    c['qwb'] = qwb.astype(np.float32)
    c['kwf'] = kwf.astype(np.float32)
    c['kwb'] = kwb.astype(np.float32)
    inv = (np.float32(10000.0) ** (-np.arange(0, 128, 2, dtype=np.float32) / np.float32(128))).astype(np.float32)
    ang = (np.arange(LB, dtype=np.float32)[:, None] * inv[None, :]).astype(np.float32)
    c['rot'] = np.concatenate([np.cos(ang), np.sin(ang)], axis=1).astype(np.float32)
    for name, L, nb in (('a', LA, 1), ('b', LB, 4)):
        t = np.linspace(0.0, 1.0, L, dtype=np.float32)
        w = (np.float32(2.0 * math.pi / L) * np.arange(L, dtype=np.float32)).astype(np.float32)
        bands = np.linspace(1e-4, HY_BANDS - 1, HY_BANDS, dtype=np.float32)
        feats = np.concatenate([t[:, None], np.cos(bands[None, :] * w[:, None]),
                                -np.sin(bands[None, :] * w[:, None])], axis=1).astype(np.float32)
        pieces = list(range(-(nb - 1), nb))
        ft = np.zeros((len(pieces), 2, 33, P), np.float32)
        tn = np.zeros((len(pieces), 2, 128, 32), np.float32)
        for pi, d in enumerate(pieces):
            for half in range(2):
                jj = np.arange(P) + half * P
                m = np.where(jj < P, jj, jj - NF)
                lag = np.abs(d * P + m)
                lag = np.minimum(lag, L - 1)
                ft[pi, half] = feats[lag].T
                tn[pi, half] = (-t[lag]).reshape(32, 128).T
        c['feat_' + name] = ft
        c['ntn_' + name] = tn
    s = np.arange(NF, dtype=np.int64)
    k = np.arange(NKC * 128, dtype=np.int64)
    ph = (s[:, None] * k[None, :]) % NF
    angm = (2.0 * np.pi / NF) * ph
    valid = (k <= NF // 2)[None, :]
    cosm = np.where(valid, np.cos(angm), 0.0)
    sinm = np.where(valid, np.sin(angm), 0.0)
    fc = cosm.reshape(64, 128, NKC, 128)
    fs = (-sinm).reshape(64, 128, NKC, 128)
    ftab = np.stack([fc, fs], axis=0)
    c['fwd'] = bf(ftab.transpose(3, 2, 1, 0, 4))
    wk = np.where((k == 0) | (k == NF // 2), 1.0, 2.0) / NF
    ci = (cosm[:P] * wk[None, :]).T
    si = (-sinm[:P] * wk[None, :]).T
    ci = ci.reshape(NKC, 128, 32, 128)
    si = si.reshape(NKC, 128, 32, 128)
    itab = np.stack([ci, si], axis=0)
    c['inv'] = bf(itab.transpose(3, 2, 1, 0, 4))
    _CONST = c
    return c


CONST_SPECS = {
    'identb': ([128, 128], BF16), 'identf': ([128, 128], F32), 'jrev': ([128, 128], F32),
    'sel': ([8, 8, 128], F32), 'maskf': ([128, 128], F32), 'maskb': ([128, 128], F32),
    'dct': ([128, 4, 128], F32), 'qwf': ([128, 4, 128], F32), 'qwb': ([128, 4, 128], F32),
    'kwf': ([128, 4], F32), 'kwb': ([128, 4], F32), 'rot': ([LB, 128], F32),
    'feat_a': ([1, 2, 33, P], F32), 'ntn_a': ([1, 2, 128, 32], F32),
    'feat_b': ([7, 2, 33, P], F32), 'ntn_b': ([7, 2, 128, 32], F32),
    'fwd': ([NKC, 128, 64, 2, 128], BF16), 'inv': ([32, 128, NKC, 2, 128], BF16),
    'fwd_own': ([5, 128, 64, 2, 128], BF16),
}
WEIGHT_SPECS = {
    'norm_g': [DEPTH, D], 'w_in': [DEPTH, D, INC], 'ml_conv_w': [DEPTH, 3, 1024], 'ml_conv_b': [DEPTH, 1024],
    'ml_gate_b': [DEPTH, 4, 4], 'ml_norm_g': [DEPTH, 512], 'rt_norm_g': [DEPTH, 512],
    'hy_conv_w': [DEPTH, 3, 1536], 'hy_conv_b': [DEPTH, 1536], 'hy_w1': [DEPTH, 33, 64], 'hy_b1': [DEPTH, 64],
    'hy_w2': [DEPTH, 64, 64], 'hy_b2': [DEPTH, 64], 'hy_w3': [DEPTH, 64, 2048], 'hy_freq': [DEPTH, 64],
    'hy_deltas': [DEPTH, 2, 2, 512], 'hy_skip': [DEPTH, 2, 512], 'w_out': [DEPTH, 1536, D], 'final_g': [D],
}


def bcast_rows(ap1d, nparts):
    return ap1d.partition_broadcast(nparts)


_UID = [0]


def sb(nc, es, name, shape, dt):
    _UID[0] += 1
    return es.enter_context(nc.sbuf_tensor('%s_%d' % (name, _UID[0]), shape, dt))


def ps(nc, es, name, shape, dt=F32):
    _UID[0] += 1
    return es.enter_context(nc.psum_tensor('%s_%d' % (name, _UID[0]), shape, dt))


def rms_rows(G, xt, key_x, junk, ss, rstd, keyp):
    Pg = G.P
    Pg.op('act', lambda e: e.activation(out=junk[:], in_=xt, func=AF.Square), reads=[key_x], writes=[keyp + 'junk'])
    Pg.op('dve', lambda e: e.reduce_sum(out=ss[:], in_=junk[:], axis=AX.X), reads=[keyp + 'junk'], writes=[keyp + 'ss'])
    Pg.op('dve', lambda e: e.tensor_scalar(out=rstd[:], in0=ss[:], scalar1=1.0 / D, scalar2=1e-6,
                                           op0=ALU.mult, op1=ALU.add), reads=[keyp + 'ss'], writes=[keyp + 'rstd'])
    Pg.op('act', lambda e: e.sqrt(out=rstd[:], in_=rstd[:]), reads=[keyp + 'rstd'], writes=[keyp + 'rstd'])
    Pg.op('dve', lambda e: e.reciprocal(out=rstd[:], in_=rstd[:]), reads=[keyp + 'rstd'], writes=[keyp + 'rstd'])


def phase_A(G, layer, seq, blk):
    nc, Pg = G.nc, G.P
    xsrc = seq.xin if layer == 0 else seq.X
    t0 = blk * P
    with ExitStack() as es:
        hT = sb(nc, es, 'a_hT', [128, 8, P], BF16)
        gbc = sb(nc, es, 'a_gbc', [128, D], F32)
        xt = [sb(nc, es, 'a_xt%d' % i, [128, D], F32) for i in range(2)]
        junk = sb(nc, es, 'a_junk', [128, D], F32)
        ss = sb(nc, es, 'a_ss', [128, 1], F32)
        rstd = sb(nc, es, 'a_rstd', [128, 1], F32)
        h16 = [sb(nc, es, 'a_h16%d' % i, [128, D], BF16) for i in range(2)]
        wst = [sb(nc, es, 'a_wst%d' % i, [128, 8, 512], F32) for i in range(2)]
        w16 = [sb(nc, es, 'a_w16%d' % i, [128, 8, 512], BF16) for i in range(2)]
        ost = [sb(nc, es, 'a_ost%d' % i, [128, 512], F32) for i in range(4)]
        tp = [ps(nc, es, 'a_tp%d' % i, [128, 512], BF16) for i in range(2)]
        mm = [ps(nc, es, 'a_mm%d' % i, [128, 512], F32) for i in range(4)]
        Pg.dma('sp', gbc[:], bcast_rows(G.w['norm_g'][layer], 128), writes=['a_gbc'])
        for t in range(32):
            b = t % 2
            Pg.dma('sp', xt[b][:], xsrc[t0 + t * 128: t0 + (t + 1) * 128, :], writes=['a_xt%d' % b])
            rms_rows(G, xt[b][:], 'a_xt%d' % b, junk, ss, rstd, 'a_')
            Pg.op('dve', lambda e: e.scalar_tensor_tensor(out=h16[b][:], in0=xt[b][:], scalar=rstd[:, 0:1], in1=gbc[:],
                                                          op0=ALU.mult, op1=ALU.mult),
                  reads=['a_xt%d' % b, 'a_rstd', 'a_gbc'], writes=['a_h16%d' % b])
            for half in range(2):
                for jj in range(4):
                    kc = half * 4 + jj
                    Pg.op('pe', lambda e: e.transpose(tp[half][:, jj * 128:(jj + 1) * 128],
                                                      h16[b][:, kc * 128:(kc + 1) * 128], G.identb[:]),
                          reads=['a_h16%d' % b], writes=['a_tp%d' % half])
                dst = hT[:, half * 4:half * 4 + 4, t * 128:(t + 1) * 128]
                src = tp[half][:].rearrange('p (j t) -> p j t', j=4)
                if half == 0:
                    Pg.op('act', lambda e: e.copy(out=dst, in_=src), reads=['a_tp0'], writes=[('a_hT', t, 0)])
                else:
                    Pg.op('dve', lambda e: e.tensor_copy(out=dst, in_=src), reads=['a_tp1'], writes=[('a_hT', t, 1)])
        groups = [('f', g * 128, 128) for g in range(8)] + [('t', 1024 + g * 512, 512) for g in range(11)] + [('t', 1024 + 11 * 512, 16)]
        oi = 0
        for gi, (kind, c0, wd) in enumerate(groups):
            wb = gi % 2
            wsrc = G.w['w_in'][layer, :, c0:c0 + wd].rearrange('(kc p) c -> p kc c', p=128)
            Pg.dma('sp', wst[wb][:, :, 0:wd], wsrc, writes=['a_wst%d' % wb])
            ceng = 'pool' if gi % 2 == 0 else 'dve'
            Pg.op(ceng, lambda e: e.tensor_copy(out=w16[wb][:, :, 0:wd], in_=wst[wb][:, :, 0:wd]),
                  reads=['a_wst%d' % wb], writes=['a_w16%d' % wb])
            for t in range(32 if kind == 't' else 8):
                pb = oi % 4
                if kind == 't':
                    for kc in range(8):
                        Pg.op('pe', lambda e: e.matmul(mm[pb][:, 0:wd], lhsT=hT[:, kc, t * 128:(t + 1) * 128],
                                                       rhs=w16[wb][:, kc, 0:wd], start=(kc == 0), stop=(kc == 7)),
                              reads=[('a_hT', t, 0), ('a_hT', t, 1), 'a_w16%d' % wb], writes=['a_mm%d' % pb])
                    dst = None
                    ow = wd
                else:
                    hk = [('a_hT', 4 * t + q, hh) for q in range(4) for hh in range(2)]
                    for kc in range(8):
                        Pg.op('pe', lambda e: e.matmul(mm[pb][:, :], lhsT=w16[wb][:, kc, 0:128],
                                                       rhs=hT[:, kc, t * 512:(t + 1) * 512], start=(kc == 0), stop=(kc == 7)),
                              reads=hk + ['a_w16%d' % wb], writes=['a_mm%d' % pb])
                    dst = seq.QKT[c0:c0 + 128, t0 + t * 512:t0 + (t + 1) * 512]
                    ow = 512
                if oi % 2 == 0:
                    Pg.op('act', lambda e: e.copy(out=ost[pb][:, 0:ow], in_=mm[pb][:, 0:ow]),
                          reads=['a_mm%d' % pb], writes=['a_ost%d' % pb])
                else:
                    Pg.op('dve', lambda e: e.tensor_copy(out=ost[pb][:, 0:ow], in_=mm[pb][:, 0:ow]),
                          reads=['a_mm%d' % pb], writes=['a_ost%d' % pb])
                if dst is None:
                    for (dap, o0, o1) in seq.U.pieces(slice(t0 + t * 128, t0 + (t + 1) * 128), c0 - 1024, c0 - 1024 + wd):
                        Pg.dma('pool' if oi % 2 == 0 else 'sp', dap, ost[pb][:, o0:o1], reads=['a_ost%d' % pb])
                else:
                    Pg.dma('pool' if oi % 2 == 0 else 'sp', dst, ost[pb][:, 0:ow], reads=['a_ost%d' % pb])
                oi += 1
    Pg.barrier()


def phase_O(G, layer, seq):
    nc, Pg = G.nc, G.P
    xsrc = seq.xin if layer == 0 else seq.X
    last = layer == DEPTH - 1
    with ExitStack() as es:
        wo = sb(nc, es, 'o_wo', [128, 12, D], BF16)
        wst = [sb(nc, es, 'o_wst%d' % i, [128, D], F32) for i in range(2)]
        mix = [sb(nc, es, 'o_mix%d' % i, [128, 1536], F32) for i in range(2)]
        m16 = [sb(nc, es, 'o_m16%d' % i, [128, 1536], BF16) for i in range(2)]
        mT = [sb(nc, es, 'o_mT%d' % i, [128, 12, 128], BF16) for i in range(2)]
        xt = [sb(nc, es, 'o_xt%d' % i, [128, D], F32) for i in range(2)]
        xn = [sb(nc, es, 'o_xn%d' % i, [128, D], F32) for i in range(2)]
        gbc = sb(nc, es, 'o_gbc', [128, D], F32)
        junk = sb(nc, es, 'o_junk', [128, D], F32)
        ss = sb(nc, es, 'o_ss', [128, 1], F32)
        rstd = sb(nc, es, 'o_rstd', [128, 1], F32)
        yo = [sb(nc, es, 'o_yo%d' % i, [128, D], F32) for i in range(2)]
        tp = [ps(nc, es, 'o_tp%d' % i, [128, 512], BF16) for i in range(3)]
        mm = [ps(nc, es, 'o_mm%d' % i, [128, 512], F32) for i in range(4)]
        for kc in range(12):
            b = kc % 2
            Pg.dma('sp', wst[b][:], G.w['w_out'][layer, kc * 128:(kc + 1) * 128, :], writes=['o_wst%d' % b])
            Pg.op('pool' if b else 'dve', lambda e: e.tensor_copy(out=wo[:, kc, :], in_=wst[b][:]),
                  reads=['o_wst%d' % b], writes=[('o_wo', kc)])
        wokeys = [('o_wo', kc) for kc in range(12)]
        if last:
            Pg.dma('sp', gbc[:], bcast_rows(G.w['final_g'], 128), writes=['o_gbc'])
        for t in range(seq.L // 128):
            b = t % 2
            rows = slice(t * 128, (t + 1) * 128)
            Pg.dma('sp', mix[b][:], seq.MIX[rows, :], writes=['o_mix%d' % b])
            Pg.dma('sp', xt[b][:], xsrc[rows, :], writes=['o_xt%d' % b])
            Pg.op('pool', lambda e: e.tensor_copy(out=m16[b][:], in_=mix[b][:]), reads=['o_mix%d' % b], writes=['o_m16%d' % b])
            for g3 in range(3):
                for jj in range(4):
                    kc = g3 * 4 + jj
                    Pg.op('pe', lambda e: e.transpose(tp[g3][:, jj * 128:(jj + 1) * 128],
                                                      m16[b][:, kc * 128:(kc + 1) * 128], G.identb[:]),
                          reads=['o_m16%d' % b], writes=['o_tp%d' % g3])
                dst = mT[b][:, g3 * 4:g3 * 4 + 4, :]
                src = tp[g3][:].rearrange('p (j t) -> p j t', j=4)
                if g3 == 1:
                    Pg.op('dve', lambda e: e.tensor_copy(out=dst, in_=src), reads=['o_tp%d' % g3], writes=[('o_mT%d' % b, g3)])
                else:
                    Pg.op('act', lambda e: e.copy(out=dst, in_=src), reads=['o_tp%d' % g3], writes=[('o_mT%d' % b, g3)])
            for n in range(2):
                pb = (t * 2 + n) % 4
                for kc in range(12):
                    Pg.op('pe', lambda e: e.matmul(mm[pb][:], lhsT=mT[b][:, kc, :], rhs=wo[:, kc, n * 512:(n + 1) * 512],
                                                   start=(kc == 0), stop=(kc == 11)),
                          reads=[('o_mT%d' % b, g3) for g3 in range(3)] + wokeys, writes=['o_mm%d' % pb])
                Pg.op('dve', lambda e: e.tensor_tensor(out=xn[b][:, n * 512:(n + 1) * 512], in0=mm[pb][:],
                                                       in1=xt[b][:, n * 512:(n + 1) * 512], op=ALU.add),
                      reads=['o_mm%d' % pb, 'o_xt%d' % b], writes=[('o_xn%d' % b, n)])
            xk = [('o_xn%d' % b, 0), ('o_xn%d' % b, 1)]
            if not last:
                Pg.dma('pool', seq.X[rows, :], xn[b][:], reads=xk)
            else:
                Pg.op('pool', lambda e: e.tensor_copy(out=xn[b][:, 0:1], in_=xn[b][:, 0:1]), reads=xk, writes=['o_xnf%d' % b])
                rms_rows(G, xn[b][:], 'o_xnf%d' % b, junk, ss, rstd, 'o_')
                Pg.op('dve', lambda e: e.scalar_tensor_tensor(out=yo[b][:], in0=xn[b][:], scalar=rstd[:, 0:1], in1=gbc[:],
                                                              op0=ALU.mult, op1=ALU.mult),
                      reads=xk + ['o_xnf%d' % b, 'o_rstd', 'o_gbc'], writes=['o_yo%d' % b])
                Pg.dma('pool', seq.yout[rows, :], yo[b][:], reads=['o_yo%d' % b])
    Pg.barrier()


def headnorm_gate(G, pre, h, hkeys, z32, zkey, gbc, gkey, dst, dma_q):
    Pg = G.P
    T = G.hn
    h3 = h[:].rearrange('p (a d) -> p a d', a=4)
    Pg.op('dve', lambda e: e.reduce_sum(out=T['s1'][:], in_=h3, axis=AX.X), reads=hkeys, writes=[pre + 's1'])
    Pg.op('dve', lambda e: e.tensor_scalar(out=T['s1'][:], in0=T['s1'][:], scalar1=1.0 / 128, scalar2=None, op0=ALU.mult),
          reads=[pre + 's1'], writes=[pre + 's1'])
    hc3 = T['hc'][:].rearrange('p (a d) -> p a d', a=4)
    Pg.op('dve', lambda e: e.tensor_tensor(out=hc3, in0=h3, in1=T['s1'][:].unsqueeze(2).to_broadcast([128, 4, 128]),
                                           op=ALU.subtract), reads=hkeys + [pre + 's1'], writes=[pre + 'hc'])
    Pg.op('act', lambda e: e.activation(out=T['sq'][:], in_=T['hc'][:], func=AF.Square), reads=[pre + 'hc'], writes=[pre + 'sq'])
    Pg.op('dve', lambda e: e.reduce_sum(out=T['s2'][:], in_=T['sq'][:].rearrange('p (a d) -> p a d', a=4), axis=AX.X),
          reads=[pre + 'sq'], writes=[pre + 's2'])
    Pg.op('dve', lambda e: e.tensor_scalar(out=T['s2'][:], in0=T['s2'][:], scalar1=1.0 / 128, scalar2=1e-5,
                                           op0=ALU.mult, op1=ALU.add), reads=[pre + 's2'], writes=[pre + 's2'])
    Pg.op('act', lambda e: e.sqrt(out=T['s2'][:], in_=T['s2'][:]), reads=[pre + 's2'], writes=[pre + 's2'])
    Pg.op('dve', lambda e: e.reciprocal(out=T['s2'][:], in_=T['s2'][:]), reads=[pre + 's2'], writes=[pre + 's2'])
    Pg.op('dve', lambda e: e.tensor_tensor(out=hc3, in0=hc3, in1=T['s2'][:].unsqueeze(2).to_broadcast([128, 4, 128]),
                                           op=ALU.mult), reads=[pre + 'hc', pre + 's2'], writes=[pre + 'hc'])
    Pg.op('pool', lambda e: e.tensor_tensor(out=T['hc'][:], in0=T['hc'][:], in1=gbc[:], op=ALU.mult),
          reads=[pre + 'hc', gkey], writes=[pre + 'hc'])
    Pg.op('act', lambda e: e.activation(out=T['sq'][:], in_=z32, func=AF.Silu), reads=[zkey, pre + 'sq'], writes=[pre + 'sq'])
    Pg.op('dve', lambda e: e.tensor_tensor(out=T['out'][:], in0=T['hc'][:], in1=T['sq'][:], op=ALU.mult),
          reads=[pre + 'hc', pre + 'sq'], writes=[pre + 'out'])
    Pg.dma(dma_q, dst, T['out'][:], reads=[pre + 'out'])


def alloc_hn(G, es, pre):
    nc = G.nc
    G.hn = {'s1': sb(nc, es, pre + 's1', [128, 4], F32), 's2': sb(nc, es, pre + 's2', [128, 4], F32),
            'hc': sb(nc, es, pre + 'hc', [128, 512], F32), 'sq': sb(nc, es, pre + 'sq', [128, 512], F32),
            'out': sb(nc, es, pre + 'out', [128, 512], F32)}


def phase_G(G, layer, seq):
    nc, Pg = G.nc, G.P
    L = seq.L
    NC = L // 128
    nseg = L // P
    gb = G.w['ml_gate_b'][layer].rearrange('(t s) h -> s t h', s=2)
    with ExitStack() as es:
        identf = sb(nc, es, 'g_identf', [128, 128], F32)
        jrev = sb(nc, es, 'g_jrev', [128, 128], F32)
        sel = sb(nc, es, 'g_sel', [8, 8, 128], F32)
        ones = sb(nc, es, 'g_ones', [8, P], F32)
        one1 = sb(nc, es, 'g_one1', [8, 1], F32)
        bi8 = sb(nc, es, 'g_bi8', [8, 1], F32)
        bf8 = sb(nc, es, 'g_bf8', [8, 1], F32)
        cr = sb(nc, es, 'g_cr', [8, 1], F32)
        cnb = sb(nc, es, 'g_cnb', [8, 1], F32)
        gtF = sb(nc, es, 'g_gtF', [128, 32, 16], F32)
        gtB = sb(nc, es, 'g_gtB', [128, 32, 16], F32)
        liF = sb(nc, es, 'g_liF', [128, 32, 8], F32)
        liB = sb(nc, es, 'g_liB', [128, 32, 8], F32)
        pfF = sb(nc, es, 'g_pfF', [128, 32, 8], F32)
        pfB = sb(nc, es, 'g_pfB', [128, 32, 8], F32)
        LI = sb(nc, es, 'g_LI', [8, P], F32)
        SP = sb(nc, es, 'g_SP', [8, P], F32)
        NB = sb(nc, es, 'g_NB', [8, P], F32)
        AL = sb(nc, es, 'g_AL', [8, P], F32)
        RR = sb(nc, es, 'g_RR', [8, P], F32)
        EN = sb(nc, es, 'g_EN', [8, P], F32)
        KW = sb(nc, es, 'g_KW', [8, P], F32)
        CC = sb(nc, es, 'g_CC', [8, P], F32)
        RP = sb(nc, es, 'g_RP', [8, 33], F32)
        AA = sb(nc, es, 'g_AA', [8, 32], F32)
        T1s = [sb(nc, es, 'g_T1s%d' % i, [128, 24], F32) for i in range(2)]
        GTF = sb(nc, es, 'g_GTF', [128, 32, 12], F32)
        GTB = sb(nc, es, 'g_GTB', [128, 32, 12], F32)
        ABCs = sb(nc, es, 'g_ABCs', [128, 8, 32], F32)
        li_ps = [ps(nc, es, 'g_lips%d' % i, [8, 512]) for i in range(2)]
        pf_ps = [ps(nc, es, 'g_pfps%d' % i, [8, 512]) for i in range(2)]
        t_ps = [ps(nc, es, 'g_tps%d' % i, [128, 24]) for i in range(2)]
        t2_ps = ps(nc, es, 'g_t2ps', [128, 24])
        bc_ps = ps(nc, es, 'g_bcps', [128, 256])
        Pg.dma('sp', identf[:], G.c['identf'][:, :], writes=['g_identf'])
        Pg.dma('sp', jrev[:], G.c['jrev'][:, :], writes=['g_jrev'])
        Pg.dma('sp', sel[:], G.c['sel'][:, :, :], writes=['g_sel'])
        gbl = G.w['ml_gate_b'][layer]
        Pg.dma('sp', bi8[0:4, :], gbl[0].unsqueeze(1), writes=['g_bi8'])
        Pg.dma('sp', bi8[4:8, :], gbl[2].unsqueeze(1), writes=['g_bi8'])
        Pg.dma('sp', bf8[0:4, :], gbl[1].unsqueeze(1), writes=['g_bf8'])
        Pg.dma('sp', bf8[4:8, :], gbl[3].unsqueeze(1), writes=['g_bf8'])
        Pg.op('dve', lambda e: e.tensor_scalar(out=bf8[:], in0=bf8[:], scalar1=-1.0, scalar2=None, op0=ALU.mult),
              reads=['g_bf8'], writes=['g_bf8'])
        Pg.op('pool', lambda e: e.memset(ones[:], 1.0), writes=['g_ones'])
        Pg.op('pool', lambda e: e.memset(one1[:], 1.0), writes=['g_one1'])
        Pg.op('pool', lambda e: e.memset(cr[:], -1e30), writes=['g_cr'])
        Pg.op('pool', lambda e: e.memset(cnb[:], 0.0), writes=['g_cnb'])
        for t_, nm in ((liF, 'g_liF'), (liB, 'g_liB'), (pfF, 'g_pfF'), (pfB, 'g_pfB')):
            Pg.op('pool', lambda e: e.memset(t_[:], 0.0), writes=[nm])
        for sg in range(nseg):
            c0 = NC - 32 * (sg + 1)
            Pg.dma('sp', gtF[:], seq.U[sg * P:(sg + 1) * P, C_G:C_G + 16].rearrange('(n p) c -> p n c', p=128),
                   writes=['g_gtF'])
            Pg.dma('pool', gtB[:], seq.U[c0 * 128:(c0 + 32) * 128, C_G:C_G + 16].rearrange('(n p) c -> p n c', p=128),
                   writes=['g_gtB'])
            Pg.op('dve', lambda e: e.tensor_copy(out=liF[:, :, 0:4], in_=gtF[:, :, 0:4]), reads=['g_gtF'], writes=['g_liF'])
            Pg.op('dve', lambda e: e.tensor_copy(out=pfF[:, :, 0:4], in_=gtF[:, :, 4:8]), reads=['g_gtF'], writes=['g_pfF'])
            Pg.op('dve', lambda e: e.tensor_copy(out=liB[:, :, 4:8], in_=gtB[:, :, 8:12]), reads=['g_gtB'], writes=['g_liB'])
            Pg.op('dve', lambda e: e.tensor_copy(out=pfB[:, :, 4:8], in_=gtB[:, :, 12:16]), reads=['g_gtB'], writes=['g_pfB'])
            for gq in range(8):
                pb = gq % 2
                for mm_ in range(4):
                    m = gq * 4 + mm_
                    cs = slice(mm_ * 128, (mm_ + 1) * 128)
                    Pg.op('pe', lambda e: e.matmul(li_ps[pb][:, cs], lhsT=liF[:, m, :], rhs=identf[:], start=True, stop=False),
                          reads=['g_liF', 'g_identf'], writes=['g_lips%d' % pb])
                    Pg.op('pe', lambda e: e.matmul(li_ps[pb][:, cs], lhsT=liB[:, 31 - m, :], rhs=jrev[:], start=False, stop=True),
                          reads=['g_liB', 'g_jrev'], writes=['g_lips%d' % pb])
                    Pg.op('pe', lambda e: e.matmul(pf_ps[pb][:, cs], lhsT=pfF[:, m, :], rhs=identf[:], start=True, stop=False),
                          reads=['g_pfF', 'g_identf'], writes=['g_pfps%d' % pb])
                    Pg.op('pe', lambda e: e.matmul(pf_ps[pb][:, cs], lhsT=pfB[:, 31 - m, :], rhs=jrev[:], start=False, stop=True),
                          reads=['g_pfB', 'g_jrev'], writes=['g_pfps%d' % pb])
                gs = slice(gq * 512, (gq + 1) * 512)
                Pg.op('act', lambda e: e.activation(out=LI[:, gs], in_=li_ps[pb][:], func=AF.Identity, bias=bi8[:, 0:1], scale=1.0),
                      reads=['g_lips%d' % pb, 'g_bi8'], writes=['g_LI'])
                Pg.op('act', lambda e: e.activation(out=SP[:, gs], in_=pf_ps[pb][:], func=AF.Exp, bias=bf8[:, 0:1], scale=-1.0),
                      reads=['g_pfps%d' % pb, 'g_bf8'], writes=['g_SP'])
            Pg.op('act', lambda e: e.activation(out=SP[:], in_=SP[:], func=AF.Ln, bias=one1[:, 0:1], scale=1.0),
                  reads=['g_SP', 'g_one1'], writes=['g_SP'])
            Pg.op('dve', lambda e: e.tensor_tensor_scan(out=NB[:], data0=ones[:], data1=SP[:], initial=cnb[:, 0:1],
                                                        op0=ALU.mult, op1=ALU.add),
                  reads=['g_ones', 'g_SP', 'g_cnb'], writes=['g_NB'])
            Pg.op('dve', lambda e: e.tensor_tensor(out=AL[:], in0=LI[:], in1=NB[:], op=ALU.add),
                  reads=['g_LI', 'g_NB'], writes=['g_AL'])
            Pg.op('dve', lambda e: e.tensor_tensor_scan(out=RR[:], data0=ones[:], data1=AL[:], initial=cr[:, 0:1],
                                                        op0=ALU.mult, op1=ALU.max),
                  reads=['g_ones', 'g_AL', 'g_cr'], writes=['g_RR'])
            Pg.op('dve', lambda e: e.tensor_tensor(out=EN[:], in0=RR[:], in1=NB[:], op=ALU.subtract),
                  reads=['g_RR', 'g_NB'], writes=['g_EN'])
            Pg.op('act', lambda e: e.activation(out=EN[:], in_=EN[:], func=AF.Exp, scale=-1.0), reads=['g_EN'], writes=['g_EN'])
            Pg.op('dve', lambda e: e.tensor_copy(out=RP[:, 0:1], in_=cr[:]), reads=['g_cr'], writes=['g_RP'])
            Pg.op('dve', lambda e: e.tensor_copy(out=RP[:, 1:33], in_=RR[:].rearrange('p (n t) -> p n t', t=128)[:, :, 127]),
                  reads=['g_RR', 'g_RP'], writes=['g_RP'])
            Pg.op('dve', lambda e: e.tensor_tensor(out=AA[:], in0=RP[:, 0:32], in1=RP[:, 1:33], op=ALU.subtract),
                  reads=['g_RP'], writes=['g_AA'])
            Pg.op('act', lambda e: e.activation(out=AA[:], in_=AA[:], func=AF.Exp), reads=['g_AA'], writes=['g_AA'])
            rb = RP[:, 1:33].unsqueeze(2).to_broadcast([8, 32, 128])
            Pg.op('dve', lambda e: e.tensor_tensor(out=KW[:].rearrange('p (n t) -> p n t', t=128),
                                                   in0=AL[:].rearrange('p (n t) -> p n t', t=128), in1=rb, op=ALU.subtract),
                  reads=['g_AL', 'g_RP'], writes=['g_KW'])
            Pg.op('act', lambda e: e.activation(out=KW[:], in_=KW[:], func=AF.Exp), reads=['g_KW'], writes=['g_KW'])
            Pg.op('dve', lambda e: e.tensor_tensor(out=CC[:].rearrange('p (n t) -> p n t', t=128), in0=rb,
                                                   in1=RR[:].rearrange('p (n t) -> p n t', t=128), op=ALU.subtract),
                  reads=['g_RR', 'g_RP'], writes=['g_CC'])
            Pg.op('act', lambda e: e.activation(out=CC[:], in_=CC[:], func=AF.Exp), reads=['g_CC'], writes=['g_CC'])
            Pg.op('dve', lambda e: e.tensor_copy(out=cr[:], in_=RR[:, P - 1:P]), reads=['g_RR', 'g_RP'], writes=['g_cr'])
            Pg.op('dve', lambda e: e.tensor_copy(out=cnb[:], in_=NB[:, P - 1:P]), reads=['g_NB'], writes=['g_cnb'])
            for m in range(32):
                tb = m % 2
                cs = slice(m * 128, (m + 1) * 128)
                for q, QT in enumerate((KW, CC, EN)):
                    Pg.op('pe', lambda e: e.matmul(t_ps[tb][:, q * 8:(q + 1) * 8], lhsT=QT[:, cs], rhs=identf[0:8, 0:8],
                                                   start=True, stop=True),
                          reads=['g_KW', 'g_CC', 'g_EN', 'g_identf'], writes=['g_tps%d' % tb])
                Pg.op('act', lambda e: e.copy(out=T1s[tb][:], in_=t_ps[tb][:]), reads=['g_tps%d' % tb], writes=['g_T1s%d' % tb])
                Pg.op('pe', lambda e: e.matmul(t2_ps[:], lhsT=jrev[:], rhs=T1s[tb][:], start=True, stop=True),
                      reads=['g_T1s%d' % tb, 'g_jrev'], writes=['g_t2ps'])
                Pg.op('dve', lambda e: e.tensor_copy(out=GTF[:, m, :].rearrange('p (q r) -> p q r', q=3),
                                                     in_=T1s[tb][:].rearrange('p (q r) -> p q r', q=3)[:, :, 0:4]),
                      reads=['g_T1s%d' % tb], writes=['g_GTF'])
                Pg.op('dve', lambda e: e.tensor_copy(out=GTB[:, 31 - m, :].rearrange('p (q r) -> p q r', q=3),
                                                     in_=t2_ps[:].rearrange('p (q r) -> p q r', q=3)[:, :, 4:8]),
                      reads=['g_t2ps'], writes=['g_GTB'])
            for r in range(8):
                Pg.op('pe', lambda e: e.matmul(bc_ps[:, r * 32:(r + 1) * 32], lhsT=sel[:, r, :], rhs=AA[:], start=True, stop=True),
                      reads=['g_sel', 'g_AA'], writes=['g_bcps'])
            Pg.op('act', lambda e: e.copy(out=ABCs[:].rearrange('p r n -> p (r n)'), in_=bc_ps[:]), reads=['g_bcps'], writes=['g_ABCs'])
            Pg.dma('sp', seq.GTF[sg * P:(sg + 1) * P, :].rearrange('(n p) c -> p n c', p=128), GTF[:], reads=['g_GTF'])
            Pg.dma('sp', seq.GTB[c0 * 128:(c0 + 32) * 128, :].rearrange('(n p) c -> p n c', p=128), GTB[:], reads=['g_GTB'])
            Pg.dma('sp', seq.ABC[:, :, sg * 32:(sg + 1) * 32], ABCs[:], reads=['g_ABCs'])
    Pg.barrier()


def phase_B(G, layer, seq):
    nc, Pg = G.nc, G.P
    L = seq.L
    NC = L // 128
    NG = L // 512
    with ExitStack() as es:
        cw = sb(nc, es, 'b_cw', [128, 8, 3], F32)
        cb = sb(nc, es, 'b_cb', [128, 8], F32)
        maskf = sb(nc, es, 'b_maskf', [128, 128], F32)
        maskb = sb(nc, es, 'b_maskb', [128, 128], F32)
        abc = sb(nc, es, 'b_abc', [128, 8, 128], F32)
        gbc = sb(nc, es, 'b_gbc', [128, 512], F32)
        win = [sb(nc, es, 'b_win%d' % i, [128, 8, 514], F32) for i in range(2)]
        tmp = [sb(nc, es, 'b_tmp%d' % i, [128, 512], F32) for i in range(2)]
        qk16 = [sb(nc, es, 'b_qk16%d' % i, [128, 8, 512], BF16) for i in range(2)]
        v32 = [sb(nc, es, 'b_v32%d' % i, [128, 512], F32) for i in range(2)]
        vaug = [sb(nc, es, 'b_vaug%d' % i, [128, 4, 129], BF16) for i in range(2)]
        gt = [sb(nc, es, 'b_gt%d' % i, [128, 12], F32) for i in range(2)]
        S32 = sb(nc, es, 'b_S32', [128, 4, 129], F32)
        S16 = sb(nc, es, 'b_S16', [128, 4, 129], BF16)
        kw16 = [sb(nc, es, 'b_kw16%d' % i, [128, 128], BF16) for i in range(4)]
        pT = [sb(nc, es, 'b_pT%d' % i, [128, 128], BF16) for i in range(4)]
        hout = [sb(nc, es, 'b_hout%d' % i, [128, 512], F32) for i in range(2)]
        hf = [sb(nc, es, 'b_hf%d' % i, [128, 512], F32) for i in range(2)]
        o32 = [sb(nc, es, 'b_o32%d' % i, [128, 512], F32) for i in range(2)]
        z32 = [sb(nc, es, 'b_z32%d' % i, [128, 512], F32) for i in range(2)]
        dd = [sb(nc, es, 'b_dd%d' % i, [128, 4], F32) for i in range(4)]
        alloc_hn(G, es, 'bn_')
        kt4 = ps(nc, es, 'b_kt4', [128, 512], BF16)
        sT4 = ps(nc, es, 'b_sT4', [128, 512])
        o2 = [ps(nc, es, 'b_o2%d' % i, [128, 2, 256]) for i in range(2)]
        kv2 = [ps(nc, es, 'b_kv2%d' % i, [128, 2, 256]) for i in range(2)]
        for k3 in range(3):
            Pg.dma('sp', cw[:, :, k3], G.w['ml_conv_w'][layer, k3].rearrange('(g p) -> p g', p=128), writes=['b_cw'],
                   allow_slow_non_contiguous=True)
        Pg.dma('sp', cb[:], G.w['ml_conv_b'][layer].rearrange('(g p) -> p g', p=128), writes=['b_cb'],
               allow_slow_non_contiguous=True)
        Pg.dma('sp', maskf[:], G.c['maskf'][:, :], writes=['b_maskf'])
        Pg.dma('sp', maskb[:], G.c['maskb'][:, :], writes=['b_maskb'])
        Pg.dma('sp', abc[:, :, 0:NC], seq.ABC[:, :, 0:NC], writes=['b_abc'])
        Pg.dma('sp', gbc[:], bcast_rows(G.w['ml_norm_g'][layer], 128), writes=['b_gbc'])
        for i in range(2):
            Pg.op('pool', lambda e: e.memset(vaug[i][:], 1.0), writes=['b_vaug%d' % i])
        it = 0
        for sweep in range(2):
            mask = maskf if sweep == 0 else maskb
            Pg.op('pool', lambda e: e.memset(S32[:], 0.0), reads=['b_S16'], writes=['b_S32'])
            gorder = range(NG) if sweep == 0 else range(NG - 1, -1, -1)
            for gi, g in enumerate(gorder):
                wb = gi % 2
                t0 = g * 512
                lo = max(t0 - 1, 0)
                hi = min(t0 + 513, L)
                wlo = lo - (t0 - 1)
                whi = wlo + (hi - lo)
                if wlo > 0:
                    Pg.op('pool', lambda e: e.memset(win[wb][:, :, 0:1], 0.0), writes=['b_win%d' % wb])
                if whi < 514:
                    Pg.op('pool', lambda e: e.memset(win[wb][:, :, 513:514], 0.0), writes=['b_win%d' % wb])
                for q8 in range(8):
                    Pg.dma('sp' if q8 % 2 == 0 else 'act', win[wb][:, q8, wlo:whi], seq.QKT[q8 * 128:(q8 + 1) * 128, lo:hi],
                           writes=[('b_win%d' % wb, q8)], reads=['b_win%d' % wb])
                for q8 in range(8):
                    ce = 'dve'
                    tb = q8 % 2
                    Pg.op(ce, lambda e: e.tensor_scalar(out=tmp[tb][:], in0=win[wb][:, q8, 0:512], scalar1=cw[:, q8, 0:1],
                                                        scalar2=None, op0=ALU.mult),
                          reads=[('b_win%d' % wb, q8), 'b_win%d' % wb, 'b_cw'], writes=['b_tmp%d' % tb])
                    for kk in (1, 2):
                        Pg.op(ce, lambda e: e.scalar_tensor_tensor(out=tmp[tb][:], in0=win[wb][:, q8, kk:kk + 512],
                                                                   scalar=cw[:, q8, kk:kk + 1], in1=tmp[tb][:],
                                                                   op0=ALU.mult, op1=ALU.add),
                              reads=[('b_win%d' % wb, q8), 'b_tmp%d' % tb], writes=['b_tmp%d' % tb])
                    Pg.op('act', lambda e: e.activation(out=qk16[wb][:, q8, :], in_=tmp[tb][:], func=AF.Silu,
                                                        bias=cb[:, q8:q8 + 1], scale=1.0),
                          reads=['b_tmp%d' % tb, 'b_cb'], writes=[('b_qk16%d' % wb, q8)])
                corder = range(4) if sweep == 0 else range(3, -1, -1)
                for cc_ in corder:
                    n = g * 4 + cc_
                    M = n if sweep == 0 else NC - 1 - n
                    cb_ = it % 2
                    it += 1
                    rows = slice(n * 128, (n + 1) * 128)
                    csl = slice(cc_ * 128, (cc_ + 1) * 128)
                    Pg.dma('sp', v32[cb_][:], seq.U[rows, C_MLV:C_MLV + 512], writes=['b_v32%d' % cb_])
                    Pg.dma('sp', gt[cb_][:], (seq.GTF if sweep == 0 else seq.GTB)[rows, :], writes=['b_gt%d' % cb_])
                    if sweep == 1:
                        Pg.dma('act', hf[cb_][:], seq.HF[rows, :], writes=['b_hf%d' % cb_])
                        Pg.dma('act', o32[cb_][:], seq.U[rows, C_MLO:C_MLO + 512], writes=['b_o32%d' % cb_])
                        Pg.dma('act', z32[cb_][:], seq.U[rows, C_MLZ:C_MLZ + 512], writes=['b_z32%d' % cb_])
                    Pg.op('pool', lambda e: e.tensor_copy(out=vaug[cb_][:, :, 0:128],
                                                          in_=v32[cb_][:].rearrange('p (a d) -> p a d', a=4)),
                          reads=['b_v32%d' % cb_], writes=['b_vaug%d' % cb_])
                    hd = []
                    for h in range(4):
                        hd.append(dict(r=h + 4 * sweep, kT=qk16[wb][:, 4 + h, csl], qT=qk16[wb][:, h, csl],
                                       kwt=gt[cb_][:, h:h + 1], ccol=gt[cb_][:, 4 + h:5 + h], encol=gt[cb_][:, 8 + h:9 + h],
                                       hs=slice(h * 128, (h + 1) * 128)))
                    for h in range(4):
                        X = hd[h]
                        Pg.op('pe', lambda e: e.transpose(kt4[:, X['hs']], X['kT'], G.identb[:]),
                              reads=[('b_qk16%d' % wb, 4 + h)], writes=['b_kt4'])
                        Pg.op('pe', lambda e: e.matmul(sT4[:, X['hs']], lhsT=X['kT'], rhs=X['qT'], start=True, stop=True),
                              reads=[('b_qk16%d' % wb, 4 + h), ('b_qk16%d' % wb, h)], writes=['b_sT4'])
                    for h in range(4):
                        X = hd[h]
                        Pg.op('act', lambda e: e.mul(out=kw16[h][:], in_=kt4[:, X['hs']], mul=X['kwt']),
                              reads=['b_kt4', 'b_gt%d' % cb_], writes=[('b_kw16', h)])
                        Pg.op('dve', lambda e: e.scalar_tensor_tensor(out=pT[h][:], in0=sT4[:, X['hs']], scalar=X['kwt'], in1=mask[:],
                                                                      op0=ALU.mult, op1=ALU.mult),
                              reads=['b_sT4', 'b_gt%d' % cb_, 'b_maskf', 'b_maskb'], writes=[('b_pT', h)])
                        Pg.op('dve', lambda e: e.tensor_scalar(out=S32[:, h, :], in0=S32[:, h, :], scalar1=abc[:, X['r'], M:M + 1],
                                                               scalar2=None, op0=ALU.mult),
                              reads=[('b_S32', h), 'b_S32', 'b_abc'], writes=[('b_S32', h)])
                        Pg.op('act', lambda e: e.mul(out=S16[:, h, :], in_=S32[:, h, :], mul=S_ML),
                              reads=[('b_S32', h)], writes=[('b_S16', h)])
                    for h in range(4):
                        X = hd[h]
                        oap = o2[h // 2][:, h % 2, 0:129]
                        kap = kv2[h // 2][:, h % 2, 0:129]
                        Pg.op('pe', lambda e: e.matmul(oap, lhsT=pT[h][:], rhs=vaug[cb_][:, h, :], start=True, stop=False),
                              reads=[('b_pT', h), 'b_vaug%d' % cb_], writes=[('b_o2', h // 2)])
                        Pg.op('pe', lambda e: e.matmul(oap, lhsT=X['qT'], rhs=S16[:, h, :], start=False, stop=True),
                              reads=[('b_qk16%d' % wb, h), ('b_S16', h)], writes=[('b_o2', h // 2)])
                        Pg.op('pe', lambda e: e.matmul(kap, lhsT=kw16[h][:], rhs=vaug[cb_][:, h, :], start=True, stop=True),
                              reads=[('b_kw16', h), 'b_vaug%d' % cb_], writes=[('b_kv2', h // 2)])
                    for h in range(4):
                        X = hd[h]
                        oap = o2[h // 2][:, h % 2, 0:129]
                        kap = kv2[h // 2][:, h % 2, 0:129]
                        d = dd[h]
                        dk = ('b_dd', h)
                        Pg.op('act', lambda e: e.activation(out=d[:, 0:1], in_=oap[:, 128:129], func=AF.Abs, scale=X['ccol']),
                              reads=[('b_o2', h // 2), 'b_gt%d' % cb_], writes=[dk])
                        Pg.op('dve', lambda e: e.tensor_tensor(out=S32[:, h, :], in0=S32[:, h, :], in1=kap, op=ALU.add),
                              reads=[('b_S32', h), ('b_kv2', h // 2)], writes=[('b_S32', h)])
                        Pg.op('dve', lambda e: e.tensor_tensor(out=d[:, 1:2], in0=d[:, 0:1], in1=X['encol'], op=ALU.max),
                              reads=[dk, 'b_gt%d' % cb_], writes=[dk])
                        Pg.op('dve', lambda e: e.reciprocal(out=d[:, 2:3], in_=d[:, 1:2]), reads=[dk], writes=[dk])
                        Pg.op('dve', lambda e: e.tensor_tensor(out=d[:, 3:4], in0=d[:, 2:3], in1=X['ccol'], op=ALU.mult),
                              reads=[dk, 'b_gt%d' % cb_], writes=[dk])
                        Pg.op('act', lambda e: e.mul(out=hout[cb_][:, X['hs']], in_=oap[:, 0:128], mul=d[:, 3:4]),
                              reads=[('b_o2', h // 2), dk], writes=[('b_hout%d' % cb_, h)])
                    hk = [('b_hout%d' % cb_, h) for h in range(4)]
                    if sweep == 0:
                        Pg.dma('pool', seq.HF[rows, :], hout[cb_][:], reads=hk)
                    else:
                        Pg.op('dve', lambda e: e.tensor_tensor(out=hout[cb_][:], in0=hout[cb_][:], in1=hf[cb_][:], op=ALU.add),
                              reads=hk + ['b_hf%d' % cb_], writes=['b_hsum%d' % cb_])
                        Pg.op('act', lambda e: e.activation(out=o32[cb_][:], in_=o32[cb_][:], func=AF.Sigmoid),
                              reads=['b_o32%d' % cb_], writes=['b_o32%d' % cb_])
                        Pg.op('dve', lambda e: e.tensor_tensor(out=hout[cb_][:], in0=hout[cb_][:], in1=o32[cb_][:], op=ALU.mult),
                              reads=['b_hsum%d' % cb_, 'b_o32%d' % cb_], writes=['b_hsum%d' % cb_])
                        headnorm_gate(G, 'bn_', hout[cb_], ['b_hsum%d' % cb_], z32[cb_][:], 'b_z32%d' % cb_, gbc, 'b_gbc',
                                      seq.MIX[rows, 0:512], 'pool')
            Pg.barrier()
    Pg.barrier()


def phase_R(G, layer, seq):
    nc, Pg = G.nc, G.P
    L = seq.L
    NC = L // 128
    g128f = [math.exp(128.0 * x) for x in RT_LGF]
    g128b = [math.exp(128.0 * x) for x in RT_LGB]
    with ExitStack() as es:
        dct = sb(nc, es, 'r_dct', [128, 4, 128], F32)
        qwf = sb(nc, es, 'r_qwf', [128, 4, 128], F32)
        qwb = sb(nc, es, 'r_qwb', [128, 4, 128], F32)
        kwf = sb(nc, es, 'r_kwf', [128, 4], F32)
        kwb = sb(nc, es, 'r_kwb', [128, 4], F32)
        gbc = sb(nc, es, 'r_gbc', [128, 512], F32)
        qkv = [sb(nc, es, 'r_qkv%d' % i, [128, 1536], F32) for i in range(2)]
        rot = [sb(nc, es, 'r_rot%d' % i, [128, 128], F32) for i in range(2)]
        ta = [sb(nc, es, 'r_ta%d' % i, [128, 256], F32) for i in range(2)]
        tb = [sb(nc, es, 'r_tb%d' % i, [128, 256], F32) for i in range(2)]
        kr32 = [sb(nc, es, 'r_kr32%d' % i, [128, 512], F32) for i in range(2)]
        q16 = [sb(nc, es, 'r_q16%d' % i, [128, 512], BF16) for i in range(2)]
        k16 = [sb(nc, es, 'r_k16%d' % i, [128, 512], BF16) for i in range(2)]
        v16 = [sb(nc, es, 'r_v16%d' % i, [128, 512], BF16) for i in range(2)]
        qT = [sb(nc, es, 'r_qT%d' % i, [128, 4, 128], BF16) for i in range(2)]
        kT = [sb(nc, es, 'r_kT%d' % i, [128, 4, 128], BF16) for i in range(2)]
        pT = [sb(nc, es, 'r_pT%d' % i, [128, 128], BF16) for i in range(4)]
        qw16 = [sb(nc, es, 'r_qw16%d' % i, [128, 128], BF16) for i in range(4)]
        kw16 = [sb(nc, es, 'r_kw16%d' % i, [128, 128], BF16) for i in range(4)]
        R32 = sb(nc, es, 'r_R32', [128, 4, 128], F32)
        R16 = sb(nc, es, 'r_R16', [128, 4, 128], BF16)
        outt = [sb(nc, es, 'r_out%d' % i, [128, 512], F32) for i in range(2)]
        rf = [sb(nc, es, 'r_rf%d' % i, [128, 512], F32) for i in range(2)]
        z32 = [sb(nc, es, 'r_z32%d' % i, [128, 512], F32) for i in range(2)]
        alloc_hn(G, es, 'rn_')
        qt_ps = ps(nc, es, 'r_qtps', [128, 512], BF16)
        kt_ps = ps(nc, es, 'r_ktps', [128, 512], BF16)
        sT4 = ps(nc, es, 'r_sT4', [128, 512])
        o_ps = [ps(nc, es, 'r_ops%d' % i, [128, 512]) for i in range(2)]
        kv4 = ps(nc, es, 'r_kv4', [128, 512])
        for nm, t_ in (('dct', dct), ('qwf', qwf), ('qwb', qwb)):
            Pg.dma('sp', t_[:], G.c[nm][:, :, :], writes=['r_' + nm])
        Pg.dma('sp', kwf[:], G.c['kwf'][:, :], writes=['r_kwf'])
        Pg.dma('sp', kwb[:], G.c['kwb'][:, :], writes=['r_kwb'])
        Pg.dma('sp', gbc[:], bcast_rows(G.w['rt_norm_g'][layer], 128), writes=['r_gbc'])
        it = 0
        for sweep in range(2):
            Pg.op('pool', lambda e: e.memset(R32[:], 0.0), writes=['r_R32'])
            Pg.op('pool', lambda e: e.memset(R16[:], 0.0), reads=['r_R16'], writes=[('r_R16', h) for h in range(4)] + ['r_R16'])
            order = range(NC) if sweep == 0 else range(NC - 1, -1, -1)
            for n in order:
                b = it % 2
                it += 1
                rows = slice(n * 128, (n + 1) * 128)
                Pg.dma('sp', qkv[b][:], seq.U[rows, C_RQ:C_RQ + 1536], writes=['r_qkv%d' % b])
                Pg.dma('act', rot[b][:], G.c['rot'][rows, :], writes=['r_rot%d' % b])
                if sweep == 1:
                    Pg.dma('act', rf[b][:], seq.HF[rows, :], writes=['r_rf%d' % b])
                    Pg.dma('act', z32[b][:], seq.U[rows, C_RZ:C_RZ + 512], writes=['r_z32%d' % b])
                cosb = rot[b][:, 0:64].unsqueeze(1).to_broadcast([128, 4, 64])
                sinb = rot[b][:, 64:128].unsqueeze(1).to_broadcast([128, 4, 64])
                for qi, (eng, src0, dst, dkey) in enumerate((('dve', 0, q16[b], 'r_q16%d' % b), ('pool', 512, kr32[b], 'r_kr32%d' % b))):
                    xv = qkv[b][:, src0:src0 + 512].rearrange('p (a t d) -> p a t d', a=4, t=2)
                    dv = dst[:].rearrange('p (a t d) -> p a t d', a=4, t=2)
                    A_ = ta[qi][:].rearrange('p (a d) -> p a d', a=4)
                    B_ = tb[qi][:].rearrange('p (a d) -> p a d', a=4)
                    ka, kb = 'r_ta%d' % qi, 'r_tb%d' % qi
                    rk = ['r_qkv%d' % b, 'r_rot%d' % b]
                    Pg.op(eng, lambda e: e.tensor_tensor(out=A_, in0=xv[:, :, 0, :], in1=cosb, op=ALU.mult), reads=rk, writes=[ka])
                    Pg.op(eng, lambda e: e.tensor_tensor(out=B_, in0=xv[:, :, 1, :], in1=sinb, op=ALU.mult), reads=rk, writes=[kb])
                    Pg.op(eng, lambda e: e.tensor_tensor(out=dv[:, :, 0, :], in0=A_, in1=B_, op=ALU.subtract),
                          reads=[ka, kb], writes=[dkey])
                    Pg.op(eng, lambda e: e.tensor_tensor(out=A_, in0=xv[:, :, 0, :], in1=sinb, op=ALU.mult), reads=rk + [dkey], writes=[ka])
                    Pg.op(eng, lambda e: e.tensor_tensor(out=B_, in0=xv[:, :, 1, :], in1=cosb, op=ALU.mult), reads=rk + [dkey], writes=[kb])
                    Pg.op(eng, lambda e: e.tensor_tensor(out=dv[:, :, 1, :], in0=A_, in1=B_, op=ALU.add),
                          reads=[ka, kb], writes=[dkey])
                Pg.op('act', lambda e: e.copy(out=k16[b][:], in_=kr32[b][:]), reads=['r_kr32%d' % b], writes=['r_k16%d' % b])
                Pg.op('act', lambda e: e.copy(out=v16[b][:], in_=qkv[b][:, 1024:1536]), reads=['r_qkv%d' % b], writes=['r_v16%d' % b])
                for h in range(4):
                    Pg.op('pe', lambda e: e.transpose(qt_ps[:, h * 128:(h + 1) * 128], q16[b][:, h * 128:(h + 1) * 128], G.identb[:]),
                          reads=['r_q16%d' % b], writes=['r_qtps'])
                    Pg.op('pe', lambda e: e.transpose(kt_ps[:, h * 128:(h + 1) * 128], k16[b][:, h * 128:(h + 1) * 128], G.identb[:]),
                          reads=['r_k16%d' % b], writes=['r_ktps'])
                Pg.op('act', lambda e: e.copy(out=qT[b][:].rearrange('p a t -> p (a t)'), in_=qt_ps[:]), reads=['r_qtps'], writes=['r_qT%d' % b])
                Pg.op('dve', lambda e: e.tensor_copy(out=kT[b][:].rearrange('p a t -> p (a t)'), in_=kt_ps[:]), reads=['r_ktps'], writes=['r_kT%d' % b])
                ob = o_ps[b]
                qw = qwf if sweep == 0 else qwb
                kwt_ = kwf if sweep == 0 else kwb
                g128l = g128f if sweep == 0 else g128b
                hss = [slice(h * 128, (h + 1) * 128) for h in range(4)]
                for h in range(4):
                    Pg.op('pool', lambda e: e.tensor_tensor(out=qw16[h][:], in0=qT[b][:, h, :], in1=qw[:, h, :], op=ALU.mult),
                          reads=['r_qT%d' % b, 'r_qwf', 'r_qwb'], writes=[('r_qw16', h)])
                    Pg.op('act', lambda e: e.mul(out=kw16[h][:], in_=kr32[b][:, hss[h]], mul=kwt_[:, h:h + 1]),
                          reads=['r_kr32%d' % b, 'r_kwf', 'r_kwb'], writes=[('r_kw16', h)])
                    if sweep == 0:
                        Pg.op('pe', lambda e: e.matmul(sT4[:, hss[h]], lhsT=kT[b][:, h, :], rhs=qT[b][:, h, :], start=True, stop=True),
                              reads=['r_kT%d' % b, 'r_qT%d' % b], writes=['r_sT4'])
                if sweep == 0:
                    for h in range(4):
                        Pg.op('dve', lambda e: e.tensor_tensor(out=pT[h][:], in0=sT4[:, hss[h]], in1=dct[:, h, :], op=ALU.mult),
                              reads=['r_sT4', 'r_dct'], writes=[('r_pT', h)])
                for h in range(4):
                    if sweep == 0:
                        Pg.op('pe', lambda e: e.matmul(ob[:, hss[h]], lhsT=pT[h][:], rhs=v16[b][:, hss[h]], start=True, stop=False),
                              reads=[('r_pT', h), 'r_v16%d' % b], writes=['r_ops%d' % b])
                    Pg.op('pe', lambda e: e.matmul(ob[:, hss[h]], lhsT=qw16[h][:], rhs=R16[:, h, :], start=(sweep == 1), stop=True),
                          reads=[('r_qw16', h), ('r_R16', h)], writes=['r_ops%d' % b])
                    Pg.op('pe', lambda e: e.matmul(kv4[:, hss[h]], lhsT=kw16[h][:], rhs=v16[b][:, hss[h]], start=True, stop=True),
                          reads=[('r_kw16', h), 'r_v16%d' % b], writes=['r_kv4'])
                for h in range(4):
                    Pg.op('dve', lambda e: e.scalar_tensor_tensor(out=R32[:, h, :], in0=R32[:, h, :], scalar=g128l[h], in1=kv4[:, hss[h]],
                                                                  op0=ALU.mult, op1=ALU.add),
                          reads=['r_R32', ('r_R32', h), 'r_kv4'], writes=[('r_R32', h)])
                    Pg.op('act', lambda e: e.copy(out=R16[:, h, :], in_=R32[:, h, :]), reads=[('r_R32', h)], writes=[('r_R16', h)])
                if sweep == 0:
                    Pg.op('act', lambda e: e.copy(out=outt[b][:], in_=ob[:]), reads=['r_ops%d' % b], writes=['r_out%d' % b])
                    Pg.dma('pool', seq.HF[rows, :], outt[b][:], reads=['r_out%d' % b])
                else:
                    Pg.op('dve', lambda e: e.tensor_tensor(out=outt[b][:], in0=ob[:], in1=rf[b][:], op=ALU.add),
                          reads=['r_ops%d' % b, 'r_rf%d' % b], writes=['r_out%d' % b])
                    headnorm_gate(G, 'rn_', outt[b], ['r_out%d' % b], z32[b][:], 'r_z32%d' % b, gbc, 'r_gbc',
                                  seq.MIX[rows, 512:1024], 'pool')
            Pg.barrier()
    Pg.barrier()


def conv3_tile(G, pre, seq, col0, stream, t, cwb, cbb, um, u0, up, acc, q):
    Pg = G.P
    L = seq.L
    r0 = t * 128
    ks = [pre + 'um', pre + 'u0', pre + 'up']
    if r0 == 0:
        Pg.op('pool', lambda e: e.memset(um[:], 0.0), writes=[ks[0]])
        Pg.dma(q, um[1:128, :], seq.U[0:127, col0:col0 + 512], reads=[ks[0]], writes=[ks[0] + 'd'])
    else:
        Pg.dma(q, um[:], seq.U[r0 - 1:r0 + 127, col0:col0 + 512], writes=[ks[0], ks[0] + 'd'])
    Pg.dma(q, u0[:], seq.U[r0:r0 + 128, col0:col0 + 512], writes=[ks[1]])
    if r0 + 128 == L:
        Pg.op('pool', lambda e: e.memset(up[:], 0.0), writes=[ks[2]])
        Pg.dma(q, up[0:127, :], seq.U[r0 + 1:L, col0:col0 + 512], reads=[ks[2]], writes=[ks[2] + 'd'])
    else:
        Pg.dma(q, up[:], seq.U[r0 + 1:r0 + 129, col0:col0 + 512], writes=[ks[2], ks[2] + 'd'])
    ak = pre + 'acc'
    Pg.op('dve', lambda e: e.tensor_tensor(out=acc[:], in0=um[:], in1=cwb[:, stream, 0, :], op=ALU.mult),
          reads=[ks[0], ks[0] + 'd', 'h_cwb'], writes=[ak])
    Pg.op('pool', lambda e: e.tensor_tensor(out=u0[:], in0=u0[:], in1=cwb[:, stream, 1, :], op=ALU.mult),
          reads=[ks[1], 'h_cwb'], writes=[ks[1]])
    Pg.op('pool', lambda e: e.tensor_tensor(out=up[:], in0=up[:], in1=cwb[:, stream, 2, :], op=ALU.mult),
          reads=[ks[2], ks[2] + 'd', 'h_cwb'], writes=[ks[2]])
    Pg.op('dve', lambda e: e.tensor_tensor(out=acc[:], in0=acc[:], in1=u0[:], op=ALU.add), reads=[ak, ks[1]], writes=[ak])
    Pg.op('dve', lambda e: e.tensor_tensor(out=acc[:], in0=acc[:], in1=up[:], op=ALU.add), reads=[ak, ks[2]], writes=[ak])
    Pg.op('dve', lambda e: e.tensor_tensor(out=acc[:], in0=acc[:], in1=cbb[:, stream, :], op=ALU.add), reads=[ak, 'h_cbb'], writes=[ak])


def phase_H(G, layer, seq):
    nc, Pg = G.nc, G.P
    L, nb = seq.L, seq.nb
    npc = 2 * nb - 1
    feat = G.c['feat_' + seq.name]
    ntn = G.c['ntn_' + seq.name]
    HPI = math.pi / 2
    with ExitStack() as es0:
        cwb = sb(nc, es0, 'h_cwb', [128, 3, 3, 512], F32)
        cbb = sb(nc, es0, 'h_cbb', [128, 3, 512], F32)
        skb = sb(nc, es0, 'h_skb', [128, 2, 512], F32)
        RNt = sb(nc, es0, 'h_RNt', [128, 512], F32)
        for st in range(3):
            for k3 in range(3):
                Pg.dma('sp', cwb[:, st, k3, :], bcast_rows(G.w['hy_conv_w'][layer, k3, st * 512:(st + 1) * 512], 128), writes=['h_cwb'])
            Pg.dma('sp', cbb[:, st, :], bcast_rows(G.w['hy_conv_b'][layer, st * 512:(st + 1) * 512], 128), writes=['h_cbb'])
        for o in range(2):
            Pg.dma('sp', skb[:, o, :], bcast_rows(G.w['hy_skip'][layer, o], 128), writes=['h_skb'])
        Pg.barrier()
        for o in range(2):
            with ExitStack() as es:
                w1 = sb(nc, es, 's_w1', [33, 64], F32)
                w2 = sb(nc, es, 's_w2', [64, 64], F32)
                w3 = sb(nc, es, 's_w3', [64, 2, 512], F32)
                fr = sb(nc, es, 's_fr', [64, 1], F32)
                fb1 = sb(nc, es, 's_fb1', [64, 1], F32)
                fb2 = sb(nc, es, 's_fb2', [64, 1], F32)
                absd = sb(nc, es, 's_absd', [128, 2, 512], F32)
                ones = sb(nc, es, 's_ones', [128, 128], F32)
                ftT = [sb(nc, es, 's_ftT%d' % i, [33, 512], F32) for i in range(2)]
                ntt = sb(nc, es, 's_ntt', [128, 32], F32)
                zt = sb(nc, es, 's_zt', [64, 512], F32)
                ct = sb(nc, es, 's_ct', [64, 512], F32)
                hid1 = sb(nc, es, 's_hid1', [64, 512], F32)
                hid2 = sb(nc, es, 's_hid2', [64, 512], F32)
                Et = [sb(nc, es, 's_E%d' % i, [128, 512], F32) for i in range(2)]
                g32 = [sb(nc, es, 's_g32%d' % i, [128, 512], F32) for i in range(2)]
                gab = [sb(nc, es, 's_gab%d' % i, [128, 512], F32) for i in range(2)]
                GT16 = sb(nc, es, 's_GT16', [128, 64, 512], BF16)
                FT = [sb(nc, es, 's_FT%d' % i, [128, 64, 2, 128], BF16) for i in range(2)]
                go16 = [sb(nc, es, 's_go16%d' % i, [128, 2, 512], BF16) for i in range(2)]
                m_ps = ps(nc, es, 's_mps', [64, 512])
                f_ps = [ps(nc, es, 's_fps%d' % i, [128, 512]) for i in range(2)]
                b0_ps = ps(nc, es, 's_b0ps', [1, 512])
                n_ps = ps(nc, es, 's_nps', [128, 512])
                x_ps = [ps(nc, es, 's_xps%d' % i, [128, 512]) for i in range(2)]
                Pg.dma('sp', w1[:], G.w['hy_w1'][layer], writes=['s_w1'])
                Pg.dma('sp', w2[:], G.w['hy_w2'][layer], writes=['s_w2'])
                Pg.dma('sp', w3[:], G.w['hy_w3'][layer, :, o * 1024:(o + 1) * 1024].rearrange('k (d c) -> k d c', d=2), writes=['s_w3'])
                Pg.dma('sp', fr[:], G.w['hy_freq'][layer].unsqueeze(1), writes=['s_fr'])
                Pg.dma('sp', fb1[:], G.w['hy_b1'][layer].unsqueeze(1), writes=['s_fb1'])
                Pg.dma('sp', fb2[:], G.w['hy_b2'][layer].unsqueeze(1), writes=['s_fb2'])
                for dr in range(2):
                    Pg.dma('sp', absd[:, dr, :], bcast_rows(G.w['hy_deltas'][layer, o, dr], 128), writes=['s_absd'])
                Pg.op('act', lambda e: e.activation(out=absd[:].rearrange('p a c -> p (a c)'), in_=absd[:].rearrange('p a c -> p (a c)'),
                                                    func=AF.Abs), reads=['s_absd'], writes=['s_absd'])
                Pg.op('dve', lambda e: e.tensor_tensor(out=fb1[:], in0=fb1[:], in1=fr[:], op=ALU.mult), reads=['s_fb1', 's_fr'], writes=['s_fb1'])
                Pg.op('dve', lambda e: e.tensor_tensor(out=fb2[:], in0=fb2[:], in1=fr[:], op=ALU.mult), reads=['s_fb2', 's_fr'], writes=['s_fb2'])
                Pg.op('pool', lambda e: e.memset(ones[:], 1.0), writes=['s_ones'])
                norm_tiles = [(pi, hf) for pi in range(npc) for hf in range(2) if hf == 0 or pi == 0]
                nleft = len(norm_tiles) * 32
                ncount = 0
                ti = 0
                xi_ = 0
                for pi in range(npc):
                    d = pi - (nb - 1)
                    for hf in range(2):
                        dr = 0 if ((hf == 0 and d >= 0) or (hf == 1 and d >= 1)) else 1
                        Pg.dma('act', ntt[:], ntn[pi, hf], writes=['s_ntt'])
                        for grp in range(8):
                            fb_ = grp % 2
                            Pg.dma('act', ftT[fb_][:], feat[pi, hf, :, grp * 512:(grp + 1) * 512], writes=['s_ftT%d' % fb_])
                            src_keys = ['s_ftT%d' % fb_]
                            rhs_ = ftT[fb_]
                            for (wm, fbm, hid, wk, hk) in ((w1, fb1, hid1, 's_w1', 's_hid1'), (w2, fb2, hid2, 's_w2', 's_hid2')):
                                Pg.op('pe', lambda e: e.matmul(m_ps[:], lhsT=wm[:], rhs=rhs_[:], start=True, stop=True),
                                      reads=src_keys + [wk], writes=['s_mps'])
                                Pg.op('dve', lambda e: e.tensor_scalar(out=zt[:], in0=m_ps[:], scalar1=fr[:, 0:1], scalar2=fbm[:, 0:1],
                                                                       op0=ALU.mult, op1=ALU.add),
                                      reads=['s_mps', 's_fr', 's_fb1', 's_fb2'], writes=['s_zt'])
                                for _ in range(2):
                                    Pg.op('dve', lambda e: e.tensor_scalar(out=ct[:], in0=zt[:], scalar1=-HPI, scalar2=HPI,
                                                                           op0=ALU.max, op1=ALU.min), reads=['s_zt'], writes=['s_ct'])
                                    Pg.op('dve', lambda e: e.scalar_tensor_tensor(out=zt[:], in0=ct[:], scalar=2.0, in1=zt[:],
                                                                                  op0=ALU.mult, op1=ALU.subtract),
                                          reads=['s_ct', 's_zt'], writes=['s_zt'])
                                Pg.op('act', lambda e: e.activation(out=hid[:], in_=zt[:], func=AF.Sin), reads=['s_zt'], writes=[hk])
                                src_keys = [hk]
                                rhs_ = hid
                            for tt in range(4):
                                tile_i = grp * 4 + tt
                                b2 = ti % 2
                                ti += 1
                                Pg.op('pe', lambda e: e.matmul(f_ps[b2][:], lhsT=hid2[:, tt * 128:(tt + 1) * 128], rhs=w3[:, dr, :],
                                                               start=True, stop=True), reads=['s_hid2', 's_w3'], writes=['s_fps%d' % b2])
                                Pg.op('act', lambda e: e.activation(out=Et[b2][:], in_=absd[:, dr, :], func=AF.Exp,
                                                                    scale=ntt[:, tile_i:tile_i + 1]),
                                      reads=['s_absd', 's_ntt'], writes=['s_E%d' % b2])
                                Pg.op('dve', lambda e: e.tensor_tensor(out=g32[b2][:], in0=f_ps[b2][:], in1=Et[b2][:], op=ALU.mult),
                                      reads=['s_fps%d' % b2, 's_E%d' % b2], writes=['s_g32%d' % b2])
                                if d == 0 and hf == 0 and tile_i == 0:
                                    Pg.op('pe', lambda e: e.matmul(b0_ps[:], lhsT=hid2[:, 0:1], rhs=w3[:, 1, :], start=True, stop=True),
                                          reads=['s_hid2', 's_w3'], writes=['s_b0ps'])
                                    Pg.op('dve', lambda e: e.tensor_tensor(out=g32[b2][0:1, :], in0=g32[b2][0:1, :], in1=b0_ps[:], op=ALU.add),
                                          reads=['s_g32%d' % b2, 's_b0ps'], writes=['s_g32%d' % b2])
                                if hf == 1 and tile_i == 0:
                                    Pg.op('dve', lambda e: e.memset(g32[b2][0:1, :], 0.0), reads=['s_g32%d' % b2], writes=['s_g32%d' % b2])
                                if (pi, hf) in norm_tiles:
                                    Pg.op('act', lambda e: e.activation(out=gab[b2][:], in_=g32[b2][:], func=AF.Abs),
                                          reads=['s_g32%d' % b2], writes=['s_gab%d' % b2])
                                    Pg.op('pe', lambda e: e.matmul(n_ps[:], lhsT=ones[:], rhs=gab[b2][:], start=(ncount == 0),
                                                                   stop=(ncount == nleft - 1)), reads=['s_gab%d' % b2, 's_ones'], writes=['s_nps'])
                                    ncount += 1
                                Pg.op('pool', lambda e: e.tensor_copy(out=GT16[:, hf * 32 + tile_i, :], in_=g32[b2][:]),
                                      reads=['s_g32%d' % b2], writes=[('s_GT16', hf * 32 + tile_i)])
                    gkeys = [('s_GT16', i) for i in range(64)]
                    for kc in range(5 if USE_CC else NKC):
                        fb_ = xi_ % 2
                        xi_ += 1
                        Pg.dma('sp' if kc % 2 == 0 else 'pool', FT[fb_][:], (G.c['fwd_own'] if USE_CC else G.c['fwd'])[kc],
                               writes=['s_FT%d' % fb_])
                        for ri in range(2):
                            for sc in range(64):
                                Pg.op('pe', lambda e: e.matmul(x_ps[ri][:], lhsT=FT[fb_][:, sc, ri, :], rhs=GT16[:, sc, :],
                                                               start=(sc == 0), stop=(sc == 63)),
                                      reads=['s_FT%d' % fb_] + (gkeys if sc in (0, 63) else []), writes=['s_xps%d' % ri])
                        Pg.op('act', lambda e: e.copy(out=go16[fb_][:, 0, :], in_=x_ps[0][:]), reads=['s_xps0'], writes=[('s_go%d' % fb_, 0)])
                        Pg.op('dve', lambda e: e.tensor_copy(out=go16[fb_][:, 1, :], in_=x_ps[1][:]), reads=['s_xps1'], writes=[('s_go%d' % fb_, 1)])
                        if nb == 1 and USE_CC:
                            gdst = G.GSPa[kc * 128:(kc + 1) * 128, :].rearrange('p (r c) -> p r c', r=2)
                        else:
                            gdst = seq.GS[pi, kc]
                        Pg.dma('act', gdst, go16[fb_][:], reads=[('s_go%d' % fb_, 0), ('s_go%d' % fb_, 1)])
                Pg.op('dve', lambda e: e.reciprocal(out=RNt[:], in_=n_ps[:]), reads=['s_nps'], writes=['h_RNt'])
            if nb == 1 and USE_CC:
                Pg.allgather(G.GSPa, G.GSAa, G.ccdummy)
            Pg.barrier()
            with ExitStack() as es:
                um = sb(nc, es, 'f_um', [128, 512], F32)
                u0 = sb(nc, es, 'f_u0', [128, 512], F32)
                up = sb(nc, es, 'f_up', [128, 512], F32)
                acc = sb(nc, es, 'f_acc', [128, 512], F32)
                v16 = sb(nc, es, 'f_v16', [128, 32, 512], BF16)
                FT = [sb(nc, es, 'f_FT%d' % i, [128, 32, 2, 128], BF16) for i in range(2)]
                xo16 = [sb(nc, es, 'f_xo16%d' % i, [128, 2, 512], BF16) for i in range(2)]
                x_ps = [ps(nc, es, 'f_xps%d' % i, [128, 512]) for i in range(2)]
                xi_ = 0
                for b in range(nb):
                    for t in range(32):
                        tg = b * 32 + t
                        if o == 0:
                            conv3_tile(G, 'f_', seq, C_HV, 0, tg, cwb, cbb, um, u0, up, acc, 'sp')
                        else:
                            Pg.dma('sp', acc[:], seq.Z1[tg * 128:(tg + 1) * 128, :], writes=['f_acc'])
                        Pg.op('act', lambda e: e.copy(out=v16[:, t, :], in_=acc[:]), reads=['f_acc'], writes=[('f_v16', t)])
                    vkeys = [('f_v16', t) for t in range(32)]
                    for kc in range(NKC if (nb == 1 or not USE_CC) else 5):
                        fb_ = xi_ % 2
                        xi_ += 1
                        Pg.dma('sp' if kc % 2 == 0 else 'pool', FT[fb_][:], (G.c['fwd'] if (nb == 1 or not USE_CC) else G.c['fwd_own'])[kc, :, 0:32],
                               writes=['f_FT%d' % fb_])
                        for ri in range(2):
                            for sc in range(32):
                                Pg.op('pe', lambda e: e.matmul(x_ps[ri][:], lhsT=FT[fb_][:, sc, ri, :], rhs=v16[:, sc, :],
                                                               start=(sc == 0), stop=(sc == 31)),
                                      reads=['f_FT%d' % fb_] + (vkeys if sc in (0, 31) else []), writes=['f_xps%d' % ri])
                        Pg.op('act', lambda e: e.copy(out=xo16[fb_][:, 0, :], in_=x_ps[0][:]), reads=['f_xps0'], writes=[('f_xo%d' % fb_, 0)])
                        Pg.op('dve', lambda e: e.tensor_copy(out=xo16[fb_][:, 1, :], in_=x_ps[1][:]), reads=['f_xps1'], writes=[('f_xo%d' % fb_, 1)])
                        Pg.dma('act', seq.XS[b, kc], xo16[fb_][:], reads=[('f_xo%d' % fb_, 0), ('f_xo%d' % fb_, 1)])
            Pg.barrier()
            with ExitStack() as es:
                Y16 = sb(nc, es, 'i_Y16', [128, NKC, 2, 512], BF16)
                Xt = [sb(nc, es, 'i_X%d' % i, [128, 2, 512], BF16) for i in range(2)]
                Gt = [sb(nc, es, 'i_G%d' % i, [128, 2, 512], BF16) for i in range(2)]
                Yr = sb(nc, es, 'i_Yr', [128, 512], F32)
                Yi = sb(nc, es, 'i_Yi', [128, 512], F32)
                t1 = sb(nc, es, 'i_t1', [128, 512], F32)
                t2 = sb(nc, es, 'i_t2', [128, 512], F32)
                t3 = sb(nc, es, 'i_t3', [128, 512], F32)
                t4 = sb(nc, es, 'i_t4', [128, 512], F32)
                IT = [sb(nc, es, 'i_IT%d' % i, [128, NKC, 2, 128], BF16) for i in range(2)]
                um = sb(nc, es, 'i_um', [128, 512], F32)
                u0 = sb(nc, es, 'i_u0', [128, 512], F32)
                up = sb(nc, es, 'i_up', [128, 512], F32)
                vacc = sb(nc, es, 'i_vacc', [128, 512], F32)
                xacc = sb(nc, es, 'i_xacc', [128, 512], F32)
                yt = sb(nc, es, 'i_yt', [128, 512], F32)
                zt_ = sb(nc, es, 'i_zt', [128, 512], F32)
                hz = sb(nc, es, 'i_hz', [128, 512], F32)
                y_ps = [ps(nc, es, 'i_yps%d' % i, [128, 512]) for i in range(2)]
                pi_ = 0
                if nb > 1 and USE_CC:
                    Yp = [sb(nc, es, 'i_Yp%d' % i, [128, 2, 512], BF16) for i in range(2)]
                    for a in range(nb):
                        for kc in range(5):
                            for b in range(nb):
                                xb_ = pi_ % 2
                                pi_ += 1
                                Pg.dma('sp', Xt[xb_][:], seq.XS[b, kc], writes=['i_X%d' % xb_])
                                Pg.dma('act', Gt[xb_][:], seq.GS[a - b + nb - 1, kc], writes=['i_G%d' % xb_])
                                rk = ['i_X%d' % xb_, 'i_G%d' % xb_]
                                first = b == 0
                                Pg.op('dve', lambda e: e.tensor_tensor(out=(Yr if first else t1)[:], in0=Xt[xb_][:, 0, :], in1=Gt[xb_][:, 0, :], op=ALU.mult),
                                      reads=rk, writes=['i_Yr' if first else 'i_t1'])
                                Pg.op('pool', lambda e: e.tensor_tensor(out=t2[:], in0=Xt[xb_][:, 1, :], in1=Gt[xb_][:, 1, :], op=ALU.mult),
                                      reads=rk, writes=['i_t2'])
                                Pg.op('dve', lambda e: e.tensor_tensor(out=(Yi if first else t3)[:], in0=Xt[xb_][:, 0, :], in1=Gt[xb_][:, 1, :], op=ALU.mult),
                                      reads=rk, writes=['i_Yi' if first else 'i_t3'])
                                Pg.op('pool', lambda e: e.tensor_tensor(out=t4[:], in0=Xt[xb_][:, 1, :], in1=Gt[xb_][:, 0, :], op=ALU.mult),
                                      reads=rk, writes=['i_t4'])
                                if not first:
                                    Pg.op('dve', lambda e: e.tensor_tensor(out=Yr[:], in0=Yr[:], in1=t1[:], op=ALU.add), reads=['i_Yr', 'i_t1'], writes=['i_Yr'])
                                    Pg.op('pool', lambda e: e.tensor_tensor(out=Yi[:], in0=Yi[:], in1=t3[:], op=ALU.add), reads=['i_Yi', 'i_t3'], writes=['i_Yi'])
                                Pg.op('dve', lambda e: e.tensor_tensor(out=Yr[:], in0=Yr[:], in1=t2[:], op=ALU.subtract), reads=['i_Yr', 'i_t2'], writes=['i_Yr'])
                                Pg.op('pool', lambda e: e.tensor_tensor(out=Yi[:], in0=Yi[:], in1=t4[:], op=ALU.add), reads=['i_Yi', 'i_t4'], writes=['i_Yi'])
                            yb_ = (a * 5 + kc) % 2
                            Pg.op('act', lambda e: e.copy(out=Yp[yb_][:, 0, :], in_=Yr[:]), reads=['i_Yr'], writes=[('i_Yp%d' % yb_, 0)])
                            Pg.op('act', lambda e: e.copy(out=Yp[yb_][:, 1, :], in_=Yi[:]), reads=['i_Yi'], writes=[('i_Yp%d' % yb_, 1)])
                            r0_ = (a * 5 + kc) * 128
                            Pg.dma('sp', G.YP[r0_:r0_ + 128, :].rearrange('p (r c) -> p r c', r=2), Yp[yb_][:],
                                   reads=[('i_Yp%d' % yb_, 0), ('i_Yp%d' % yb_, 1)])
                    Pg.allgather(G.YP, G.YA, G.ccdummy)
                for a in range(nb):
                    for kc in range(NKC):
                        if nb > 1 and USE_CC:
                            r0_ = ((kc % 8) * 20 + a * 5 + kc // 8) * 128
                            Pg.dma('sp' if kc % 2 == 0 else 'act', Y16[:, kc, :, :],
                                   G.YA[r0_:r0_ + 128, :].rearrange('p (r c) -> p r c', r=2),
                                   writes=[('i_Y16', kc, 0), ('i_Y16', kc, 1)])
                            continue
                        for b in range(nb):
                            xb_ = pi_ % 2
                            pi_ += 1
                            Pg.dma('sp', Xt[xb_][:], seq.XS[b, kc], writes=['i_X%d' % xb_])
                            if USE_CC:
                                g0_ = ((kc % 8) * 5 + kc // 8) * 128
                                gsrc_ = G.GSAa[g0_:g0_ + 128, :].rearrange('p (r c) -> p r c', r=2)
                            else:
                                gsrc_ = seq.GS[a - b + nb - 1, kc]
                            Pg.dma('act', Gt[xb_][:], gsrc_, writes=['i_G%d' % xb_])
                            rk = ['i_X%d' % xb_, 'i_G%d' % xb_]
                            first = b == 0
                            Pg.op('dve', lambda e: e.tensor_tensor(out=(Yr if first else t1)[:], in0=Xt[xb_][:, 0, :], in1=Gt[xb_][:, 0, :], op=ALU.mult),
                                  reads=rk, writes=['i_Yr' if first else 'i_t1'])
                            Pg.op('pool', lambda e: e.tensor_tensor(out=t2[:], in0=Xt[xb_][:, 1, :], in1=Gt[xb_][:, 1, :], op=ALU.mult),
                                  reads=rk, writes=['i_t2'])
                            Pg.op('dve', lambda e: e.tensor_tensor(out=(Yi if first else t3)[:], in0=Xt[xb_][:, 0, :], in1=Gt[xb_][:, 1, :], op=ALU.mult),
                                  reads=rk, writes=['i_Yi' if first else 'i_t3'])
                            Pg.op('pool', lambda e: e.tensor_tensor(out=t4[:], in0=Xt[xb_][:, 1, :], in1=Gt[xb_][:, 0, :], op=ALU.mult),
                                  reads=rk, writes=['i_t4'])
                            if not first:
                                Pg.op('dve', lambda e: e.tensor_tensor(out=Yr[:], in0=Yr[:], in1=t1[:], op=ALU.add), reads=['i_Yr', 'i_t1'], writes=['i_Yr'])
                                Pg.op('pool', lambda e: e.tensor_tensor(out=Yi[:], in0=Yi[:], in1=t3[:], op=ALU.add), reads=['i_Yi', 'i_t3'], writes=['i_Yi'])
                            Pg.op('dve', lambda e: e.tensor_tensor(out=Yr[:], in0=Yr[:], in1=t2[:], op=ALU.subtract), reads=['i_Yr', 'i_t2'], writes=['i_Yr'])
                            Pg.op('pool', lambda e: e.tensor_tensor(out=Yi[:], in0=Yi[:], in1=t4[:], op=ALU.add), reads=['i_Yi', 'i_t4'], writes=['i_Yi'])
                        Pg.op('act', lambda e: e.copy(out=Y16[:, kc, 0, :], in_=Yr[:]), reads=['i_Yr'], writes=[('i_Y16', kc, 0)])
                        Pg.op('act', lambda e: e.copy(out=Y16[:, kc, 1, :], in_=Yi[:]), reads=['i_Yi'], writes=[('i_Y16', kc, 1)])
                    ykeys = [('i_Y16', kc, ri) for kc in range(NKC) for ri in range(2)]
                    for sc in range(32):
                        ib = sc % 2
                        tg = a * 32 + sc
                        rows = slice(tg * 128, (tg + 1) * 128)
                        Pg.dma('sp' if sc % 2 == 0 else 'pool', IT[ib][:], G.c['inv'][sc], writes=['i_IT%d' % ib])
                        n_mm = 0
                        for kc in range(NKC):
                            for ri in range(2):
                                Pg.op('pe', lambda e: e.matmul(y_ps[ib][:], lhsT=IT[ib][:, kc, ri, :], rhs=Y16[:, kc, ri, :],
                                                               start=(n_mm == 0), stop=(n_mm == 2 * NKC - 1)),
                                      reads=['i_IT%d' % ib] + (ykeys if n_mm in (0, 2 * NKC - 1) else []), writes=['i_yps%d' % ib])
                                n_mm += 1
                        if o == 0:
                            conv3_tile(G, 'i_', seq, C_HV, 0, tg, cwb, cbb, um, u0, up, vacc, 'act')
                            vk = 'i_acc'
                            vt = vacc
                        else:
                            Pg.dma('act', vacc[:], seq.Z1[rows, :], writes=['i_vz'])
                            vk = 'i_vz'
                            vt = vacc
                        Pg.op('dve', lambda e: e.tensor_tensor(out=yt[:], in0=y_ps[ib][:], in1=RNt[:], op=ALU.mult),
                              reads=['i_yps%d' % ib, 'h_RNt'], writes=['i_yt'])
                        Pg.op('pool', lambda e: e.tensor_tensor(out=zt_[:], in0=vt[:], in1=skb[:, o, :], op=ALU.mult),
                              reads=[vk, 'h_skb'], writes=['i_zt'])
                        Pg.op('dve', lambda e: e.tensor_tensor(out=yt[:], in0=yt[:], in1=zt_[:], op=ALU.add), reads=['i_yt', 'i_zt'], writes=['i_yt'])
                        Pg.op('pool', lambda e: e.tensor_copy(out=zt_[:, 0:1], in_=zt_[:, 0:1]), reads=['i_yt', vk, 'i_acc', 'i_vz'], writes=['i_zt'])
                        conv3_tile(G, 'i_', seq, C_H1 if o == 0 else C_H2, 1 + o, tg, cwb, cbb, um, u0, up, xacc, 'act')
                        Pg.op('dve', lambda e: e.tensor_tensor(out=yt[:], in0=yt[:], in1=xacc[:], op=ALU.mult), reads=['i_yt', 'i_acc'], writes=['i_yt'])
                        if o == 0:
                            Pg.dma('pool', seq.Z1[rows, :], yt[:], reads=['i_yt'])
                        else:
                            Pg.dma('act', hz[:], seq.U[rows, C_HZ:C_HZ + 512], writes=['i_hz'])
                            Pg.op('act', lambda e: e.activation(out=hz[:], in_=hz[:], func=AF.Silu), reads=['i_hz'], writes=['i_hz'])
                            Pg.op('dve', lambda e: e.tensor_tensor(out=yt[:], in0=yt[:], in1=hz[:], op=ALU.mult), reads=['i_yt', 'i_hz'], writes=['i_yt'])
                            Pg.dma('pool', seq.MIX[rows, 1024:1536], yt[:], reads=['i_yt'])
            Pg.barrier()
    Pg.barrier()


def build(dbg=None, mixers=None):
    dbg = dbg or {}
    nlayers = dbg.get('nlayers', DEPTH)
    nc = bass.Bass("TRN2", target_bir_lowering=False)
    G = Ctx()
    G.nc = nc
    G.dbg = dbg
    G.w = {k: nc.dram_tensor(k, s, F32, kind="ExternalInput").ap() for k, s in WEIGHT_SPECS.items()}
    G.c = {k: nc.dram_tensor('c_' + k, s, dt, kind="ExternalInput").ap() for k, (s, dt) in CONST_SPECS.items()}
    xa = nc.dram_tensor('xa', [LA, D], F32, kind="ExternalInput").ap()
    xb = nc.dram_tensor('xb', [LB, D], F32, kind="ExternalInput").ap()
    ya = nc.dram_tensor('ya', [LA, D], F32, kind="ExternalOutput").ap()
    yb = nc.dram_tensor('yb', [LB, D], F32, kind="ExternalOutput").ap()
    sk = "ExternalOutput" if dbg.get('dump') else "Internal"
    U = USplit(nc.dram_tensor('s_U1', [LB, USPLIT], F32, kind="Internal").ap(),
               nc.dram_tensor('s_U2', [LB, UT - USPLIT], F32, kind="Internal").ap())
    QKT = nc.dram_tensor('s_QKT', [1024, LB], F32, kind=sk).ap()
    MIX = nc.dram_tensor('s_MIX', [LB, 1536], F32, kind=sk).ap()
    Xa = nc.dram_tensor('s_Xa', [LA, D], F32, kind=sk).ap()
    Xb = nc.dram_tensor('s_Xb', [LB, D], F32, kind="Internal").ap()
    GTFd = nc.dram_tensor('s_GTF', [LB, 12], F32, kind=sk).ap()
    GTBd = nc.dram_tensor('s_GTB', [LB, 12], F32, kind=sk).ap()
    ABCd = nc.dram_tensor('s_ABC', [128, 8, 128], F32, kind=sk).ap()
    HFd = nc.dram_tensor('s_HF', [LB, 512], F32, kind="Internal").ap()
    XSd = nc.dram_tensor('s_XS', [4, NKC, 128, 2, 512], BF16, kind="Internal").ap()
    GSd = nc.dram_tensor('s_GS', [7, NKC, 128, 2, 512], BF16, kind="Internal").ap()
    Z1d = nc.dram_tensor('s_Z1', [LB, 512], F32, kind="Internal").ap()
    G.GSPa = nc.dram_tensor('s_GSPa', [5 * 128, 1024], BF16, kind="Internal").ap()
    G.GSAa = nc.dram_tensor('s_GSAa', [8 * 5 * 128, 1024], BF16, kind="Internal").ap()
    G.YP = nc.dram_tensor('s_YP', [20 * 128, 1024], BF16, kind="Internal").ap()
    G.YA = nc.dram_tensor('s_YA', [8 * 20 * 128, 1024], BF16, kind="Internal").ap()
    seqs = []
    for name, L, xin, X, yout in (('a', LA, xa, Xa, ya), ('b', LB, xb, Xb, yb)):
        s = Ctx()
        s.name, s.L, s.nb, s.xin, s.X, s.yout = name, L, L // P, xin, X, yout
        s.U, s.QKT, s.MIX = U, QKT, MIX
        s.GTF, s.GTB, s.ABC, s.HF = GTFd, GTBd, ABCd, HFd
        s.XS, s.GS, s.Z1 = XSd, GSd, Z1d
        seqs.append(s)
    if dbg.get('only_a'):
        seqs = seqs[:1]
    with ExitStack() as es:
        G.mixers = mixers or all_mixers
        G.P = Prog(nc, es)
        G.identb = sb(nc, es, 'identb', [128, 128], BF16)
        G.ccdummy = sb(nc, es, 'ccdummy', [1, 8], F32)
        G.P.dma('sp', G.identb[:], G.c['identb'][:, :], writes=['identb'])
        G.P.barrier()
        for layer in range(nlayers):
            for seq in seqs:
                for blk in range(seq.nb):
                    phase_A(G, layer, seq, blk)
                G.mixers(G, layer, seq)
                if not dbg.get('skip_O'):
                    phase_O(G, layer, seq)
        G.P.barrier()
    return nc, G


def no_mixers(G, layer, seq):
    pass


def all_mixers(G, layer, seq):
    phase_G(G, layer, seq)
    phase_B(G, layer, seq)
    phase_R(G, layer, seq)
    phase_H(G, layer, seq)


def hy_mixers(G, layer, seq):
    phase_H(G, layer, seq)


def rt_mixers(G, layer, seq):
    phase_R(G, layer, seq)


def ml_mixers(G, layer, seq):
    phase_G(G, layer, seq)
    phase_B(G, layer, seq)


def build_with(dbg, mixers):
    return build(dbg, mixers)


_NC = None


def kernel(**inputs):
    global _NC
    if _NC is None:
        _NC = build()[0]
    nc = _NC
    c = host_consts()
    maps = []
    xs = np.ascontiguousarray(np.asarray(inputs['x_sample'], dtype=np.float32)[0])
    for i in range(8):
        m = {k: np.ascontiguousarray(np.asarray(inputs[k], dtype=np.float32)) for k in WEIGHT_SPECS}
        for k in CONST_SPECS:
            if k != 'fwd_own':
                m['c_' + k] = c[k]
        own = np.zeros((5, 128, 64, 2, 128), dtype=c['fwd'].dtype)
        for j in range(5):
            if i + 8 * j < NKC:
                own[j] = c['fwd'][i + 8 * j]
        m['c_fwd_own'] = own
        m['xa'] = np.ascontiguousarray(np.asarray(inputs['x_prompt'], dtype=np.float32)[i])
        m['xb'] = xs
        maps.append(m)
    res = run_bass_kernel_spmd(nc, maps, core_ids=list(range(8)))
    y_prompt = np.stack([np.asarray(r['ya'], dtype=np.float32) for r in res.results], axis=0)
    y_sample = np.asarray(res.results[0]['yb'], dtype=np.float32)[None]
    return (y_prompt, y_sample)
```
